# Optimizing a Trainium2 kernel written in Bass

```python
import jax, jax.numpy as jnp
from jax import lax
import numpy as np

D_MODEL = 1024
BATCH = 2
SEQ = 16384
DEPTH = 2

GRID_W = 64
CTX_LEN = 256
HEAD_DIM = 64
N_HEADS = 8
N_KV_HEADS = 2
D_ATTN = N_HEADS * HEAD_DIM
D_CONV_A = 256
D_CONF = 256
D_MIX = D_CONV_A + D_ATTN + D_CONF
SHORT_CONV_W = 3
CONF_CONV_W = 31
WINDOW = 128
BLOCK = 128
D_FF = 4 * D_MODEL
ROPE_BASE = 10000.0
EPS = 1e-6
NEG_INF = -1e30

OFF_Q = 3 * D_CONV_A
OFF_K = OFF_Q + D_ATTN
OFF_V = OFF_K + N_KV_HEADS * HEAD_DIM
OFF_C = OFF_V + N_KV_HEADS * HEAD_DIM
D_IN = OFF_C + 2 * D_CONF

kernel_name = "hybrid_parallel_groups_diffusion_block"


def rms_norm(x, g):
    x32 = x.astype(jnp.float32)
    y = x32 * lax.rsqrt(jnp.mean(x32 * x32, axis=-1, keepdims=True) + EPS)
    return y.astype(x.dtype) * g


def layer_norm(x, g, b):
    x32 = x.astype(jnp.float32)
    mu = jnp.mean(x32, axis=-1, keepdims=True)
    xc = x32 - mu
    y = xc * lax.rsqrt(jnp.mean(xc * xc, axis=-1, keepdims=True) + EPS)
    return y.astype(x.dtype) * g + b


def modulate(h, shift, scale):
    return h * (1 + scale) + shift


def heads(t, n):
    return t.reshape(t.shape[:-1] + (n, HEAD_DIM))


def depthwise_conv(x, w, b=None):
    k = w.shape[0]
    y = lax.conv_general_dilated(
        x, w[:, None, :], window_strides=(1,), padding=[(k // 2, k // 2)],
        dimension_numbers=('NWC', 'WIO', 'NWC'), feature_group_count=x.shape[-1])
    return y if b is None else y + b


def axial_rope(t, row, col):
    d_axis = HEAD_DIM // 2
    half = d_axis // 2
    inv_freq = ROPE_BASE ** (-jnp.arange(0, d_axis, 2, dtype=jnp.float32) / d_axis)

    def rot(u, pos):
        ang = pos.astype(jnp.float32)[:, None] * inv_freq[None, :]
        cos = jnp.cos(ang)[None, :, None, :].astype(u.dtype)
        sin = jnp.sin(ang)[None, :, None, :].astype(u.dtype)
        u1, u2 = u[..., :half], u[..., half:]
        return jnp.concatenate([u1 * cos - u2 * sin, u1 * sin + u2 * cos], axis=-1)

    return jnp.concatenate([rot(t[..., :d_axis], row), rot(t[..., d_axis:], col)], axis=-1)


def window_attention(q, k, v, kc, vc, sink):
    B, S, H, dh = q.shape
    KVH = k.shape[2]
    G = H // KVH
    nb = S // BLOCK
    n_ctx = kc.shape[1]
    scale = dh ** -0.5
    qb = q.reshape(B, nb, BLOCK, KVH, G, dh)

    def band(t):
        tp = jnp.pad(t, ((0, 0), (BLOCK, BLOCK), (0, 0), (0, 0))).reshape(B, nb + 2, BLOCK, KVH, dh)
        return jnp.concatenate([tp[:, :-2], tp[:, 1:-1], tp[:, 2:]], axis=2)

    kw, vw = band(k), band(v)
    s_loc = jnp.einsum('bnqkgd,bnskd->bnkgqs', qb, kw).astype(jnp.float32) * scale
    qi = jnp.arange(BLOCK)
    si = jnp.arange(3 * BLOCK)
    rel = si[None, :] - BLOCK - qi[:, None]
    kpos = jnp.arange(nb)[:, None] * BLOCK - BLOCK + si[None, :]
    valid = (jnp.abs(rel) <= WINDOW)[None] & ((kpos >= 0) & (kpos < S))[:, None, :]
    s_loc = jnp.where(valid[None, :, None, None], s_loc, NEG_INF)
    s_ctx = jnp.einsum('bnqkgd,bckd->bnkgqc', qb, kc).astype(jnp.float32) * scale
    sink_col = jnp.broadcast_to(sink.astype(jnp.float32).reshape(1, 1, KVH, G, 1, 1),
                                s_loc.shape[:-1] + (1,))
    p = jax.nn.softmax(jnp.concatenate([s_loc, s_ctx, sink_col], axis=-1), axis=-1).astype(v.dtype)
    n_loc = 3 * BLOCK
    o = (jnp.einsum('bnkgqs,bnskd->bnqkgd', p[..., :n_loc], vw)
         + jnp.einsum('bnkgqc,bckd->bnqkgd', p[..., n_loc:n_loc + n_ctx], vc))
    return o.reshape(B, S, H * dh)


def context_attention(q, k, v, sink):
    B, C, H, dh = q.shape
    KVH = k.shape[2]
    G = H // KVH
    qg = q.reshape(B, C, KVH, G, dh)
    s = jnp.einsum('bqkgd,bckd->bkgqc', qg, k).astype(jnp.float32) * (dh ** -0.5)
    sink_col = jnp.broadcast_to(sink.astype(jnp.float32).reshape(1, KVH, G, 1, 1), s.shape[:-1] + (1,))
    p = jax.nn.softmax(jnp.concatenate([s, sink_col], axis=-1), axis=-1).astype(v.dtype)
    o = jnp.einsum('bkgqc,bckd->bqkgd', p[..., :-1], v)
    return o.reshape(B, C, H * dh)


def short_conv_mix(u, w):
    x_in, b_gate, c_gate = jnp.split(u, 3, axis=-1)
    return b_gate * depthwise_conv(c_gate * x_in, w)


def conformer_conv(u, w, b, g, beta):
    val, gate = jnp.split(u, 2, axis=-1)
    y = depthwise_conv(val * jax.nn.sigmoid(gate), w, b)
    return jax.nn.silu(layer_norm(y, g, beta))


def sq_relu_mlp(h, w1, w2):
    return jnp.square(jax.nn.relu(h @ w1)) @ w2


def setup_inputs(seed: int = 0) -> dict:
    key = jax.random.key(seed)
    ks = jax.random.split(key, 24)
    L = DEPTH

    def nrm(k, shape, s):
        return jax.random.normal(k, shape, jnp.float32) * s

    return {
        "x": nrm(ks[0], (BATCH, SEQ, D_MODEL), 1.0),
        "c": nrm(ks[1], (BATCH, D_MODEL), 1.0),
        "ctx": nrm(ks[2], (BATCH, CTX_LEN, D_MODEL), 1.0),
        "c_ctx": nrm(ks[3], (D_MODEL,), 1.0),
        "w_mod": nrm(ks[4], (L, D_MODEL, 6 * D_MODEL), 0.5 * D_MODEL ** -0.5),
        "b_mod": nrm(ks[5], (L, 6 * D_MODEL), 0.02),
        "norm1_g": 1.0 + nrm(ks[6], (L, D_MODEL), 0.05),
        "w_in": nrm(ks[7], (L, D_MODEL, D_IN), D_MODEL ** -0.5),
        "conv_a_w": nrm(ks[8], (L, SHORT_CONV_W, D_CONV_A), SHORT_CONV_W ** -0.5),
        "q_norm_g": 1.0 + nrm(ks[9], (L, HEAD_DIM), 0.05),
        "k_norm_g": 1.0 + nrm(ks[10], (L, HEAD_DIM), 0.05),
        "attn_sink": nrm(ks[11], (L, N_HEADS), 0.5),
        "conv_c_w": nrm(ks[12], (L, CONF_CONV_W, D_CONF), CONF_CONV_W ** -0.5),
        "conv_c_b": nrm(ks[13], (L, D_CONF), 0.02),
        "ln_c_g": 1.0 + nrm(ks[14], (L, D_CONF), 0.05),
        "ln_c_b": nrm(ks[15], (L, D_CONF), 0.02),
        "w_out": nrm(ks[16], (L, D_MIX, D_MODEL), D_MIX ** -0.5),
        "norm2_g": 1.0 + nrm(ks[17], (L, D_MODEL), 0.05),
        "w_mlp1": nrm(ks[18], (L, D_MODEL, D_FF), D_MODEL ** -0.5),
        "w_mlp2": nrm(ks[19], (L, D_FF, D_MODEL), D_FF ** -0.5),
    }


def reference(x, c, ctx, c_ctx, w_mod, b_mod, norm1_g, w_in, conv_a_w, q_norm_g, k_norm_g,
              attn_sink, conv_c_w, conv_c_b, ln_c_g, ln_c_b, w_out, norm2_g, w_mlp1, w_mlp2):
    S = x.shape[1]
    ROWS = S // GRID_W
    row = jnp.repeat(jnp.arange(ROWS, dtype=jnp.int32), GRID_W)
    col = jnp.tile(jnp.arange(GRID_W, dtype=jnp.int32), ROWS)
    xc = ctx
    silu_c = jax.nn.silu(c)
    silu_cc = jax.nn.silu(c_ctx)

    for i in range(DEPTH):
        last = i == DEPTH - 1
        mod_l = (silu_c @ w_mod[i] + b_mod[i])[:, None, :]
        sh1, sc1, g1, sh2, sc2, g2 = jnp.split(mod_l, 6, axis=-1)
        mod_c = silu_cc @ w_mod[i] + b_mod[i]
        csh1, csc1, cg1, csh2, csc2, cg2 = jnp.split(mod_c, 6, axis=-1)

        hc = modulate(rms_norm(xc, norm1_g[i]), csh1, csc1)
        if last:
            uc_kv = hc @ w_in[i][:, OFF_K:OFF_C]
            kc_raw, vc_raw = uc_kv[..., :OFF_V - OFF_K], uc_kv[..., OFF_V - OFF_K:]
        else:
            uc = hc @ w_in[i]
            kc_raw, vc_raw = uc[..., OFF_K:OFF_V], uc[..., OFF_V:OFF_C]
        kc = rms_norm(heads(kc_raw, N_KV_HEADS), k_norm_g[i])
        vc = heads(vc_raw, N_KV_HEADS)

        h = modulate(rms_norm(x, norm1_g[i]), sh1, sc1)
        u = h @ w_in[i]
        q = axial_rope(rms_norm(heads(u[..., OFF_Q:OFF_K], N_HEADS), q_norm_g[i]), row, col)
        k = axial_rope(rms_norm(heads(u[..., OFF_K:OFF_V], N_KV_HEADS), k_norm_g[i]), row, col)
        v = heads(u[..., OFF_V:OFF_C], N_KV_HEADS)
        y = jnp.concatenate([
            short_conv_mix(u[..., :OFF_Q], conv_a_w[i]),
            window_attention(q, k, v, kc, vc, attn_sink[i]),
            conformer_conv(u[..., OFF_C:], conv_c_w[i], conv_c_b[i], ln_c_g[i], ln_c_b[i]),
        ], axis=-1)
        x = x + g1 * (y @ w_out[i])
        x = x + g2 * sq_relu_mlp(modulate(rms_norm(x, norm2_g[i]), sh2, sc2), w_mlp1[i], w_mlp2[i])

        if not last:
            qc = rms_norm(heads(uc[..., OFF_Q:OFF_K], N_HEADS), q_norm_g[i])
            yc = jnp.concatenate([
                short_conv_mix(uc[..., :OFF_Q], conv_a_w[i]),
                context_attention(qc, kc, vc, attn_sink[i]),
                conformer_conv(uc[..., OFF_C:], conv_c_w[i], conv_c_b[i], ln_c_g[i], ln_c_b[i]),
            ], axis=-1)
            xc = xc + cg1 * (yc @ w_out[i])
            xc = xc + cg2 * sq_relu_mlp(modulate(rms_norm(xc, norm2_g[i]), csh2, csc2),
                                        w_mlp1[i], w_mlp2[i])
    return x
```

```python
import contextlib
import numpy as np
import concourse.bass as bass
import concourse.mybir as mybir
from concourse.bass_utils import run_bass_kernel_spmd

F32 = mybir.dt.float32
BF16 = mybir.dt.bfloat16
F32R = mybir.dt.float32r
ALU = mybir.AluOpType
AF = mybir.ActivationFunctionType

D = 1024
SEQ = 16384
CTX = 256
NL = 2
T = 512
NT = 9
NTOK = NT * T
HALO = 256
OWN = 4096
NB = NTOK // 128
H = 16
ZW = H + T + H
OFF_Q = 768
EPS = 1e-6
NG = 44
NSET = 1
GE = 2048

ENGS = ("pe", "act", "dve", "pool", "sp")
LNF = "force"
SELF_SYNC = True
DEBUG_SCHED = False


class Prog:
    def __init__(self, nc, stack):
        self.nc = nc
        self.stack = stack
        self.ops = {e: [] for e in ENGS}
        self.sig = {e: 0 for e in ENGS}
        self.pending = {e: False for e in ENGS}
        self.sems = {e: stack.enter_context(nc.semaphore("s_" + e)) for e in ENGS}
        self.seen = {e: {} for e in ENGS}
        self.lastw = {}
        self.readers = {}
        self.dsems = {}
        self.semobj = {("e", e): self.sems[e] for e in ENGS}

    def dma_sem(self, name):
        if name not in self.dsems:
            s = self.stack.enter_context(self.nc.semaphore("d_" + name))
            self.dsems[name] = [s, 0]
            self.semobj[("d", name)] = s
        return self.dsems[name]

    def _deps(self, reads, writes):
        deps = {}

        def add(tok):
            if tok is None:
                return
            k, v = tok
            if deps.get(k, 0) < v:
                deps[k] = v
        for k in reads:
            add(self.lastw.get(k))
        for k in writes:
            add(self.lastw.get(k))
            for sk, v in self.readers.get(k, {}).items():
                add((sk, v))
        return deps

    def _waits(self, eng, deps):
        waits = []
        seen = self.seen[eng]
        for sk, v in deps.items():
            if sk == ("e", eng) and (eng == "pe" or not SELF_SYNC):
                continue
            if seen.get(sk, 0) >= v:
                continue
            seen[sk] = v
            waits.append((self.semobj[sk], v))
        return waits

    def _note(self, tok, reads, writes):
        sk, v = tok
        for k in reads:
            r = self.readers.setdefault(k, {})
            if r.get(sk, 0) < v:
                r[sk] = v
        for k in writes:
            self.lastw[k] = tok
            self.readers[k] = {}

    def op(self, eng, fn, reads=(), writes=(), signal=True):
        waits = self._waits(eng, self._deps(reads, writes))
        if signal:
            self.sig[eng] += 1
            tok = (("e", eng), self.sig[eng])
            self.pending[eng] = False
        else:
            tok = (("e", eng), self.sig[eng] + 1)
            self.pending[eng] = True
        self.ops[eng].append((waits, fn, (self.sems[eng], 1) if signal else None))
        self._note(tok, reads, writes)

    def dma(self, eng, fn, reads=(), writes=(), sem="misc"):
        waits = self._waits(eng, self._deps(reads, writes))
        ds = self.dma_sem(sem)
        ds[1] += 16
        tok = (("d", sem), ds[1])
        self.ops[eng].append((waits, fn, (ds[0], 16)))
        self._note(tok, reads, writes)

    def wait_all(self, eng, keys):
        waits = self._waits(eng, self._deps(keys, keys))
        self.ops[eng].append((waits, None, None))

    def emit(self):
        nc = self.nc
        for e in ENGS:
            assert not self.pending[e], f"engine {e} has trailing unsignalled ops"
        with nc.Block() as block:
            def run(e):
                def body(engine):
                    for waits, fn, inc in self.ops[e]:
                        for s, v in waits:
                            engine.wait_ge(s, v)
                        if fn is not None:
                            ins = fn(engine)
                            if inc is not None:
                                ins.then_inc(inc[0], inc[1])
                return body
            block.tensor(run("pe"))
            block.scalar(run("act"))
            block.vector(run("dve"))
            block.gpsimd(run("pool"))
            block.sync(run("sp"))


def _cp_layout():
    off = {}
    n = 0

    def add(name, w):
        nonlocal n
        off[name] = n
        n += w
    for l in range(NL):
        add(("n1g", l), 8)
        add(("n2g", l), 8)
        add(("bmod", l), 48)
        add(("caw", l), 6)
        add(("ccw", l), 62)
        add(("ccb", l), 2)
        add(("lng", l), 2)
        add(("lnb", l), 2)
        add(("qg", l), 1)
        add(("kg", l), 1)
        add(("sink", l), 8)
    add("c", 16)
    add("valid", NB)
    return off, n


CPO, NCP = _cp_layout()
NKP = 5 * 128


def build(debug=False):
    nc = bass.Bass("TRN2", target_bir_lowering=False)
    xT = nc.dram_tensor("xT", [NT, 128, 8 * T], F32, kind="ExternalInput").ap()
    ctxT = nc.dram_tensor("ctxT", [D, CTX], F32, kind="ExternalInput").ap()
    wpack = nc.dram_tensor("wpack", [NL * NG, 128, GE], F32, kind="ExternalInput").ap()
    wmod = nc.dram_tensor("wmod", [NL, 12, 128, 8 * 512], F32, kind="ExternalInput").ap()
    cpack = nc.dram_tensor("cpack", [128, NCP], F32, kind="ExternalInput").ap()
    kpack = nc.dram_tensor("kpack", [128, NKP], F32, kind="ExternalInput").ap()
    ropeC = nc.dram_tensor("ropeC", [128, NTOK], F32, kind="ExternalInput").ap()
    ropeS = nc.dram_tensor("ropeS", [128, NTOK], F32, kind="ExternalInput").ap()
    outT = nc.dram_tensor("outT", [NT, 128, 8 * T], F32, kind="ExternalOutput").ap()
    wbf = nc.dram_tensor("wbf", [NL * NG, 128, GE], BF16, kind="Internal").ap()
    x1T = nc.dram_tensor("x1T", [NT, 128, 8 * T], F32, kind="Internal").ap()
    xc1T = nc.dram_tensor("xc1T", [D, CTX], F32, kind="Internal").ap()
    xT_v, x1T_v, outT_v = xT, x1T, outT
    ctxT_v = ctxT.rearrange("(c p) t -> p c t", p=128)
    xc1T_v = xc1T.rearrange("(c p) t -> p c t", p=128)

    with contextlib.ExitStack() as st:
        P = Prog(nc, st)

        def sb(name, shape, dt):
            return st.enter_context(nc.sbuf_tensor(name, shape, dt))

        xs = [sb(f"xs{i}", [128, 8, T], F32) for i in range(3)]
        wring = [sb(f"wr{i}", [128, GE], BF16) for i in range(6)]
        hT = sb("hT", [128, 8, T], BF16)
        hT2 = sb("hT2", [128, 8, T], BF16)
        tmlp = [sb(f"tmlp{i}", [128, T], F32) for i in range(2)]
        yacc = [sb(f"yacc{i}", [128, T], F32) for i in range(2)]
        ymix = sb("ymix", [128, 8, T], BF16)
        hid = sb("hid", [128, 16, T], BF16)
        pT = [sb(f"pT{i}", [128, T], BF16) for i in range(4)]
        ctab = sb("ctab", [128, T], F32)
        stab = sb("stab", [128, T], F32)
        QT = [[sb(f"QT{l}_{b}", [128, 4, T], BF16) for b in range(2)] for l in range(NSET)]
        kring = [sb(f"kring{g}", [128, 12 * 128], BF16) for g in range(2)]
        vring = [sb(f"vring{l}", [128, 12, 2, 128], BF16) for l in range(NSET)]
        za = [sb(f"za{l}", [128, 2, 2, ZW], F32) for l in range(NSET)]
        zc = [sb(f"zc{l}", [128, 2, 2, ZW], F32) for l in range(NSET)]
        bgate = [sb(f"bg{l}", [128, 2, 2, T], BF16) for l in range(NSET)]
        ctxK = [sb(f"ctxK{g}", [128, CTX], BF16) for g in range(2)]
        ctxV = [sb(f"ctxV{l}", [128, 2, 2, 128], BF16) for l in range(NSET)]
        NT32, NTB = 9, 6
        t32 = [sb(f"t32_{i}", [128, T], F32) for i in range(NT32)]
        tb16 = [sb(f"tb_{i}", [128, T], BF16) for i in range(NTB)]
        cp = sb("cp", [128, NCP], F32)
        kb = sb("kb", [128, 5, 128], BF16)
        ones = sb("ones", [128, 128], BF16)
        epsc = sb("epsc", [128, 1], F32)
        cone = sb("cone", [128, 1], F32)
        silc = sb("silc", [128, 16], F32)
        wst = sb("wst", [128, 8, 128], F32)
        mod = [sb(f"mod{l}", [128, 96], F32) for l in range(NL)]
        avec = [sb(f"avec{l}", [128, 2, 2, 8], F32) for l in range(NL)]
        esk = [sb(f"esk{l}", [128, 8], F32) for l in range(NSET)]
        esrow = sb("esrow", [1, 2, 4, 128], BF16)
        sinkL = sb("sinkL", [1, 128], BF16)
        Rq = [sb(f"Rq{l}", [128, 128], BF16) for l in range(NSET)]
        Rk = [sb(f"Rk{l}", [128, 128], BF16) for l in range(NSET)]
        banks = [st.enter_context(nc.psum_tensor(f"bank{i}", [128, T], F32)) for i in range(8)]
        kp32 = hid[:].rearrange("p a b -> p (a b)").bitcast(F32)[:, 0:NKP]

        ident = kb[:, 0, :]
        BDm = kb[:, 1, :]

        rr = {"mm": 0, "st": 0, "t32": 0, "tb": 0, "pT": 0, "wr": 0, "tmlp": 0}
        bank_groups = {"mm": [0, 1, 2, 3], "st": [6, 7]}

        last_bank = {"key": None}

        def bank(group):
            ids = bank_groups[group]
            i = ids[rr[group] % len(ids)]
            rr[group] += 1
            if group == "mm":
                last_bank["key"] = ("bank", i)
            return banks[i], ("bank", i)

        def tmp32():
            i = rr["t32"] % NT32
            rr["t32"] += 1
            return t32[i], ("t32", i)

        def tmpb():
            i = rr["tb"] % NTB
            rr["tb"] += 1
            return tb16[i], ("tb", i)

        def cpc(name, c=0, w=1):
            o = CPO[name] + c
            return cp[:, o:o + w]

        gstate = {"n": 0}

        def load_granule(gidx):
            s = rr["wr"] % 6
            rr["wr"] += 1
            P.dma("sp", lambda e: e.dma_start(out=wring[s][:], in_=wbf[gidx]),
                  reads=[("wbf", gidx // 2)], writes=[("wr", s)], sem=f"wr{s}")
            return wring[s], ("wr", s)

        def mm_group(out_ap, bkey, pairs, reads, kreads=None):
            n = len(pairs)
            for k, (l_ap, r_ap) in enumerate(pairs):
                rd = list(reads) + (list(kreads[k]) if kreads else [])
                P.op("pe", lambda e, l_ap=l_ap, r_ap=r_ap, k=k: e.matmul(out_ap, lhsT=l_ap, rhs=r_ap, start=(k == 0), stop=(k == n - 1)),
                     reads=rd, writes=[bkey], signal=(k == n - 1))

        P.dma("sp", lambda e: e.dma_start(out=cp[:], in_=cpack), writes=["cp"], sem="c0")
        P.dma("sp", lambda e: e.dma_start(out=kp32, in_=kpack), writes=[("hid", 0), ("hid", 1), ("hid", 2)], sem="c2")
        def cast_layer(l, lo=0, hi=NG // 2, gate=()):
            for g2 in range(l * NG // 2 + lo, l * NG // 2 + hi):
                P.dma("pool", lambda e, g2=g2: e.dma_start(out=wbf[2 * g2:2 * g2 + 2], in_=wpack[2 * g2:2 * g2 + 2]),
                      reads=list(gate), writes=[("wbf", g2)], sem=f"cast{g2}")
        cast_layer(0, 0, 4)
        P.op("dve", lambda e: e.tensor_copy(out=kb[:].rearrange("p a b -> p (a b)"), in_=kp32), reads=[("hid", 0), ("hid", 1), ("hid", 2)], writes=["kb"])
        P.op("pool", lambda e: e.memset(ones[:], 1.0), writes=["ones"])
        P.op("pool", lambda e: e.memset(epsc[:], EPS), writes=["epsc"])
        P.op("pool", lambda e: e.memset(sinkL[0:1, 0:64], 0.0), writes=["sinkL"])
        P.op("pool", lambda e: e.memset(sinkL[0:1, 64:128], 1.0), writes=["sinkL"])
        P.op("pool", lambda e: e.memset(cone[:], 1.0), writes=["cone"])
        for l in range(NSET):
            for g_ in range(2):
                P.op("pool", lambda e, g_=g_: e.memset(kring[g_][:], 0.0), writes=[("kring", 0)])
                P.op("pool", lambda e, g_=g_: e.memset(ctxK[g_][:], 0.0), writes=[("ctxK", 0)])
            P.op("pool", lambda e, l=l: e.memset(vring[l % NSET][:].rearrange("p a b c -> p (a b c)"), 0.0), writes=[("vring", l % NSET)])
            for b in range(2):
                P.op("pool", lambda e, l=l, b=b: e.memset(za[l % NSET][:, b].rearrange("p a b -> p (a b)"), 0.0), writes=[("za", l % NSET, b)])
                P.op("pool", lambda e, l=l, b=b: e.memset(zc[l % NSET][:, b].rearrange("p a b -> p (a b)"), 0.0), writes=[("zc", l % NSET, b)])
        P.op("act", lambda e: e.activation(out=silc[:], in_=cpc("c", 0, 16), func=AF.Silu), reads=["cp"], writes=["silc"])
        def mod_finish(l):
            for s_ in range(2):
                for w, (sco, gname) in enumerate(((8, "n1g"), (32, "n2g"))):
                    P.op("dve", lambda e, s_=s_, w=w, sco=sco, gname=gname: e.scalar_tensor_tensor(
                        out=avec[l][:, s_, w, :], in0=mod[l][:, sco * 2 + s_:(sco + 8) * 2 + s_:2], scalar=1.0,
                        in1=cpc((gname, l), 0, 8), op0=ALU.add, op1=ALU.mult),
                        reads=[("mod", l), "cp"], writes=[("avec", l)])

        hid32 = hid[:].rearrange("p a b -> p (a b)").bitcast(F32)
        HIDK = [("hid", k) for k in range(16)]

        def mod_startup(l):
            mb, mbk = banks[4 + l], ("bank", 4 + l)
            for piece in range(12):
                if piece % 2 == 0:
                    stg, skey, ssem = xs[1][:].rearrange("p a b -> p (a b)"), [("xs", 1)], "xs1"
                else:
                    stg, skey, ssem = hid32, HIDK, "hidst"
                P.dma("pool", lambda e, piece=piece, stg=stg: e.dma_start(out=stg, in_=wmod[l][piece]), writes=skey, sem=ssem)
                stv = stg.rearrange("p (k c) -> p k c", k=8)
                for jj in range(4):
                    j = piece * 4 + jj
                    for k in range(8):
                        P.op("pe", lambda e, stv=stv, jj=jj, j=j, k=k: e.matmul(
                            mb[:, j * 2:j * 2 + 2], lhsT=stv[:, k, jj * 128:(jj + 1) * 128], rhs=silc[:, k * 2:k * 2 + 2],
                            start=(k == 0), stop=(k == 7)),
                            reads=skey + ["silc"], writes=[mbk], signal=(k == 7))
                if piece == 11:
                    cast_layer(0, 4, 6, gate=skey)
                    for g2 in range(6, NG // 2):
                        bgfast.append(lambda g2=g2: cast_layer(0, g2, g2 + 1, gate=[last_bank["key"]] if last_bank["key"] else ()))
            for s_ in range(2):
                P.op("dve", lambda e, s_=s_: e.tensor_tensor(
                    out=mod[l][:, s_:96:2], in0=mb[:, s_:96:2], in1=cpc(("bmod", l), 0, 48), op=ALU.add),
                    reads=["cp"], writes=[mbk, ("mod", l)])
            mod_finish(l)

        def mod_background(l):
            items = []
            for j in range(48):
                def dma_j(j=j):
                    P.dma("sp", lambda e: e.dma_start(out=wst[:], in_=wmod[l][j // 4].rearrange("p (k c) -> p k c", k=8)[:, :, (j % 4) * 128:(j % 4 + 1) * 128]), writes=["wst"], sem="wst")

                def mm_j(j=j):
                    bk, bkey = bank("st")
                    for k in range(8):
                        P.op("pe", lambda e, k=k: e.matmul(bk[:, 0:2], lhsT=wst[:, k, :], rhs=silc[:, k * 2:k * 2 + 2], start=(k == 0), stop=(k == 7)),
                             reads=["wst", "silc"], writes=[bkey], signal=(k == 7))
                    P.op("dve", lambda e: e.tensor_scalar(out=mod[l][:, 2 * j:2 * j + 2], in0=bk[:, 0:2], scalar1=cpc(("bmod", l), j), scalar2=None, op0=ALU.add),
                         reads=["cp"], writes=[bkey, ("mod", l)])
                items += [dma_j, mm_j]
            items.append(lambda: mod_finish(l))
            return items

        bgfast = []
        P.dma("sp", lambda e: e.dma_start(out=xs[2][:, :, 0:CTX], in_=ctxT_v), writes=[("xs", 2)], sem="xs2")
        P.dma("sp", lambda e: e.dma_start(out=xs[0][:].rearrange("p a b -> p (a b)"), in_=xT_v[0]), writes=[("xs", 0)], sem="xs0")
        mod_startup(0)
        bgq = mod_background(1)

        if debug:
            dbg0 = nc.dram_tensor("dbg0", [128, 192], F32, kind="ExternalOutput").ap()
            for l_ in range(NL):
                P.dma("sp", lambda e, l_=l_: e.dma_start(out=dbg0[:, l_ * 96:(l_ + 1) * 96], in_=mod[l_][:]), reads=[("mod", l_)], writes=[("dbg0", l_)], sem="dbg0")

        def layer_setup(l):
            ls = l % NSET
            P.op("act", lambda e: e.activation(out=esk[ls][:], in_=cpc(("sink", l), 0, 8), func=AF.Exp), reads=["cp"], writes=[("esk", ls)])
            for g in range(2):
                for j in range(4):
                    P.op("dve", lambda e, g=g, j=j: e.tensor_copy(out=esrow[0:1, g, j, :], in_=esk[ls][0:1, 4 * g + j:4 * g + j + 1].broadcast_to([1, 128])),
                         reads=[("esk", ls)], writes=["esrow"])
            P.op("dve", lambda e: e.tensor_scalar(out=Rq[ls][:], in0=kb[:, 2, :], scalar1=cpc(("qg", l)), scalar2=None, op0=ALU.mult),
                 reads=["kb", "cp"], writes=[("Rq", ls)])
            P.op("dve", lambda e: e.tensor_scalar(out=Rk[ls][:], in0=kb[:, 2, :], scalar1=cpc(("kg", l)), scalar2=None, op0=ALU.mult),
                 reads=["kb", "cp"], writes=[("Rk", ls)])

        def modc(l, j, s):
            return mod[l][:, j * 2 + s:j * 2 + s + 1]

        def tile_ctx(l, kind, i):
            if kind == "lat":
                slot = i % 3
                return T, 0, i % 2, xs[slot][:], ("xs", slot), slot
            slot = 2 if l == 0 else 1
            return CTX, 1, 1, xs[slot][:, :, 0:CTX], ("xs", slot), slot

        def live_range(l, kind, i):
            if kind != "lat":
                return 0, CTX
            trim = 128 if l == 0 else 256
            if i == 0:
                return trim, T - trim
            if i == NT - 1:
                return 0, T - trim
            return 0, T

        def rms_norm_to(hbuf, hkey, xap, xkey, n, l, s, which, mark="force"):
            sq = ymix
            P.op("act", lambda e: e.activation(out=sq[:, :, :n], in_=xap, func=AF.Square), reads=[xkey], writes=["ymix"])
            yield mark
            bk, bkey = bank("st")
            mm_group(bk[:, :n], bkey, [(ones[:], sq[:, c, :n]) for c in range(8)], reads=["ymix", "ones"])
            rs, rskey = tmp32()
            P.op("act", lambda e: e.activation(out=rs[:, :n], in_=bk[:, :n], func=AF.Ln, bias=epsc[:], scale=1.0 / D),
                 reads=["epsc"], writes=[bkey, rskey])
            P.op("act", lambda e: e.activation(out=rs[:, :n], in_=rs[:, :n], func=AF.Exp, scale=-0.5), reads=[rskey], writes=[rskey])
            sho = 0 if which == 0 else 24
            for c in range(8):
                tm, tmkey = tmp32()
                P.op("dve", lambda e, c=c, tm=tm: e.scalar_tensor_tensor(
                    out=tm[:, :n], in0=xap[:, c, :], scalar=avec[l][:, s, which, c:c + 1], in1=rs[:, :n], op0=ALU.mult, op1=ALU.mult),
                    reads=[xkey, rskey, ("avec", l)], writes=[tmkey])
                P.op("act", lambda e, c=c, tm=tm: e.activation(out=hbuf[:, c, :n], in_=tm[:, :n], func=AF.Identity, bias=modc(l, sho + c, s)),
                     reads=[tmkey, ("mod", l)], writes=[(hkey, c)])
            yield mark
            yield mark

        def qk_chain(bk, bkey, n, l, is_q, out_ap, out_key, ct, st_, tabkeys):
            qb, qbk = tmpb()
            P.op("act", lambda e: e.activation(out=qb[:, :n], in_=bk[:, :n], func=AF.Identity), writes=[bkey, qbk])
            sqq, sqk = tmpb()
            P.op("pool", lambda e: e.tensor_tensor(out=sqq[:, :n], in0=qb[:, :n], in1=qb[:, :n], op=ALU.mult), reads=[qbk], writes=[sqk])

            def part2():
                b1, b1k = bank("st")
                mm_group(b1[:, :n], b1k, [(BDm, sqq[:, :n])], reads=[sqk, "kb"])
                if ct is not None:
                    b2, b2k = bank("st")
                    R = Rq[0] if is_q else Rk[0]
                    mm_group(b2[:, :n], b2k, [(R[:], qb[:, :n])], reads=[qbk, ("Rq", 0), ("Rk", 0)])
                rq, rqk = tmp32()
                P.op("act", lambda e: e.activation(out=rq[:, :n], in_=b1[:, :n], func=AF.Ln, bias=epsc[:], scale=1.0 / 64),
                     reads=["epsc"], writes=[b1k, rqk])
                P.op("act", lambda e: e.activation(out=rq[:, :n], in_=rq[:, :n], func=AF.Exp, scale=-0.5), reads=[rqk], writes=[rqk])
                gcol = cpc(("qg", l)) if is_q else cpc(("kg", l))
                if ct is None:
                    for (o_ap, p0, p1) in out_ap:
                        P.op("dve", lambda e, o_ap=o_ap, p0=p0, p1=p1: e.scalar_tensor_tensor(
                            out=o_ap, in0=qb[p0:p1, :n], scalar=gcol[p0:p1], in1=rq[p0:p1, :n], op0=ALU.mult, op1=ALU.mult),
                            reads=[qbk, "cp", rqk], writes=[out_key])
                    return
                t1, t1k = tmp32()
                P.op("dve", lambda e: e.scalar_tensor_tensor(out=t1[:, :n], in0=qb[:, :n], scalar=gcol, in1=ct, op0=ALU.mult, op1=ALU.mult),
                     reads=[qbk, "cp"] + tabkeys, writes=[t1k])
                t2, t2k = tmp32()
                P.op("dve", lambda e: e.tensor_tensor(out=t2[:, :n], in0=b2[:, :n], in1=st_, op=ALU.mult), reads=tabkeys, writes=[b2k, t2k])
                P.op("dve", lambda e: e.tensor_tensor(out=t1[:, :n], in0=t1[:, :n], in1=t2[:, :n], op=ALU.add), reads=[t1k, t2k], writes=[t1k])
                for (o_ap, p0, p1) in out_ap:
                    P.op("dve", lambda e, o_ap=o_ap, p0=p0, p1=p1: e.tensor_tensor(out=o_ap, in0=t1[p0:p1, :n], in1=rq[p0:p1, :n], op=ALU.mult),
                         reads=[t1k, rqk], writes=[out_key])
            return part2

        def gen_A(l, kind, i):
            lat = kind == "lat"
            n, s, zb, xap, xkey, slot = tile_ctx(l, kind, i)
            if lat:
                src = xT_v if l == 0 else x1T_v
                P.dma("sp", lambda e: e.dma_start(out=ctab[:], in_=ropeC[:, i * T:(i + 1) * T]), writes=["ctab"], sem="tabc")
                P.dma("sp", lambda e: e.dma_start(out=stab[:], in_=ropeS[:, i * T:(i + 1) * T]), writes=["stab"], sem="tabs")
                if not (l == 0 and i == 0):
                    P.dma("pool", lambda e: e.dma_start(out=xs[slot][:].rearrange("p a b -> p (a b)"), in_=src[i]),
                          reads=[("x1T", i)] if l > 0 else [], writes=[xkey], sem=f"xs{slot}")
                ct, st_, tabkeys = ctab[:, :n], stab[:, :n], ["ctab", "stab"]
            else:
                src = ctxT_v if l == 0 else xc1T_v
                if l > 0:
                    P.dma("sp", lambda e: e.dma_start(out=xs[slot][:, :, 0:CTX], in_=src),
                          reads=["xc1T"], writes=[xkey], sem=f"xs{slot}")
                ct, st_, tabkeys = None, None, []
            yield
            yield from rms_norm_to(hT, "hT", xap, xkey, n, l, s, 0)
            zak, zck, bgk, qtk = ("za", 0, zb), ("zc", 0, zb), ("bg", 0, zb), ("QT", 0, zb)
            zav, zcv = za[0][:, zb], zc[0][:, zb]
            if lat and i >= 1:
                pb = (i - 1) % 2
                for zt, nm in ((za, "za"), (zc, "zc")):
                    P.op("pool", lambda e, zt=zt, pb=pb: e.tensor_copy(out=zt[0][:, zb, :, 0:H], in_=zt[0][:, pb, :, T:T + H]),
                         reads=[(nm, 0, pb)], writes=[(nm, 0, zb)])
            else:
                for zt, nm in ((za, "za"), (zc, "zc")):
                    P.op("pool", lambda e, zt=zt: e.memset(zt[0][:, zb, :, 0:H], 0.0), writes=[(nm, 0, zb)])
            if (not lat) or i == NT - 1:
                for zt, nm in ((za, "za"), (zc, "zc")):
                    P.op("pool", lambda e, zt=zt: e.memset(zt[0][:, zb, :, H + n:H + n + H], 0.0), writes=[(nm, 0, zb)])
            xin = [None, None]
            sig = [None, None]
            deferred = []
            for gi in range(8):
                wt, wkey = load_granule(l * NG + gi)
                for jj in range(2):
                    j = gi * 2 + jj
                    bk, bkey = bank("mm")
                    mm_group(bk[:, :n], bkey, [(wt[:, (jj * 8 + k) * 128:(jj * 8 + k + 1) * 128], hT[:, k, :n]) for k in range(8)],
                             reads=[wkey], kreads=[[("hT", k)] for k in range(8)])
                    while len(deferred) > 1:
                        deferred.pop(0)()
                    if j in (0, 1):
                        tm, tk = tmp32()
                        xin[j] = (tm, tk)
                        P.op("act", lambda e, tm=tm, bk=bk: e.activation(out=tm[:, :n], in_=bk[:, :n], func=AF.Identity), writes=[bkey, tk])
                    elif j in (2, 3):
                        tm, tk = xin[j - 2]
                        P.op("dve", lambda e, tm=tm, bk=bk, j=j: e.tensor_tensor(out=zav[:, j - 2, H:H + n], in0=bk[:, :n], in1=tm[:, :n], op=ALU.mult),
                             reads=[tk], writes=[bkey, zak])
                    elif j in (4, 5):
                        P.op("act", lambda e, bk=bk, j=j: e.activation(out=bgate[0][:, zb, j - 4, :n], in_=bk[:, :n], func=AF.Identity), writes=[bkey, bgk])
                    elif 6 <= j <= 9:
                        deferred.append(qk_chain(bk, bkey, n, l, True, [(QT[0][zb][:, j - 6, :n], 0, 128)], qtk, ct, st_, tabkeys))
                    elif j == 10:
                        if lat:
                            c0 = (i % 3) * T
                            o_ap, o_key = [(kring[g_][64 * g_:64 * g_ + 64, c0:c0 + T], 64 * g_, 64 * g_ + 64) for g_ in range(2)], ("kring", 0)
                        else:
                            o_ap, o_key = [(ctxK[g_][64 * g_:64 * g_ + 64, :], 64 * g_, 64 * g_ + 64) for g_ in range(2)], ("ctxK", 0)
                        deferred.append(qk_chain(bk, bkey, n, l, False, o_ap, o_key, ct, st_, tabkeys))
                    elif j == 11:
                        vt, vtk = tmpb()
                        P.op("act", lambda e, vt=vt, bk=bk: e.activation(out=vt[:, :n], in_=bk[:, :n], func=AF.Identity), writes=[bkey, vtk])

                        def vpart(vt=vt, vtk=vtk):
                            for bl in range(n // 128):
                                tb_, tbk = bank("st")
                                tbv = tb_[:].bitcast(BF16)[:, 0:128]
                                P.op("pe", lambda e, vt=vt, bl=bl, tbv=tbv: e.transpose(tbv, vt[:, bl * 128:(bl + 1) * 128], ident),
                                     reads=[vtk, "kb"], writes=[tbk])
                                if lat:
                                    gb = i * 4 + bl
                                    dst = vring[0][:, gb % 12]
                                    dkey = ("vring", 0)
                                    vcol = cpc("valid", gb)
                                else:
                                    dst = ctxV[0][:, bl]
                                    dkey = ("ctxV", 0)
                                    vcol = cone[:, 0:1]
                                P.op("dve", lambda e, dst=dst, tbv=tbv, vcol=vcol: e.tensor_scalar(
                                    out=dst[:, :, 0:64], in0=tbv.rearrange("p (g d) -> p g d", g=2), scalar1=vcol, scalar2=None, op0=ALU.mult),
                                    reads=["cp", "cone"], writes=[tbk, dkey])
                                P.op("pool", lambda e, dst=dst, vcol=vcol: e.tensor_scalar(
                                    out=dst[:, :, 64:128], in0=ones[:].rearrange("p (g d) -> p g d", g=2), scalar1=vcol, scalar2=None, op0=ALU.mult),
                                    reads=["cp", "cone", "ones"], writes=[dkey])
                        deferred.append(vpart)
                    elif j in (12, 13):
                        tm, tk = tmp32()
                        sig[j - 12] = (tm, tk)
                        P.op("act", lambda e, tm=tm, bk=bk: e.activation(out=tm[:, :n], in_=bk[:, :n], func=AF.Sigmoid), writes=[bkey, tk])
                    else:
                        tm, tk = sig[j - 14]
                        P.op("dve", lambda e, tm=tm, bk=bk, j=j: e.tensor_tensor(out=zcv[:, j - 14, H:H + n], in0=bk[:, :n], in1=tm[:, :n], op=ALU.mult),
                             reads=[tk], writes=[bkey, zck])
                    yield
            for fn in deferred:
                fn()
            if lat and i in (0, NT - 1):
                for zt, nm in ((za, "za"), (zc, "zc")):
                    for b in range(4):
                        P.op("pool", lambda e, zt=zt, b=b: e.tensor_scalar(
                            out=zt[0][:, zb, :, H + b * 128:H + (b + 1) * 128], in0=zt[0][:, zb, :, H + b * 128:H + (b + 1) * 128],
                            scalar1=cpc("valid", i * 4 + b), scalar2=None, op0=ALU.mult),
                            reads=["cp", (nm, 0, zb)], writes=[(nm, 0, zb)])
            if lat and i >= 1:
                pb = (i - 1) % 2
                for zt, nm in ((za, "za"), (zc, "zc")):
                    P.op("pool", lambda e, zt=zt, pb=pb: e.tensor_copy(out=zt[0][:, pb, :, H + T:H + T + H], in_=zt[0][:, zb, :, H:2 * H]),
                         reads=[(nm, 0, zb)], writes=[(nm, 0, pb)])
            yield

        def gen_attention(l, kind, i, n, zb, taps):
            lat = kind == "lat"
            lt0, ln_ = live_range(l, kind, i)
            for bl in range(lt0 // 128, (lt0 + ln_) // 128):
                gb = i * 4 + bl
                chunks = []
                if lat:
                    def kcol(b):
                        return slice((b % 12) * 128, (b % 12 + 1) * 128)
                    if gb >= 1:
                        chunks.append(("prev", [kring[g_][:, kcol(gb - 1)] for g_ in range(2)], vring[0][:, (gb - 1) % 12], ("kring", 0), ("vring", 0)))
                    chunks.append(("own", [kring[g_][:, kcol(gb)] for g_ in range(2)], vring[0][:, gb % 12], ("kring", 0), ("vring", 0)))
                    if gb + 1 < NB:
                        chunks.append(("next", [kring[g_][:, kcol(gb + 1)] for g_ in range(2)], vring[0][:, (gb + 1) % 12], ("kring", 0), ("vring", 0)))
                for cb in range(2):
                    chunks.append(("ctx", [ctxK[g_][:, cb * 128:(cb + 1) * 128] for g_ in range(2)], ctxV[0][:, cb], ("ctxK", 0), ("ctxV", 0)))
                pend = []

                def flush(pend):
                    for (g, v_ap2, pt2, ptk2, vkey2, ci2) in pend:
                        P.op("pe", lambda e, g=g, v_ap2=v_ap2, pt2=pt2, ci2=ci2: e.matmul(
                            banks[4 + g][:], lhsT=v_ap2[:, g, :], rhs=pt2[:], start=(ci2 == 0), stop=False),
                            reads=[vkey2, ptk2], writes=[("bank", 4 + g)], signal=False)
                for ci, (ckind, k_ap, v_ap, kkey, vkey) in enumerate(chunks):
                    cur = []
                    masked = ckind in ("prev", "next")
                    for g in range(2):
                        bk, bkey = bank("mm")
                        q_ap = QT[0][zb][:, :, bl * 128:(bl + 1) * 128]
                        P.op("pe", lambda e, bk=bk, k_ap=k_ap, q_ap=q_ap, g=g, masked=masked: e.matmul(
                            bk[:].rearrange("p (a b) -> p a b", a=4), lhsT=k_ap[g], rhs=q_ap, start=True, stop=not masked),
                            reads=[kkey, ("QT", 0, zb)], writes=[bkey], signal=not masked)
                        if masked:
                            mk = kb[:, 3 if ckind == "prev" else 4, :].unsqueeze(1).broadcast_to([128, 4, 128])
                            P.op("pe", lambda e, bk=bk, mk=mk: e.matmul(bk[:].rearrange("p (a b) -> p a b", a=4), lhsT=ident, rhs=mk, start=False, stop=True),
                                 reads=["kb"], writes=[bkey])
                        pi = rr["pT"] % 4
                        rr["pT"] += 1
                        pt, ptk = pT[pi], ("pT", pi)
                        P.op("act", lambda e, pt=pt, bk=bk: e.activation(out=pt[:], in_=bk[:], func=AF.Exp, scale=0.125), writes=[bkey, ptk])
                        cur.append((g, v_ap, pt, ptk, vkey, ci))
                    for fn in taps[:3]:
                        fn()
                    del taps[:3]
                    flush(pend)
                    pend = cur
                    yield
                flush(pend)
                for g in range(2):
                    P.op("pe", lambda e, g=g: e.matmul(banks[4 + g][:], lhsT=sinkL[0:1, :], rhs=esrow[0:1, g].rearrange("p a b -> p (a b)"), start=False, stop=True),
                         reads=["sinkL", "esrow"], writes=[("bank", 4 + g)])
                for g in range(2):
                    pv, pvk = banks[4 + g], ("bank", 4 + g)
                    rec, reck = tmp32()
                    P.op("act", lambda e, rec=rec, pv=pv: e.activation(out=rec[64:128, :], in_=pv[64:128, :], func=AF.Ln), writes=[pvk, reck])
                    P.op("act", lambda e, rec=rec: e.activation(out=rec[64:128, :], in_=rec[64:128, :], func=AF.Exp, scale=-1.0), reads=[reck], writes=[reck])
                    for par in range(2):
                        P.op("dve", lambda e, rec=rec, pv=pv, g=g, par=par, bl=bl: e.tensor_tensor(
                            out=ymix[64 * par:64 * par + 64, 2 + 2 * g:4 + 2 * g, bl * 128:(bl + 1) * 128],
                            in0=pv[0:64, :].rearrange("p (a b c) -> p a b c", a=2, b=2)[:, :, par, :],
                            in1=rec[64:128, :].rearrange("p (a b c) -> p a b c", a=2, b=2)[:, :, par, :], op=ALU.mult),
                            reads=[reck], writes=[pvk, "ymix"])
                yield

        def gen_mix(l, kind, i):
            lat = kind == "lat"
            n, s, zb, xap, xkey, slot = tile_ctx(l, kind, i)
            zak, zck, bgk = ("za", 0, zb), ("zc", 0, zb), ("bg", 0, zb)
            zav, zcv = za[0][:, zb], zc[0][:, zb]
            for c in range(2):
                acc, acck = tmp32()
                P.op("dve", lambda e, acc=acc, c=c: e.tensor_scalar(
                    out=acc[:, :n], in0=zav[:, c, H - 1:H - 1 + n], scalar1=cpc(("caw", l), 0 * 2 + c), scalar2=None, op0=ALU.mult),
                    reads=[zak, "cp"], writes=[acck])
                for tap in (1, 2):
                    P.op("dve", lambda e, acc=acc, c=c, tap=tap: e.scalar_tensor_tensor(
                        out=acc[:, :n], in0=zav[:, c, H - 1 + tap:H - 1 + tap + n], scalar=cpc(("caw", l), tap * 2 + c), in1=acc[:, :n],
                        op0=ALU.mult, op1=ALU.add), reads=[zak, "cp", acck], writes=[acck])
                P.op("dve", lambda e, acc=acc, c=c: e.tensor_tensor(out=ymix[:, c, :n], in0=acc[:, :n], in1=bgate[0][:, zb, c, :n], op=ALU.mult),
                     reads=[acck, bgk], writes=["ymix"])
                yield
            ycs = []
            taps = []
            for c in range(2):
                acc, acck = yacc[c], ("yacc", c)
                ycs.append((acc, acck))
                P.op("dve", lambda e, acc=acc, c=c: e.tensor_scalar(
                    out=acc[:, :n], in0=zcv[:, c, H - 15:H - 15 + n], scalar1=cpc(("ccw", l), c), scalar2=cpc(("ccb", l), c),
                    op0=ALU.mult, op1=ALU.add), reads=[zck, "cp"], writes=[acck])
            for tap in range(1, 31):
                for c in range(2):
                    acc, acck = ycs[c]
                    taps.append(lambda acc=acc, acck=acck, c=c, tap=tap: P.op("dve", lambda e: e.scalar_tensor_tensor(
                        out=acc[:, :n], in0=zcv[:, c, H - 15 + tap:H - 15 + tap + n], scalar=cpc(("ccw", l), tap * 2 + c), in1=acc[:, :n],
                        op0=ALU.mult, op1=ALU.add), reads=[zck, "cp", acck], writes=[acck]))
            yield
            yield from gen_attention(l, kind, i, n, zb, taps)
            for fn in taps:
                fn()
            b1, b1k = bank("st")
            b2, b2k = bank("st")
            ybs = []
            for c in range(2):
                acc, acck = ycs[c]
                yb, ybk = tmpb()
                ysq, ysqk = tmpb()
                ybs.append((yb, ybk, ysq, ysqk))
                P.op("act", lambda e, yb=yb, acc=acc: e.activation(out=yb[:, :n], in_=acc[:, :n], func=AF.Identity), reads=[acck], writes=[ybk])
                P.op("pool", lambda e, ysq=ysq, acc=acc: e.tensor_tensor(out=ysq[:, :n], in0=acc[:, :n], in1=acc[:, :n], op=ALU.mult), reads=[acck], writes=[ysqk])
            yield LNF
            mm_group(b1[:, :n], b1k, [(ones[:], ybs[c][0][:, :n]) for c in range(2)], reads=[ybs[0][1], ybs[1][1], "ones"])
            mm_group(b2[:, :n], b2k, [(ones[:], ybs[c][2][:, :n]) for c in range(2)], reads=[ybs[0][3], ybs[1][3], "ones"])
            mean, meank = tmp32()
            P.op("act", lambda e: e.activation(out=mean[:, :n], in_=b1[:, :n], func=AF.Identity, scale=1.0 / 256), writes=[b1k, meank])
            msq, msqk = tmp32()
            P.op("pool", lambda e: e.tensor_tensor(out=msq[:, :n], in0=mean[:, :n], in1=mean[:, :n], op=ALU.mult), reads=[meank], writes=[msqk])
            P.op("dve", lambda e: e.scalar_tensor_tensor(out=msq[:, :n], in0=b2[:, :n], scalar=1.0 / 256, in1=msq[:, :n], op0=ALU.mult, op1=ALU.subtract),
                 reads=[msqk], writes=[b2k, msqk])
            P.op("act", lambda e: e.activation(out=msq[:, :n], in_=msq[:, :n], func=AF.Ln, bias=epsc[:], scale=1.0), reads=[msqk, "epsc"], writes=[msqk])
            P.op("act", lambda e: e.activation(out=msq[:, :n], in_=msq[:, :n], func=AF.Exp, scale=-0.5), reads=[msqk], writes=[msqk])
            for c in range(2):
                acc, acck = ycs[c]
                P.op("pool", lambda e, acc=acc: e.tensor_tensor(out=acc[:, :n], in0=acc[:, :n], in1=mean[:, :n], op=ALU.subtract), reads=[acck, meank], writes=[acck])
                P.op("pool", lambda e, acc=acc: e.tensor_tensor(out=acc[:, :n], in0=acc[:, :n], in1=msq[:, :n], op=ALU.mult), reads=[acck, msqk], writes=[acck])
                P.op("act", lambda e, acc=acc, c=c: e.activation(out=ymix[:, 6 + c, :n], in_=acc[:, :n], func=AF.Silu,
                                                                  bias=cpc(("lnb", l), c), scale=cpc(("lng", l), c)),
                     reads=[acck, "cp"], writes=["ymix"])
            yield LNF
            yield LNF
            lt0, ln_ = live_range(l, kind, i)
            for gi in range(4):
                wt, wkey = load_granule(l * NG + 8 + gi)
                for jj in range(2):
                    m = gi * 2 + jj
                    bk, bkey = bank("mm")
                    mm_group(bk[:, :ln_], bkey, [(wt[:, (jj * 8 + k) * 128:(jj * 8 + k + 1) * 128], ymix[:, k, lt0:lt0 + ln_]) for k in range(8)],
                             reads=[wkey, "ymix"])
                    P.op("dve", lambda e, bk=bk, m=m: e.scalar_tensor_tensor(
                        out=xap[:, m, lt0:lt0 + ln_], in0=bk[:, :ln_], scalar=modc(l, 16 + m, s), in1=xap[:, m, lt0:lt0 + ln_], op0=ALU.mult, op1=ALU.add),
                        reads=[("mod", l), xkey], writes=[bkey, xkey])
                    yield
            yield from rms_norm_to(hT2, "hT2", xap, xkey, n, l, s, 1, mark="tail")

        def mlp_pieces(l, kind, i, last):
            lat = kind == "lat"
            n, s, zb, xap, xkey, slot = tile_ctx(l, kind, i)
            pieces = []

            lt0, ln_ = live_range(l, kind, i)

            def p_mlp1(half, gi, part):
                wt, wkey = load_granule(l * NG + 12 + half * 16 + gi * 2 + part)
                for jj in range(2):
                    hc = gi * 4 + part * 2 + jj
                    bk, bkey = bank("mm")
                    mm_group(bk[:, :ln_], bkey, [(wt[:, (jj * 8 + k) * 128:(jj * 8 + k + 1) * 128], hT2[:, k, lt0:lt0 + ln_]) for k in range(8)],
                             reads=[wkey], kreads=[[("hT2", k)] for k in range(8)])
                    ti = rr["tmlp"] % 2
                    rr["tmlp"] += 1
                    tm, tk = tmlp[ti], ("tmlp", ti)
                    P.op("act", lambda e, tm=tm, bk=bk: e.activation(out=tm[:, :ln_], in_=bk[:, :ln_], func=AF.Relu), writes=[bkey, tk])
                    P.op("pool", lambda e, tm=tm, hc=hc: e.tensor_tensor(out=hid[:, hc, :ln_], in0=tm[:, :ln_], in1=tm[:, :ln_], op=ALU.mult),
                         reads=[tk], writes=[("hid", hc)])

            def p_mlp2(half, gi, part):
                m = gi * 2 + part
                wt, wkey = load_granule(l * NG + 12 + half * 16 + 8 + m)
                bk, bkey = bank("mm")
                mm_group(bk[:, :ln_], bkey, [(wt[:, k * 128:(k + 1) * 128], hid[:, k, :ln_]) for k in range(16)],
                         reads=[wkey], kreads=[[("hid", k)] for k in range(16)])
                P.op("dve", lambda e, bk=bk, m=m: e.scalar_tensor_tensor(
                    out=xap[:, m, lt0:lt0 + ln_], in0=bk[:, :ln_], scalar=modc(l, 40 + m, s), in1=xap[:, m, lt0:lt0 + ln_], op0=ALU.mult, op1=ALU.add),
                    reads=[("mod", l), xkey], writes=[bkey, xkey])
                if half == 1 and gi == 3 and part == 1:
                    if lat:
                        dst = outT_v if last else x1T_v
                        P.dma("pool", lambda e: e.dma_start(out=dst[i], in_=xs[slot][:].rearrange("p a b -> p (a b)")),
                              reads=[xkey], writes=[("outT", i) if last else ("x1T", i)], sem=f"st{slot}")
                    else:
                        P.dma("pool", lambda e: e.dma_start(out=xc1T_v, in_=xs[slot][:, :, 0:CTX]), reads=[xkey], writes=["xc1T"], sem=f"st{slot}")
            for half in range(2):
                for gi in range(4):
                    for part in range(2):
                        pieces.append(lambda half=half, gi=gi, part=part: p_mlp1(half, gi, part))
                for gi in range(4):
                    for part in range(2):
                        pieces.append(lambda half=half, gi=gi, part=part: p_mlp2(half, gi, part))
            return pieces

        def run_step(lat_gens, pieces, lead=4, nforce=12, ntail=6, nyield=50):
            pieces = list(pieces)
            tail = pieces[len(pieces) - ntail:] if len(pieces) > ntail + lead else []
            pieces = pieces[:len(pieces) - len(tail)]
            lat_gens = list(lat_gens)
            if lat_gens:
                next(lat_gens[0], None)
            for _ in range(min(lead, len(pieces))):
                pieces.pop(0)()
            nfree = max(1, len(pieces) - nforce)
            every = max(2, nyield // nfree)
            cnt = 0
            for g in lat_gens:
                for y in g:
                    cnt += 1
                    if bgfast:
                        bgfast.pop(0)()
                    if bgq and cnt % 4 == 0:
                        bgq.pop(0)()
                    if y == "tail":
                        if pieces and DEBUG_SCHED:
                            print("run_step: leftover pieces at norm2:", len(pieces), "yields so far", cnt, "every", every)
                        while pieces:
                            pieces.pop(0)()
                        for _ in range(2):
                            if tail:
                                tail.pop(0)()
                    elif y == "force":
                        for _ in range(2):
                            if pieces:
                                pieces.pop(0)()
                    elif pieces and cnt % every == 0:
                        pieces.pop(0)()
            while pieces:
                pieces.pop(0)()
            while tail:
                tail.pop(0)()

        def gen_call(fn, *a):
            fn(*a)
            yield

        carry = []
        for l in range(NL):
            last = l == NL - 1
            if l == 1:
                while bgq:
                    bgq.pop(0)()
            head = [gen_call(layer_setup, l), gen_A(l, "ctx", 0)]
            if not last:
                run_step(head, carry)
                run_step([gen_mix(l, "ctx", 0)], [])
                carry = mlp_pieces(l, "ctx", 0, last)
                head = []
            gA0, gA1 = gen_A(l, "lat", 0), gen_A(l, "lat", 1)
            next(gA0)
            run_step(head + [gA0, gA1, gen_mix(l, "lat", 0)], carry, nyield=110 + 20 * len(head))
            if l + 1 < NL:
                for g2 in range(NG // 2):
                    bgq.insert(min(len(bgq), 3 * g2), lambda g2=g2: cast_layer(1, g2, g2 + 1))
            carry = mlp_pieces(l, "lat", 0, last)
            for i in range(1, NT):
                gens = []
                if i + 1 < NT:
                    gens.append(gen_A(l, "lat", i + 1))
                gens.append(gen_mix(l, "lat", i))
                run_step(gens, carry, nyield=54 if len(gens) == 2 else 20)
                carry = mlp_pieces(l, "lat", i, last)
        run_step([], carry)
        if debug:
            dbg = nc.dram_tensor("dbg", [128, 192 + 64], F32, kind="ExternalOutput").ap()
            for l_ in range(NL):
                P.dma("sp", lambda e, l_=l_: e.dma_start(out=dbg[:, l_ * 96:(l_ + 1) * 96], in_=mod[l_][:]), reads=[("mod", l_)], writes=[("dbg", l_)], sem="dbg")
                P.dma("sp", lambda e, l_=l_: e.dma_start(out=dbg[:, 192 + l_ * 32:192 + (l_ + 1) * 32], in_=avec[l_][:].rearrange("p a b c -> p (a b c)")), reads=[("avec", l_)], writes=[("dbg", l_)], sem="dbg")
            P.wait_all("sp", [("dbg", 0), ("dbg", 1)])
        P.wait_all("pool", [("outT", i) for i in range(NT)])
        P.wait_all("sp", [("outT", i) for i in range(NT)])
        P.emit()
    return nc


def _fm(v):
    return np.ascontiguousarray(np.asarray(v, np.float32).reshape(-1, 128).T)


def _w_in_colperm():
    cols = []
    cols += list(range(0, 256))
    cols += list(range(512, 768))
    cols += list(range(256, 512))
    for jq in range(4):
        cols += list(range(OFF_Q + 64 * jq, OFF_Q + 64 * jq + 64))
        cols += list(range(OFF_Q + 64 * (jq + 4), OFF_Q + 64 * (jq + 4) + 64))
    cols += list(range(1280, 1408))
    cols += list(range(1408, 1536))
    cols += list(range(1792, 2048))
    cols += list(range(1536, 1792))
    return np.array(cols)


def _granules_k8(W):
    nc_ = W.shape[1] // 128
    Wr = W.reshape(8, 128, nc_ // 2, 2, 128)
    return np.ascontiguousarray(Wr.transpose(2, 1, 3, 0, 4)).reshape(nc_ // 2, 128, GE)


def _granules_mlp2(W2, half):
    Wh = W2[half * 2048:(half + 1) * 2048].reshape(16, 128, 8, 128)
    return np.ascontiguousarray(Wh.transpose(2, 1, 0, 3)).reshape(8, 128, GE)


def _prep(inputs):
    f = lambda k: np.asarray(inputs[k], np.float32)
    x, c, ctx, c_ctx = f("x"), f("c"), f("ctx"), f("c_ctx")
    perm = _w_in_colperm()
    wp = []
    for l in range(NL):
        wp.append(_granules_k8(f("w_in")[l][:, perm]))
        wp.append(_granules_k8(f("w_out")[l]))
        w1 = _granules_k8(f("w_mlp1")[l])
        w2 = f("w_mlp2")[l]
        for half in range(2):
            wp.append(w1[half * 8:(half + 1) * 8])
            wp.append(_granules_mlp2(w2, half))
    wpack = np.ascontiguousarray(np.concatenate(wp, axis=0))
    assert wpack.shape == (NL * NG, 128, GE)
    wmod = np.ascontiguousarray(f("w_mod").reshape(NL, 8, 128, 12, 512).transpose(0, 3, 2, 1, 4)).reshape(NL, 12, 128, 8 * 512)
    kpack = np.zeros((128, 5, 128), np.float32)
    kpack[:, 0, :] = np.eye(128)
    kpack[:, 1, :] = np.kron(np.eye(2), np.ones((64, 64)))
    R = np.zeros((128, 128), np.float32)
    for m in range(128):
        d = m % 64
        dd = d % 32
        if dd < 16:
            R[m + 16, m] = -1.0
        else:
            R[m - 16, m] = 1.0
    kpack[:, 2, :] = R
    kk = np.arange(128)[:, None]
    qq = np.arange(128)[None, :]
    kpack[:, 3, :] = np.where(kk >= qq, 0.0, -30000.0)
    kpack[:, 4, :] = np.where(kk <= qq, 0.0, -30000.0)
    kpack = kpack.reshape(128, NKP)
    inv_freq = (10000.0 ** (-np.arange(0, 32, 2, dtype=np.float32) / 32)).astype(np.float32)
    d = np.arange(128) % 64
    fidx = d % 16
    use_col = d >= 32
    in_maps = []
    for r in range(8):
        b, a = r // 4, (r % 4) * OWN
        pos = np.arange(a - HALO, a - HALO + NTOK)
        ok = (pos >= 0) & (pos < SEQ)
        xt = np.zeros((D, NTOK), np.float32)
        xt[:, ok] = x[b, pos[ok]].T
        xt = np.ascontiguousarray(xt.reshape(8, 128, NT, T).transpose(2, 1, 0, 3)).reshape(NT, 128, 8 * T)
        cpk = np.zeros((128, NCP), np.float32)
        for l in range(NL):
            cpk[:, CPO[("n1g", l)]:CPO[("n1g", l)] + 8] = _fm(f("norm1_g")[l])
            cpk[:, CPO[("n2g", l)]:CPO[("n2g", l)] + 8] = _fm(f("norm2_g")[l])
            cpk[:, CPO[("bmod", l)]:CPO[("bmod", l)] + 48] = _fm(f("b_mod")[l])
            cpk[:, CPO[("caw", l)]:CPO[("caw", l)] + 6] = f("conv_a_w")[l].reshape(3, 2, 128).transpose(2, 0, 1).reshape(128, 6)
            cpk[:, CPO[("ccw", l)]:CPO[("ccw", l)] + 62] = f("conv_c_w")[l].reshape(31, 2, 128).transpose(2, 0, 1).reshape(128, 62)
            cpk[:, CPO[("ccb", l)]:CPO[("ccb", l)] + 2] = _fm(f("conv_c_b")[l])
            cpk[:, CPO[("lng", l)]:CPO[("lng", l)] + 2] = _fm(f("ln_c_g")[l])
            cpk[:, CPO[("lnb", l)]:CPO[("lnb", l)] + 2] = _fm(f("ln_c_b")[l])
            cpk[:, CPO[("qg", l)]] = f("q_norm_g")[l][d]
            cpk[:, CPO[("kg", l)]] = f("k_norm_g")[l][d]
            cpk[:, CPO[("sink", l)]:CPO[("sink", l)] + 8] = f("attn_sink")[l][None, :]
        cc = np.stack([_fm(c[b]), _fm(c_ctx)], axis=2).reshape(128, 16)
        cpk[:, CPO["c"]:CPO["c"] + 16] = cc
        cpk[:, CPO["valid"]:CPO["valid"] + NB] = ok.reshape(NB, 128).T
        p = np.clip(pos, 0, SEQ - 1)
        row, col = (p // 64).astype(np.float32), (p % 64).astype(np.float32)
        ang = np.where(use_col[:, None], col[None, :], row[None, :]).astype(np.float32) * inv_freq[fidx][:, None]
        in_maps.append({
            "xT": xt, "ctxT": np.ascontiguousarray(ctx[b].T), "wpack": wpack, "wmod": wmod, "cpack": cpk, "kpack": kpack,
            "ropeC": np.cos(ang).astype(np.float32), "ropeS": np.sin(ang).astype(np.float32),
        })
    return in_maps


def kernel(**inputs):
    in_maps = _prep(inputs)
    nc = build()
    res = run_bass_kernel_spmd(nc, in_maps, core_ids=list(range(8)))
    out = np.zeros((2, SEQ, D), np.float32)
    for r in range(8):
        b, a = r // 4, (r % 4) * OWN
        o = res.results[r]["outT"].reshape(NT, 128, 8, T).transpose(2, 1, 0, 3).reshape(D, NTOK)
        out[b, a:a + OWN] = o[:, HALO:HALO + OWN].T
    return out
```

```python
import contextlib
import numpy as np
import concourse.bass as bass
import concourse.mybir as mybir
from concourse.bass_utils import run_bass_kernel_spmd

F32 = mybir.dt.float32
BF16 = mybir.dt.bfloat16
F32R = mybir.dt.float32r
ALU = mybir.AluOpType
AF = mybir.ActivationFunctionType

D = 1024
SEQ = 16384
CTX = 256
NL = 2
T = 512
NT = 9
NTOK = NT * T
HALO = 256
OWN = 4096
NB = NTOK // 128
H = 16
ZW = H + T + H
OFF_Q = 768
EPS = 1e-6
NG = 44
NSET = 1
GE = 2048

ENGS = ("pe", "act", "dve", "pool", "sp")
LNF = "force"
SELF_SYNC = True
DEBUG_SCHED = False


class Prog:
    def __init__(self, nc, stack):
        self.nc = nc
        self.stack = stack
        self.ops = {e: [] for e in ENGS}
        self.sig = {e: 0 for e in ENGS}
        self.pending = {e: False for e in ENGS}
        self.sems = {e: stack.enter_context(nc.semaphore("s_" + e)) for e in ENGS}
        self.seen = {e: {} for e in ENGS}
        self.lastw = {}
        self.readers = {}
        self.dsems = {}
        self.semobj = {("e", e): self.sems[e] for e in ENGS}

    def dma_sem(self, name):
        if name not in self.dsems:
            s = self.stack.enter_context(self.nc.semaphore("d_" + name))
            self.dsems[name] = [s, 0]
            self.semobj[("d", name)] = s
        return self.dsems[name]

    def _deps(self, reads, writes):
        deps = {}

        def add(tok):
            if tok is None:
                return
            k, v = tok
            if deps.get(k, 0) < v:
                deps[k] = v
        for k in reads:
            add(self.lastw.get(k))
        for k in writes:
            add(self.lastw.get(k))
            for sk, v in self.readers.get(k, {}).items():
                add((sk, v))
        return deps

    def _waits(self, eng, deps):
        waits = []
        seen = self.seen[eng]
        for sk, v in deps.items():
            if sk == ("e", eng) and (eng == "pe" or not SELF_SYNC):
                continue
            if seen.get(sk, 0) >= v:
                continue
            seen[sk] = v
            waits.append((self.semobj[sk], v))
        return waits

    def _note(self, tok, reads, writes):
        sk, v = tok
        for k in reads:
            r = self.readers.setdefault(k, {})
            if r.get(sk, 0) < v:
                r[sk] = v
        for k in writes:
            self.lastw[k] = tok
            self.readers[k] = {}

    def op(self, eng, fn, reads=(), writes=(), signal=True):
        waits = self._waits(eng, self._deps(reads, writes))
        if signal:
            self.sig[eng] += 1
            tok = (("e", eng), self.sig[eng])
            self.pending[eng] = False
        else:
            tok = (("e", eng), self.sig[eng] + 1)
            self.pending[eng] = True
        self.ops[eng].append((waits, fn, (self.sems[eng], 1) if signal else None))
        self._note(tok, reads, writes)

    def dma(self, eng, fn, reads=(), writes=(), sem="misc"):
        waits = self._waits(eng, self._deps(reads, writes))
        ds = self.dma_sem(sem)
        ds[1] += 16
        tok = (("d", sem), ds[1])
        self.ops[eng].append((waits, fn, (ds[0], 16)))
        self._note(tok, reads, writes)

    def wait_all(self, eng, keys):
        waits = self._waits(eng, self._deps(keys, keys))
        self.ops[eng].append((waits, None, None))

    def emit(self):
        nc = self.nc
        for e in ENGS:
            assert not self.pending[e], f"engine {e} has trailing unsignalled ops"
        with nc.Block() as block:
            def run(e):
                def body(engine):
                    for waits, fn, inc in self.ops[e]:
                        for s, v in waits:
                            engine.wait_ge(s, v)
                        if fn is not None:
                            ins = fn(engine)
                            if inc is not None:
                                ins.then_inc(inc[0], inc[1])
                return body
            block.tensor(run("pe"))
            block.scalar(run("act"))
            block.vector(run("dve"))
            block.gpsimd(run("pool"))
            block.sync(run("sp"))


def _cp_layout():
    off = {}
    n = 0

    def add(name, w):
        nonlocal n
        off[name] = n
        n += w
    for l in range(NL):
        add(("n1g", l), 8)
        add(("n2g", l), 8)
        add(("bmod", l), 48)
        add(("caw", l), 6)
        add(("ccw", l), 62)
        add(("ccb", l), 2)
        add(("lng", l), 2)
        add(("lnb", l), 2)
        add(("qg", l), 1)
        add(("kg", l), 1)
        add(("sink", l), 8)
    add("c", 16)
    add("valid", NB)
    return off, n


CPO, NCP = _cp_layout()
NKP = 5 * 128


def build(debug=False):
    nc = bass.Bass("TRN2", target_bir_lowering=False)
    xT = nc.dram_tensor("xT", [NT, 128, 8 * T], F32, kind="ExternalInput").ap()
    ctxT = nc.dram_tensor("ctxT", [D, CTX], F32, kind="ExternalInput").ap()
    wpack = nc.dram_tensor("wpack", [NL * NG, 128, GE], F32, kind="ExternalInput").ap()
    wmod = nc.dram_tensor("wmod", [NL, 12, 128, 8 * 512], F32, kind="ExternalInput").ap()
    cpack = nc.dram_tensor("cpack", [128, NCP], F32, kind="ExternalInput").ap()
    kpack = nc.dram_tensor("kpack", [128, NKP], F32, kind="ExternalInput").ap()
    ropeC = nc.dram_tensor("ropeC", [128, NTOK], F32, kind="ExternalInput").ap()
    ropeS = nc.dram_tensor("ropeS", [128, NTOK], F32, kind="ExternalInput").ap()
    outT = nc.dram_tensor("outT", [NT, 128, 8 * T], F32, kind="ExternalOutput").ap()
    wbf = nc.dram_tensor("wbf", [NL * NG, 128, GE], BF16, kind="Internal").ap()
    x1T = nc.dram_tensor("x1T", [NT, 128, 8 * T], F32, kind="Internal").ap()
    xc1T = nc.dram_tensor("xc1T", [D, CTX], F32, kind="Internal").ap()
    xT_v, x1T_v, outT_v = xT, x1T, outT
    ctxT_v = ctxT.rearrange("(c p) t -> p c t", p=128)
    xc1T_v = xc1T.rearrange("(c p) t -> p c t", p=128)

    with contextlib.ExitStack() as st:
        P = Prog(nc, st)

        def sb(name, shape, dt):
            return st.enter_context(nc.sbuf_tensor(name, shape, dt))

        xs = [sb(f"xs{i}", [128, 8, T], F32) for i in range(3)]
        wring = [sb(f"wr{i}", [128, GE], BF16) for i in range(6)]
        hT = sb("hT", [128, 8, T], BF16)
        hT2 = sb("hT2", [128, 8, T], BF16)
        tmlp = [sb(f"tmlp{i}", [128, T], F32) for i in range(2)]
        yacc = [sb(f"yacc{i}", [128, T], F32) for i in range(2)]
        ymix = sb("ymix", [128, 8, T], BF16)
        hid = sb("hid", [128, 16, T], BF16)
        pT = [sb(f"pT{i}", [128, T], BF16) for i in range(4)]
        ctab = sb("ctab", [128, T], F32)
        stab = sb("stab", [128, T], F32)
        QT = [[sb(f"QT{l}_{b}", [128, 4, T], BF16) for b in range(2)] for l in range(NSET)]
        kring = [sb(f"kring{g}", [128, 12 * 128], BF16) for g in range(2)]
        vring = [sb(f"vring{l}", [128, 12, 2, 128], BF16) for l in range(NSET)]
        za = [sb(f"za{l}", [128, 2, 2, ZW], F32) for l in range(NSET)]
        zc = [sb(f"zc{l}", [128, 2, 2, ZW], F32) for l in range(NSET)]
        bgate = [sb(f"bg{l}", [128, 2, 2, T], BF16) for l in range(NSET)]
        ctxK = [sb(f"ctxK{g}", [128, CTX], BF16) for g in range(2)]
        ctxV = [sb(f"ctxV{l}", [128, 2, 2, 128], BF16) for l in range(NSET)]
        NT32, NTB = 9, 6
        t32 = [sb(f"t32_{i}", [128, T], F32) for i in range(NT32)]
        tb16 = [sb(f"tb_{i}", [128, T], BF16) for i in range(NTB)]
        cp = sb("cp", [128, NCP], F32)
        kb = sb("kb", [128, 5, 128], BF16)
        ones = sb("ones", [128, 128], BF16)
        epsc = sb("epsc", [128, 1], F32)
        cone = sb("cone", [128, 1], F32)
        silc = sb("silc", [128, 16], F32)
        wst = sb("wst", [128, 8, 128], F32)
        mod = [sb(f"mod{l}", [128, 96], F32) for l in range(NL)]
        avec = [sb(f"avec{l}", [128, 2, 2, 8], F32) for l in range(NL)]
        esk = [sb(f"esk{l}", [128, 8], F32) for l in range(NSET)]
        esrow = sb("esrow", [1, 2, 4, 128], BF16)
        sinkL = sb("sinkL", [1, 128], BF16)
        Rq = [sb(f"Rq{l}", [128, 128], BF16) for l in range(NSET)]
        Rk = [sb(f"Rk{l}", [128, 128], BF16) for l in range(NSET)]
        banks = [st.enter_context(nc.psum_tensor(f"bank{i}", [128, T], F32)) for i in range(8)]
        kp32 = hid[:].rearrange("p a b -> p (a b)").bitcast(F32)[:, 0:NKP]

        ident = kb[:, 0, :]
        BDm = kb[:, 1, :]

        rr = {"mm": 0, "st": 0, "t32": 0, "tb": 0, "pT": 0, "wr": 0, "tmlp": 0}
        bank_groups = {"mm": [0, 1, 2, 3], "st": [6, 7]}

        last_bank = {"key": None}

        def bank(group):
            ids = bank_groups[group]
            i = ids[rr[group] % len(ids)]
            rr[group] += 1
            if group == "mm":
                last_bank["key"] = ("bank", i)
            return banks[i], ("bank", i)

        def tmp32():
            i = rr["t32"] % NT32
            rr["t32"] += 1
            return t32[i], ("t32", i)

        def tmpb():
            i = rr["tb"] % NTB
            rr["tb"] += 1
            return tb16[i], ("tb", i)

        def cpc(name, c=0, w=1):
            o = CPO[name] + c
            return cp[:, o:o + w]

        gstate = {"n": 0}

        def load_granule(gidx):
            s = rr["wr"] % 6
            rr["wr"] += 1
            P.dma("sp", lambda e: e.dma_start(out=wring[s][:], in_=wbf[gidx]),
                  reads=[("wbf", gidx // 2)], writes=[("wr", s)], sem=f"wr{s}")
            return wring[s], ("wr", s)

        def mm_group(out_ap, bkey, pairs, reads, kreads=None):
            n = len(pairs)
            for k, (l_ap, r_ap) in enumerate(pairs):
                rd = list(reads) + (list(kreads[k]) if kreads else [])
                P.op("pe", lambda e, l_ap=l_ap, r_ap=r_ap, k=k: e.matmul(out_ap, lhsT=l_ap, rhs=r_ap, start=(k == 0), stop=(k == n - 1)),
                     reads=rd, writes=[bkey], signal=(k == n - 1))

        P.dma("sp", lambda e: e.dma_start(out=cp[:], in_=cpack), writes=["cp"], sem="c0")
        P.dma("sp", lambda e: e.dma_start(out=kp32, in_=kpack), writes=[("hid", 0), ("hid", 1), ("hid", 2)], sem="c2")
        def cast_layer(l, lo=0, hi=NG // 2, gate=()):
            for g2 in range(l * NG // 2 + lo, l * NG // 2 + hi):
                P.dma("pool", lambda e, g2=g2: e.dma_start(out=wbf[2 * g2:2 * g2 + 2], in_=wpack[2 * g2:2 * g2 + 2]),
                      reads=list(gate), writes=[("wbf", g2)], sem=f"cast{g2}")
        cast_layer(0, 0, 4)
        P.op("dve", lambda e: e.tensor_copy(out=kb[:].rearrange("p a b -> p (a b)"), in_=kp32), reads=[("hid", 0), ("hid", 1), ("hid", 2)], writes=["kb"])
        P.op("pool", lambda e: e.memset(ones[:], 1.0), writes=["ones"])
        P.op("pool", lambda e: e.memset(epsc[:], EPS), writes=["epsc"])
        P.op("pool", lambda e: e.memset(sinkL[0:1, 0:64], 0.0), writes=["sinkL"])
        P.op("pool", lambda e: e.memset(sinkL[0:1, 64:128], 1.0), writes=["sinkL"])
        P.op("pool", lambda e: e.memset(cone[:], 1.0), writes=["cone"])
        for l in range(NSET):
            for g_ in range(2):
                P.op("pool", lambda e, g_=g_: e.memset(kring[g_][:], 0.0), writes=[("kring", 0)])
                P.op("pool", lambda e, g_=g_: e.memset(ctxK[g_][:], 0.0), writes=[("ctxK", 0)])
            P.op("pool", lambda e, l=l: e.memset(vring[l % NSET][:].rearrange("p a b c -> p (a b c)"), 0.0), writes=[("vring", l % NSET)])
            for b in range(2):
                P.op("pool", lambda e, l=l, b=b: e.memset(za[l % NSET][:, b].rearrange("p a b -> p (a b)"), 0.0), writes=[("za", l % NSET, b)])
                P.op("pool", lambda e, l=l, b=b: e.memset(zc[l % NSET][:, b].rearrange("p a b -> p (a b)"), 0.0), writes=[("zc", l % NSET, b)])
        P.op("act", lambda e: e.activation(out=silc[:], in_=cpc("c", 0, 16), func=AF.Silu), reads=["cp"], writes=["silc"])
        def mod_finish(l):
            for s_ in range(2):
                for w, (sco, gname) in enumerate(((8, "n1g"), (32, "n2g"))):
                    P.op("dve", lambda e, s_=s_, w=w, sco=sco, gname=gname: e.scalar_tensor_tensor(
                        out=avec[l][:, s_, w, :], in0=mod[l][:, sco * 2 + s_:(sco + 8) * 2 + s_:2], scalar=1.0,
                        in1=cpc((gname, l), 0, 8), op0=ALU.add, op1=ALU.mult),
                        reads=[("mod", l), "cp"], writes=[("avec", l)])

        hid32 = hid[:].rearrange("p a b -> p (a b)").bitcast(F32)
        HIDK = [("hid", k) for k in range(16)]

        def mod_startup(l):
            mb, mbk = banks[4 + l], ("bank", 4 + l)
            for piece in range(12):
                if piece % 2 == 0:
                    stg, skey, ssem = xs[1][:].rearrange("p a b -> p (a b)"), [("xs", 1)], "xs1"
                else:
                    stg, skey, ssem = hid32, HIDK, "hidst"
                P.dma("pool", lambda e, piece=piece, stg=stg: e.dma_start(out=stg, in_=wmod[l][piece]), writes=skey, sem=ssem)
                stv = stg.rearrange("p (k c) -> p k c", k=8)
                for jj in range(4):
                    j = piece * 4 + jj
                    for k in range(8):
                        P.op("pe", lambda e, stv=stv, jj=jj, j=j, k=k: e.matmul(
                            mb[:, j * 2:j * 2 + 2], lhsT=stv[:, k, jj * 128:(jj + 1) * 128], rhs=silc[:, k * 2:k * 2 + 2],
                            start=(k == 0), stop=(k == 7)),
                            reads=skey + ["silc"], writes=[mbk], signal=(k == 7))
                if piece == 11:
                    cast_layer(0, 4, 6, gate=skey)
                    for g2 in range(6, NG // 2):
                        bgfast.append(lambda g2=g2: cast_layer(0, g2, g2 + 1, gate=[last_bank["key"]] if last_bank["key"] else ()))
            for s_ in range(2):
                P.op("dve", lambda e, s_=s_: e.tensor_tensor(
                    out=mod[l][:, s_:96:2], in0=mb[:, s_:96:2], in1=cpc(("bmod", l), 0, 48), op=ALU.add),
                    reads=["cp"], writes=[mbk, ("mod", l)])
            mod_finish(l)

        def mod_background(l):
            items = []
            for j in range(48):
                def dma_j(j=j):
                    P.dma("sp", lambda e: e.dma_start(out=wst[:], in_=wmod[l][j // 4].rearrange("p (k c) -> p k c", k=8)[:, :, (j % 4) * 128:(j % 4 + 1) * 128]), writes=["wst"], sem="wst")

                def mm_j(j=j):
                    bk, bkey = bank("st")
                    for k in range(8):
                        P.op("pe", lambda e, k=k: e.matmul(bk[:, 0:2], lhsT=wst[:, k, :], rhs=silc[:, k * 2:k * 2 + 2], start=(k == 0), stop=(k == 7)),
                             reads=["wst", "silc"], writes=[bkey], signal=(k == 7))
                    P.op("dve", lambda e: e.tensor_scalar(out=mod[l][:, 2 * j:2 * j + 2], in0=bk[:, 0:2], scalar1=cpc(("bmod", l), j), scalar2=None, op0=ALU.add),
                         reads=["cp"], writes=[bkey, ("mod", l)])
                items += [dma_j, mm_j]
            items.append(lambda: mod_finish(l))
            return items

        bgfast = []
        P.dma("sp", lambda e: e.dma_start(out=xs[2][:, :, 0:CTX], in_=ctxT_v), writes=[("xs", 2)], sem="xs2")
        P.dma("sp", lambda e: e.dma_start(out=xs[0][:].rearrange("p a b -> p (a b)"), in_=xT_v[0]), writes=[("xs", 0)], sem="xs0")
        mod_startup(0)
        bgq = mod_background(1)

        if debug:
            dbg0 = nc.dram_tensor("dbg0", [128, 192], F32, kind="ExternalOutput").ap()
            for l_ in range(NL):
                P.dma("sp", lambda e, l_=l_: e.dma_start(out=dbg0[:, l_ * 96:(l_ + 1) * 96], in_=mod[l_][:]), reads=[("mod", l_)], writes=[("dbg0", l_)], sem="dbg0")

        def layer_setup(l):
            ls = l % NSET
            P.op("act", lambda e: e.activation(out=esk[ls][:], in_=cpc(("sink", l), 0, 8), func=AF.Exp), reads=["cp"], writes=[("esk", ls)])
            for g in range(2):
                for j in range(4):
                    P.op("dve", lambda e, g=g, j=j: e.tensor_copy(out=esrow[0:1, g, j, :], in_=esk[ls][0:1, 4 * g + j:4 * g + j + 1].broadcast_to([1, 128])),
                         reads=[("esk", ls)], writes=["esrow"])
            P.op("dve", lambda e: e.tensor_scalar(out=Rq[ls][:], in0=kb[:, 2, :], scalar1=cpc(("qg", l)), scalar2=None, op0=ALU.mult),
                 reads=["kb", "cp"], writes=[("Rq", ls)])
            P.op("dve", lambda e: e.tensor_scalar(out=Rk[ls][:], in0=kb[:, 2, :], scalar1=cpc(("kg", l)), scalar2=None, op0=ALU.mult),
                 reads=["kb", "cp"], writes=[("Rk", ls)])

        def modc(l, j, s):
            return mod[l][:, j * 2 + s:j * 2 + s + 1]

        def tile_ctx(l, kind, i):
            if kind == "lat":
                slot = i % 3
                return T, 0, i % 2, xs[slot][:], ("xs", slot), slot
            slot = 2 if l == 0 else 1
            return CTX, 1, 1, xs[slot][:, :, 0:CTX], ("xs", slot), slot

        def live_range(l, kind, i):
            if kind != "lat":
                return 0, CTX
            trim = 128 if l == 0 else 256
            if i == 0:
                return trim, T - trim
            if i == NT - 1:
                return 0, T - trim
            return 0, T

        def rms_norm_to(hbuf, hkey, xap, xkey, n, l, s, which, mark="force"):
            sq = ymix
            P.op("act", lambda e: e.activation(out=sq[:, :, :n], in_=xap, func=AF.Square), reads=[xkey], writes=["ymix"])
            yield mark
            bk, bkey = bank("st")
            mm_group(bk[:, :n], bkey, [(ones[:], sq[:, c, :n]) for c in range(8)], reads=["ymix", "ones"])
            rs, rskey = tmp32()
            P.op("act", lambda e: e.activation(out=rs[:, :n], in_=bk[:, :n], func=AF.Ln, bias=epsc[:], scale=1.0 / D),
                 reads=["epsc"], writes=[bkey, rskey])
            P.op("act", lambda e: e.activation(out=rs[:, :n], in_=rs[:, :n], func=AF.Exp, scale=-0.5), reads=[rskey], writes=[rskey])
            sho = 0 if which == 0 else 24
            for c in range(8):
                tm, tmkey = tmp32()
                P.op("dve", lambda e, c=c, tm=tm: e.scalar_tensor_tensor(
                    out=tm[:, :n], in0=xap[:, c, :], scalar=avec[l][:, s, which, c:c + 1], in1=rs[:, :n], op0=ALU.mult, op1=ALU.mult),
                    reads=[xkey, rskey, ("avec", l)], writes=[tmkey])
                P.op("act", lambda e, c=c, tm=tm: e.activation(out=hbuf[:, c, :n], in_=tm[:, :n], func=AF.Identity, bias=modc(l, sho + c, s)),
                     reads=[tmkey, ("mod", l)], writes=[(hkey, c)])
            yield mark
            yield mark

        def qk_chain(bk, bkey, n, l, is_q, out_ap, out_key, ct, st_, tabkeys):
            qb, qbk = tmpb()
            P.op("act", lambda e: e.activation(out=qb[:, :n], in_=bk[:, :n], func=AF.Identity), writes=[bkey, qbk])
            sqq, sqk = tmpb()
            P.op("pool", lambda e: e.tensor_tensor(out=sqq[:, :n], in0=qb[:, :n], in1=qb[:, :n], op=ALU.mult), reads=[qbk], writes=[sqk])

            def part2():
                b1, b1k = bank("st")
                mm_group(b1[:, :n], b1k, [(BDm, sqq[:, :n])], reads=[sqk, "kb"])
                if ct is not None:
                    b2, b2k = bank("st")
                    R = Rq[0] if is_q else Rk[0]
                    mm_group(b2[:, :n], b2k, [(R[:], qb[:, :n])], reads=[qbk, ("Rq", 0), ("Rk", 0)])
                rq, rqk = tmp32()
                P.op("act", lambda e: e.activation(out=rq[:, :n], in_=b1[:, :n], func=AF.Ln, bias=epsc[:], scale=1.0 / 64),
                     reads=["epsc"], writes=[b1k, rqk])
                P.op("act", lambda e: e.activation(out=rq[:, :n], in_=rq[:, :n], func=AF.Exp, scale=-0.5), reads=[rqk], writes=[rqk])
                gcol = cpc(("qg", l)) if is_q else cpc(("kg", l))
                if ct is None:
                    for (o_ap, p0, p1) in out_ap:
                        P.op("dve", lambda e, o_ap=o_ap, p0=p0, p1=p1: e.scalar_tensor_tensor(
                            out=o_ap, in0=qb[p0:p1, :n], scalar=gcol[p0:p1], in1=rq[p0:p1, :n], op0=ALU.mult, op1=ALU.mult),
                            reads=[qbk, "cp", rqk], writes=[out_key])
                    return
                t1, t1k = tmp32()
                P.op("dve", lambda e: e.scalar_tensor_tensor(out=t1[:, :n], in0=qb[:, :n], scalar=gcol, in1=ct, op0=ALU.mult, op1=ALU.mult),
                     reads=[qbk, "cp"] + tabkeys, writes=[t1k])
                t2, t2k = tmp32()
                P.op("dve", lambda e: e.tensor_tensor(out=t2[:, :n], in0=b2[:, :n], in1=st_, op=ALU.mult), reads=tabkeys, writes=[b2k, t2k])
                P.op("dve", lambda e: e.tensor_tensor(out=t1[:, :n], in0=t1[:, :n], in1=t2[:, :n], op=ALU.add), reads=[t1k, t2k], writes=[t1k])
                for (o_ap, p0, p1) in out_ap:
                    P.op("dve", lambda e, o_ap=o_ap, p0=p0, p1=p1: e.tensor_tensor(out=o_ap, in0=t1[p0:p1, :n], in1=rq[p0:p1, :n], op=ALU.mult),
                         reads=[t1k, rqk], writes=[out_key])
            return part2

        def gen_A(l, kind, i):
            lat = kind == "lat"
            n, s, zb, xap, xkey, slot = tile_ctx(l, kind, i)
            if lat:
                src = xT_v if l == 0 else x1T_v
                P.dma("sp", lambda e: e.dma_start(out=ctab[:], in_=ropeC[:, i * T:(i + 1) * T]), writes=["ctab"], sem="tabc")
                P.dma("sp", lambda e: e.dma_start(out=stab[:], in_=ropeS[:, i * T:(i + 1) * T]), writes=["stab"], sem="tabs")
                if not (l == 0 and i == 0):
                    P.dma("sp" if (l == 0 and i == 1) else "pool",
                          lambda e: e.dma_start(out=xs[slot][:].rearrange("p a b -> p (a b)"), in_=src[i]),
                          reads=[("x1T", i)] if l > 0 else [], writes=[xkey], sem=f"xs{slot}")
                ct, st_, tabkeys = ctab[:, :n], stab[:, :n], ["ctab", "stab"]
            else:
                src = ctxT_v if l == 0 else xc1T_v
                if l > 0:
                    P.dma("sp", lambda e: e.dma_start(out=xs[slot][:, :, 0:CTX], in_=src),
                          reads=["xc1T"], writes=[xkey], sem=f"xs{slot}")
                ct, st_, tabkeys = None, None, []
            yield
            yield from rms_norm_to(hT, "hT", xap, xkey, n, l, s, 0)
            zak, zck, bgk, qtk = ("za", 0, zb), ("zc", 0, zb), ("bg", 0, zb), ("QT", 0, zb)
            zav, zcv = za[0][:, zb], zc[0][:, zb]
            if lat and i >= 1:
                pb = (i - 1) % 2
                for zt, nm in ((za, "za"), (zc, "zc")):
                    P.op("pool", lambda e, zt=zt, pb=pb: e.tensor_copy(out=zt[0][:, zb, :, 0:H], in_=zt[0][:, pb, :, T:T + H]),
                         reads=[(nm, 0, pb)], writes=[(nm, 0, zb)])
            else:
                for zt, nm in ((za, "za"), (zc, "zc")):
                    P.op("pool", lambda e, zt=zt: e.memset(zt[0][:, zb, :, 0:H], 0.0), writes=[(nm, 0, zb)])
            if (not lat) or i == NT - 1:
                for zt, nm in ((za, "za"), (zc, "zc")):
                    P.op("pool", lambda e, zt=zt: e.memset(zt[0][:, zb, :, H + n:H + n + H], 0.0), writes=[(nm, 0, zb)])
            xin = [None, None]
            sig = [None, None]
            deferred = []
            for gi in range(8):
                wt, wkey = load_granule(l * NG + gi)
                for jj in range(2):
                    j = gi * 2 + jj
                    bk, bkey = bank("mm")
                    mm_group(bk[:, :n], bkey, [(wt[:, (jj * 8 + k) * 128:(jj * 8 + k + 1) * 128], hT[:, k, :n]) for k in range(8)],
                             reads=[wkey], kreads=[[("hT", k)] for k in range(8)])
                    while len(deferred) > 1:
                        deferred.pop(0)()
                    if j in (0, 1):
                        tm, tk = tmp32()
                        xin[j] = (tm, tk)
                        P.op("act", lambda e, tm=tm, bk=bk: e.activation(out=tm[:, :n], in_=bk[:, :n], func=AF.Identity), writes=[bkey, tk])
                    elif j in (2, 3):
                        tm, tk = xin[j - 2]
                        P.op("dve", lambda e, tm=tm, bk=bk, j=j: e.tensor_tensor(out=zav[:, j - 2, H:H + n], in0=bk[:, :n], in1=tm[:, :n], op=ALU.mult),
                             reads=[tk], writes=[bkey, zak])
                    elif j in (4, 5):
                        P.op("act", lambda e, bk=bk, j=j: e.activation(out=bgate[0][:, zb, j - 4, :n], in_=bk[:, :n], func=AF.Identity), writes=[bkey, bgk])
                    elif 6 <= j <= 9:
                        deferred.append(qk_chain(bk, bkey, n, l, True, [(QT[0][zb][:, j - 6, :n], 0, 128)], qtk, ct, st_, tabkeys))
                    elif j == 10:
                        if lat:
                            c0 = (i % 3) * T
                            o_ap, o_key = [(kring[g_][64 * g_:64 * g_ + 64, c0:c0 + T], 64 * g_, 64 * g_ + 64) for g_ in range(2)], ("kring", 0)
                        else:
                            o_ap, o_key = [(ctxK[g_][64 * g_:64 * g_ + 64, :], 64 * g_, 64 * g_ + 64) for g_ in range(2)], ("ctxK", 0)
                        deferred.append(qk_chain(bk, bkey, n, l, False, o_ap, o_key, ct, st_, tabkeys))
                    elif j == 11:
                        vt, vtk = tmpb()
                        P.op("act", lambda e, vt=vt, bk=bk: e.activation(out=vt[:, :n], in_=bk[:, :n], func=AF.Identity), writes=[bkey, vtk])

                        def vpart(vt=vt, vtk=vtk):
                            for bl in range(n // 128):
                                tb_, tbk = bank("st")
                                tbv = tb_[:].bitcast(BF16)[:, 0:128]
                                P.op("pe", lambda e, vt=vt, bl=bl, tbv=tbv: e.transpose(tbv, vt[:, bl * 128:(bl + 1) * 128], ident),
                                     reads=[vtk, "kb"], writes=[tbk])
                                if lat:
                                    gb = i * 4 + bl
                                    dst = vring[0][:, gb % 12]
                                    dkey = ("vring", 0)
                                    vcol = cpc("valid", gb)
                                else:
                                    dst = ctxV[0][:, bl]
                                    dkey = ("ctxV", 0)
                                    vcol = cone[:, 0:1]
                                P.op("dve", lambda e, dst=dst, tbv=tbv, vcol=vcol: e.tensor_scalar(
                                    out=dst[:, :, 0:64], in0=tbv.rearrange("p (g d) -> p g d", g=2), scalar1=vcol, scalar2=None, op0=ALU.mult),
                                    reads=["cp", "cone"], writes=[tbk, dkey])
                                P.op("pool", lambda e, dst=dst, vcol=vcol: e.tensor_scalar(
                                    out=dst[:, :, 64:128], in0=ones[:].rearrange("p (g d) -> p g d", g=2), scalar1=vcol, scalar2=None, op0=ALU.mult),
                                    reads=["cp", "cone", "ones"], writes=[dkey])
                        deferred.append(vpart)
                    elif j in (12, 13):
                        tm, tk = tmp32()
                        sig[j - 12] = (tm, tk)
                        P.op("act", lambda e, tm=tm, bk=bk: e.activation(out=tm[:, :n], in_=bk[:, :n], func=AF.Sigmoid), writes=[bkey, tk])
                    else:
                        tm, tk = sig[j - 14]
                        P.op("dve", lambda e, tm=tm, bk=bk, j=j: e.tensor_tensor(out=zcv[:, j - 14, H:H + n], in0=bk[:, :n], in1=tm[:, :n], op=ALU.mult),
                             reads=[tk], writes=[bkey, zck])
                    yield
            for fn in deferred:
                fn()
            if lat and i in (0, NT - 1):
                for zt, nm in ((za, "za"), (zc, "zc")):
                    for b in range(4):
                        P.op("pool", lambda e, zt=zt, b=b: e.tensor_scalar(
                            out=zt[0][:, zb, :, H + b * 128:H + (b + 1) * 128], in0=zt[0][:, zb, :, H + b * 128:H + (b + 1) * 128],
                            scalar1=cpc("valid", i * 4 + b), scalar2=None, op0=ALU.mult),
                            reads=["cp", (nm, 0, zb)], writes=[(nm, 0, zb)])
            if lat and i >= 1:
                pb = (i - 1) % 2
                for zt, nm in ((za, "za"), (zc, "zc")):
                    P.op("pool", lambda e, zt=zt, pb=pb: e.tensor_copy(out=zt[0][:, pb, :, H + T:H + T + H], in_=zt[0][:, zb, :, H:2 * H]),
                         reads=[(nm, 0, zb)], writes=[(nm, 0, pb)])
            yield

        def gen_attention(l, kind, i, n, zb, taps):
            lat = kind == "lat"
            lt0, ln_ = live_range(l, kind, i)
            for bl in range(lt0 // 128, (lt0 + ln_) // 128):
                gb = i * 4 + bl
                chunks = []
                if lat:
                    def kcol(b):
                        return slice((b % 12) * 128, (b % 12 + 1) * 128)
                    if gb >= 1:
                        chunks.append(("prev", [kring[g_][:, kcol(gb - 1)] for g_ in range(2)], vring[0][:, (gb - 1) % 12], ("kring", 0), ("vring", 0)))
                    chunks.append(("own", [kring[g_][:, kcol(gb)] for g_ in range(2)], vring[0][:, gb % 12], ("kring", 0), ("vring", 0)))
                    if gb + 1 < NB:
                        chunks.append(("next", [kring[g_][:, kcol(gb + 1)] for g_ in range(2)], vring[0][:, (gb + 1) % 12], ("kring", 0), ("vring", 0)))
                for cb in range(2):
                    chunks.append(("ctx", [ctxK[g_][:, cb * 128:(cb + 1) * 128] for g_ in range(2)], ctxV[0][:, cb], ("ctxK", 0), ("ctxV", 0)))
                pend = []

                def flush(pend):
                    for (g, v_ap2, pt2, ptk2, vkey2, ci2) in pend:
                        P.op("pe", lambda e, g=g, v_ap2=v_ap2, pt2=pt2, ci2=ci2: e.matmul(
                            banks[4 + g][:], lhsT=v_ap2[:, g, :], rhs=pt2[:], start=(ci2 == 0), stop=False),
                            reads=[vkey2, ptk2], writes=[("bank", 4 + g)], signal=False)
                for ci, (ckind, k_ap, v_ap, kkey, vkey) in enumerate(chunks):
                    cur = []
                    masked = ckind in ("prev", "next")
                    for g in range(2):
                        bk, bkey = bank("mm")
                        q_ap = QT[0][zb][:, :, bl * 128:(bl + 1) * 128]
                        P.op("pe", lambda e, bk=bk, k_ap=k_ap, q_ap=q_ap, g=g, masked=masked: e.matmul(
                            bk[:].rearrange("p (a b) -> p a b", a=4), lhsT=k_ap[g], rhs=q_ap, start=True, stop=not masked),
                            reads=[kkey, ("QT", 0, zb)], writes=[bkey], signal=not masked)
                        if masked:
                            mk = kb[:, 3 if ckind == "prev" else 4, :].unsqueeze(1).broadcast_to([128, 4, 128])
                            P.op("pe", lambda e, bk=bk, mk=mk: e.matmul(bk[:].rearrange("p (a b) -> p a b", a=4), lhsT=ident, rhs=mk, start=False, stop=True),
                                 reads=["kb"], writes=[bkey])
                        pi = rr["pT"] % 4
                        rr["pT"] += 1
                        pt, ptk = pT[pi], ("pT", pi)
                        P.op("act", lambda e, pt=pt, bk=bk: e.activation(out=pt[:], in_=bk[:], func=AF.Exp, scale=0.125), writes=[bkey, ptk])
                        cur.append((g, v_ap, pt, ptk, vkey, ci))
                    for fn in taps[:3]:
                        fn()
                    del taps[:3]
                    flush(pend)
                    pend = cur
                    yield
                flush(pend)
                for g in range(2):
                    P.op("pe", lambda e, g=g: e.matmul(banks[4 + g][:], lhsT=sinkL[0:1, :], rhs=esrow[0:1, g].rearrange("p a b -> p (a b)"), start=False, stop=True),
                         reads=["sinkL", "esrow"], writes=[("bank", 4 + g)])
                for g in range(2):
                    pv, pvk = banks[4 + g], ("bank", 4 + g)
                    rec, reck = tmp32()
                    P.op("act", lambda e, rec=rec, pv=pv: e.activation(out=rec[64:128, :], in_=pv[64:128, :], func=AF.Ln), writes=[pvk, reck])
                    P.op("act", lambda e, rec=rec: e.activation(out=rec[64:128, :], in_=rec[64:128, :], func=AF.Exp, scale=-1.0), reads=[reck], writes=[reck])
                    for par in range(2):
                        P.op("dve", lambda e, rec=rec, pv=pv, g=g, par=par, bl=bl: e.tensor_tensor(
                            out=ymix[64 * par:64 * par + 64, 2 + 2 * g:4 + 2 * g, bl * 128:(bl + 1) * 128],
                            in0=pv[0:64, :].rearrange("p (a b c) -> p a b c", a=2, b=2)[:, :, par, :],
                            in1=rec[64:128, :].rearrange("p (a b c) -> p a b c", a=2, b=2)[:, :, par, :], op=ALU.mult),
                            reads=[reck], writes=[pvk, "ymix"])
                yield

        def gen_mix(l, kind, i):
            lat = kind == "lat"
            n, s, zb, xap, xkey, slot = tile_ctx(l, kind, i)
            zak, zck, bgk = ("za", 0, zb), ("zc", 0, zb), ("bg", 0, zb)
            zav, zcv = za[0][:, zb], zc[0][:, zb]
            for c in range(2):
                acc, acck = tmp32()
                P.op("dve", lambda e, acc=acc, c=c: e.tensor_scalar(
                    out=acc[:, :n], in0=zav[:, c, H - 1:H - 1 + n], scalar1=cpc(("caw", l), 0 * 2 + c), scalar2=None, op0=ALU.mult),
                    reads=[zak, "cp"], writes=[acck])
                for tap in (1, 2):
                    P.op("dve", lambda e, acc=acc, c=c, tap=tap: e.scalar_tensor_tensor(
                        out=acc[:, :n], in0=zav[:, c, H - 1 + tap:H - 1 + tap + n], scalar=cpc(("caw", l), tap * 2 + c), in1=acc[:, :n],
                        op0=ALU.mult, op1=ALU.add), reads=[zak, "cp", acck], writes=[acck])
                P.op("dve", lambda e, acc=acc, c=c: e.tensor_tensor(out=ymix[:, c, :n], in0=acc[:, :n], in1=bgate[0][:, zb, c, :n], op=ALU.mult),
                     reads=[acck, bgk], writes=["ymix"])
                yield
            ycs = []
            taps = []
            for c in range(2):
                acc, acck = yacc[c], ("yacc", c)
                ycs.append((acc, acck))
                P.op("dve", lambda e, acc=acc, c=c: e.tensor_scalar(
                    out=acc[:, :n], in0=zcv[:, c, H - 15:H - 15 + n], scalar1=cpc(("ccw", l), c), scalar2=cpc(("ccb", l), c),
                    op0=ALU.mult, op1=ALU.add), reads=[zck, "cp"], writes=[acck])
            for tap in range(1, 31):
                for c in range(2):
                    acc, acck = ycs[c]
                    taps.append(lambda acc=acc, acck=acck, c=c, tap=tap: P.op("dve", lambda e: e.scalar_tensor_tensor(
                        out=acc[:, :n], in0=zcv[:, c, H - 15 + tap:H - 15 + tap + n], scalar=cpc(("ccw", l), tap * 2 + c), in1=acc[:, :n],
                        op0=ALU.mult, op1=ALU.add), reads=[zck, "cp", acck], writes=[acck]))
            yield
            yield from gen_attention(l, kind, i, n, zb, taps)
            for fn in taps:
                fn()
            b1, b1k = bank("st")
            b2, b2k = bank("st")
            ybs = []
            for c in range(2):
                acc, acck = ycs[c]
                yb, ybk = tmpb()
                ysq, ysqk = tmpb()
                ybs.append((yb, ybk, ysq, ysqk))
                P.op("act", lambda e, yb=yb, acc=acc: e.activation(out=yb[:, :n], in_=acc[:, :n], func=AF.Identity), reads=[acck], writes=[ybk])
                P.op("pool", lambda e, ysq=ysq, acc=acc: e.tensor_tensor(out=ysq[:, :n], in0=acc[:, :n], in1=acc[:, :n], op=ALU.mult), reads=[acck], writes=[ysqk])
            yield LNF
            mm_group(b1[:, :n], b1k, [(ones[:], ybs[c][0][:, :n]) for c in range(2)], reads=[ybs[0][1], ybs[1][1], "ones"])
            mm_group(b2[:, :n], b2k, [(ones[:], ybs[c][2][:, :n]) for c in range(2)], reads=[ybs[0][3], ybs[1][3], "ones"])
            mean, meank = tmp32()
            P.op("act", lambda e: e.activation(out=mean[:, :n], in_=b1[:, :n], func=AF.Identity, scale=1.0 / 256), writes=[b1k, meank])
            msq, msqk = tmp32()
            P.op("pool", lambda e: e.tensor_tensor(out=msq[:, :n], in0=mean[:, :n], in1=mean[:, :n], op=ALU.mult), reads=[meank], writes=[msqk])
            P.op("dve", lambda e: e.scalar_tensor_tensor(out=msq[:, :n], in0=b2[:, :n], scalar=1.0 / 256, in1=msq[:, :n], op0=ALU.mult, op1=ALU.subtract),
                 reads=[msqk], writes=[b2k, msqk])
            P.op("act", lambda e: e.activation(out=msq[:, :n], in_=msq[:, :n], func=AF.Ln, bias=epsc[:], scale=1.0), reads=[msqk, "epsc"], writes=[msqk])
            P.op("act", lambda e: e.activation(out=msq[:, :n], in_=msq[:, :n], func=AF.Exp, scale=-0.5), reads=[msqk], writes=[msqk])
            for c in range(2):
                acc, acck = ycs[c]
                P.op("pool", lambda e, acc=acc: e.tensor_tensor(out=acc[:, :n], in0=acc[:, :n], in1=mean[:, :n], op=ALU.subtract), reads=[acck, meank], writes=[acck])
                P.op("pool", lambda e, acc=acc: e.tensor_tensor(out=acc[:, :n], in0=acc[:, :n], in1=msq[:, :n], op=ALU.mult), reads=[acck, msqk], writes=[acck])
                P.op("act", lambda e, acc=acc, c=c: e.activation(out=ymix[:, 6 + c, :n], in_=acc[:, :n], func=AF.Silu,
                                                                  bias=cpc(("lnb", l), c), scale=cpc(("lng", l), c)),
                     reads=[acck, "cp"], writes=["ymix"])
            yield LNF
            yield LNF
            lt0, ln_ = live_range(l, kind, i)
            for gi in range(4):
                wt, wkey = load_granule(l * NG + 8 + gi)
                for jj in range(2):
                    m = gi * 2 + jj
                    bk, bkey = bank("mm")
                    mm_group(bk[:, :ln_], bkey, [(wt[:, (jj * 8 + k) * 128:(jj * 8 + k + 1) * 128], ymix[:, k, lt0:lt0 + ln_]) for k in range(8)],
                             reads=[wkey, "ymix"])
                    P.op("dve", lambda e, bk=bk, m=m: e.scalar_tensor_tensor(
                        out=xap[:, m, lt0:lt0 + ln_], in0=bk[:, :ln_], scalar=modc(l, 16 + m, s), in1=xap[:, m, lt0:lt0 + ln_], op0=ALU.mult, op1=ALU.add),
                        reads=[("mod", l), xkey], writes=[bkey, xkey])
                    yield
            yield from rms_norm_to(hT2, "hT2", xap, xkey, n, l, s, 1, mark="tail")

        def mlp_pieces(l, kind, i, last):
            lat = kind == "lat"
            n, s, zb, xap, xkey, slot = tile_ctx(l, kind, i)
            pieces = []

            lt0, ln_ = live_range(l, kind, i)

            def p_mlp1(half, gi, part):
                wt, wkey = load_granule(l * NG + 12 + half * 16 + gi * 2 + part)
                for jj in range(2):
                    hc = gi * 4 + part * 2 + jj
                    bk, bkey = bank("mm")
                    mm_group(bk[:, :ln_], bkey, [(wt[:, (jj * 8 + k) * 128:(jj * 8 + k + 1) * 128], hT2[:, k, lt0:lt0 + ln_]) for k in range(8)],
                             reads=[wkey], kreads=[[("hT2", k)] for k in range(8)])
                    ti = rr["tmlp"] % 2
                    rr["tmlp"] += 1
                    tm, tk = tmlp[ti], ("tmlp", ti)
                    P.op("act", lambda e, tm=tm, bk=bk: e.activation(out=tm[:, :ln_], in_=bk[:, :ln_], func=AF.Relu), writes=[bkey, tk])
                    P.op("pool", lambda e, tm=tm, hc=hc: e.tensor_tensor(out=hid[:, hc, :ln_], in0=tm[:, :ln_], in1=tm[:, :ln_], op=ALU.mult),
                         reads=[tk], writes=[("hid", hc)])

            def p_mlp2(half, gi, part):
                m = gi * 2 + part
                wt, wkey = load_granule(l * NG + 12 + half * 16 + 8 + m)
                bk, bkey = bank("mm")
                mm_group(bk[:, :ln_], bkey, [(wt[:, k * 128:(k + 1) * 128], hid[:, k, :ln_]) for k in range(16)],
                         reads=[wkey], kreads=[[("hid", k)] for k in range(16)])
                P.op("dve", lambda e, bk=bk, m=m: e.scalar_tensor_tensor(
                    out=xap[:, m, lt0:lt0 + ln_], in0=bk[:, :ln_], scalar=modc(l, 40 + m, s), in1=xap[:, m, lt0:lt0 + ln_], op0=ALU.mult, op1=ALU.add),
                    reads=[("mod", l), xkey], writes=[bkey, xkey])
                if half == 1 and gi == 3 and part == 1:
                    if lat:
                        dst = outT_v if last else x1T_v
                        P.dma("pool", lambda e: e.dma_start(out=dst[i], in_=xs[slot][:].rearrange("p a b -> p (a b)")),
                              reads=[xkey], writes=[("outT", i) if last else ("x1T", i)], sem=f"st{slot}")
                    else:
                        P.dma("pool", lambda e: e.dma_start(out=xc1T_v, in_=xs[slot][:, :, 0:CTX]), reads=[xkey], writes=["xc1T"], sem=f"st{slot}")
            for half in range(2):
                for gi in range(4):
                    for part in range(2):
                        pieces.append(lambda half=half, gi=gi, part=part: p_mlp1(half, gi, part))
                for gi in range(4):
                    for part in range(2):
                        pieces.append(lambda half=half, gi=gi, part=part: p_mlp2(half, gi, part))
            return pieces

        def run_step(lat_gens, pieces, lead=4, nforce=12, ntail=6, nyield=50):
            pieces = list(pieces)
            tail = pieces[len(pieces) - ntail:] if len(pieces) > ntail + lead else []
            pieces = pieces[:len(pieces) - len(tail)]
            lat_gens = list(lat_gens)
            if lat_gens:
                next(lat_gens[0], None)
            for _ in range(min(lead, len(pieces))):
                pieces.pop(0)()
            nfree = max(1, len(pieces) - nforce)
            every = max(2, nyield // nfree)
            cnt = 0
            for g in lat_gens:
                for y in g:
                    cnt += 1
                    if bgfast:
                        bgfast.pop(0)()
                    if bgq and cnt % 4 == 0:
                        bgq.pop(0)()
                    if y == "tail":
                        if pieces and DEBUG_SCHED:
                            print("run_step: leftover pieces at norm2:", len(pieces), "yields so far", cnt, "every", every)
                        while pieces:
                            pieces.pop(0)()
                        for _ in range(2):
                            if tail:
                                tail.pop(0)()
                    elif y == "force":
                        for _ in range(2):
                            if pieces:
                                pieces.pop(0)()
                    elif pieces and cnt % every == 0:
                        pieces.pop(0)()
            while pieces:
                pieces.pop(0)()
            while tail:
                tail.pop(0)()

        def gen_call(fn, *a):
            fn(*a)
            yield

        carry = []
        for l in range(NL):
            last = l == NL - 1
            if l == 1:
                while bgq:
                    bgq.pop(0)()
            head = [gen_call(layer_setup, l), gen_A(l, "ctx", 0)]
            if not last:
                run_step(head, carry)
                run_step([gen_mix(l, "ctx", 0)], [])
                carry = mlp_pieces(l, "ctx", 0, last)
                head = []
            gA0, gA1 = gen_A(l, "lat", 0), gen_A(l, "lat", 1)
            next(gA0)
            run_step(head + [gA0, gA1, gen_mix(l, "lat", 0)], carry, nyield=110 + 20 * len(head))
            if l + 1 < NL:
                for g2 in range(NG // 2):
                    bgq.insert(min(len(bgq), 3 * g2), lambda g2=g2: cast_layer(1, g2, g2 + 1))
            carry = mlp_pieces(l, "lat", 0, last)
            for i in range(1, NT):
                gens = []
                if i + 1 < NT:
                    gens.append(gen_A(l, "lat", i + 1))
                gens.append(gen_mix(l, "lat", i))
                run_step(gens, carry, nyield=54 if len(gens) == 2 else 20)
                carry = mlp_pieces(l, "lat", i, last)
        run_step([], carry)
        if debug:
            dbg = nc.dram_tensor("dbg", [128, 192 + 64], F32, kind="ExternalOutput").ap()
            for l_ in range(NL):
                P.dma("sp", lambda e, l_=l_: e.dma_start(out=dbg[:, l_ * 96:(l_ + 1) * 96], in_=mod[l_][:]), reads=[("mod", l_)], writes=[("dbg", l_)], sem="dbg")
                P.dma("sp", lambda e, l_=l_: e.dma_start(out=dbg[:, 192 + l_ * 32:192 + (l_ + 1) * 32], in_=avec[l_][:].rearrange("p a b c -> p (a b c)")), reads=[("avec", l_)], writes=[("dbg", l_)], sem="dbg")
            P.wait_all("sp", [("dbg", 0), ("dbg", 1)])
        P.wait_all("pool", [("outT", i) for i in range(NT)])
        P.wait_all("sp", [("outT", i) for i in range(NT)])
        P.emit()
    return nc


def _fm(v):
    return np.ascontiguousarray(np.asarray(v, np.float32).reshape(-1, 128).T)


def _w_in_colperm():
    cols = []
    cols += list(range(0, 256))
    cols += list(range(512, 768))
    cols += list(range(256, 512))
    for jq in range(4):
        cols += list(range(OFF_Q + 64 * jq, OFF_Q + 64 * jq + 64))
        cols += list(range(OFF_Q + 64 * (jq + 4), OFF_Q + 64 * (jq + 4) + 64))
    cols += list(range(1280, 1408))
    cols += list(range(1408, 1536))
    cols += list(range(1792, 2048))
    cols += list(range(1536, 1792))
    return np.array(cols)


def _granules_k8(W):
    nc_ = W.shape[1] // 128
    Wr = W.reshape(8, 128, nc_ // 2, 2, 128)
    return np.ascontiguousarray(Wr.transpose(2, 1, 3, 0, 4)).reshape(nc_ // 2, 128, GE)


def _granules_mlp2(W2, half):
    Wh = W2[half * 2048:(half + 1) * 2048].reshape(16, 128, 8, 128)
    return np.ascontiguousarray(Wh.transpose(2, 1, 0, 3)).reshape(8, 128, GE)


def _prep(inputs):
    f = lambda k: np.asarray(inputs[k], np.float32)
    x, c, ctx, c_ctx = f("x"), f("c"), f("ctx"), f("c_ctx")
    perm = _w_in_colperm()
    wp = []
    for l in range(NL):
        wp.append(_granules_k8(f("w_in")[l][:, perm]))
        wp.append(_granules_k8(f("w_out")[l]))
        w1 = _granules_k8(f("w_mlp1")[l])
        w2 = f("w_mlp2")[l]
        for half in range(2):
            wp.append(w1[half * 8:(half + 1) * 8])
            wp.append(_granules_mlp2(w2, half))
    wpack = np.ascontiguousarray(np.concatenate(wp, axis=0))
    assert wpack.shape == (NL * NG, 128, GE)
    wmod = np.ascontiguousarray(f("w_mod").reshape(NL, 8, 128, 12, 512).transpose(0, 3, 2, 1, 4)).reshape(NL, 12, 128, 8 * 512)
    kpack = np.zeros((128, 5, 128), np.float32)
    kpack[:, 0, :] = np.eye(128)
    kpack[:, 1, :] = np.kron(np.eye(2), np.ones((64, 64)))
    R = np.zeros((128, 128), np.float32)
    for m in range(128):
        d = m % 64
        dd = d % 32
        if dd < 16:
            R[m + 16, m] = -1.0
        else:
            R[m - 16, m] = 1.0
    kpack[:, 2, :] = R
    kk = np.arange(128)[:, None]
    qq = np.arange(128)[None, :]
    kpack[:, 3, :] = np.where(kk >= qq, 0.0, -30000.0)
    kpack[:, 4, :] = np.where(kk <= qq, 0.0, -30000.0)
    kpack = kpack.reshape(128, NKP)
    inv_freq = (10000.0 ** (-np.arange(0, 32, 2, dtype=np.float32) / 32)).astype(np.float32)
    d = np.arange(128) % 64
    fidx = d % 16
    use_col = d >= 32
    in_maps = []
    for r in range(8):
        b, a = r // 4, (r % 4) * OWN
        pos = np.arange(a - HALO, a - HALO + NTOK)
        ok = (pos >= 0) & (pos < SEQ)
        xt = np.zeros((D, NTOK), np.float32)
        xt[:, ok] = x[b, pos[ok]].T
        xt = np.ascontiguousarray(xt.reshape(8, 128, NT, T).transpose(2, 1, 0, 3)).reshape(NT, 128, 8 * T)
        cpk = np.zeros((128, NCP), np.float32)
        for l in range(NL):
            cpk[:, CPO[("n1g", l)]:CPO[("n1g", l)] + 8] = _fm(f("norm1_g")[l])
            cpk[:, CPO[("n2g", l)]:CPO[("n2g", l)] + 8] = _fm(f("norm2_g")[l])
            cpk[:, CPO[("bmod", l)]:CPO[("bmod", l)] + 48] = _fm(f("b_mod")[l])
            cpk[:, CPO[("caw", l)]:CPO[("caw", l)] + 6] = f("conv_a_w")[l].reshape(3, 2, 128).transpose(2, 0, 1).reshape(128, 6)
            cpk[:, CPO[("ccw", l)]:CPO[("ccw", l)] + 62] = f("conv_c_w")[l].reshape(31, 2, 128).transpose(2, 0, 1).reshape(128, 62)
            cpk[:, CPO[("ccb", l)]:CPO[("ccb", l)] + 2] = _fm(f("conv_c_b")[l])
            cpk[:, CPO[("lng", l)]:CPO[("lng", l)] + 2] = _fm(f("ln_c_g")[l])
            cpk[:, CPO[("lnb", l)]:CPO[("lnb", l)] + 2] = _fm(f("ln_c_b")[l])
            cpk[:, CPO[("qg", l)]] = f("q_norm_g")[l][d]
            cpk[:, CPO[("kg", l)]] = f("k_norm_g")[l][d]
            cpk[:, CPO[("sink", l)]:CPO[("sink", l)] + 8] = f("attn_sink")[l][None, :]
        cc = np.stack([_fm(c[b]), _fm(c_ctx)], axis=2).reshape(128, 16)
        cpk[:, CPO["c"]:CPO["c"] + 16] = cc
        cpk[:, CPO["valid"]:CPO["valid"] + NB] = ok.reshape(NB, 128).T
        p = np.clip(pos, 0, SEQ - 1)
        row, col = (p // 64).astype(np.float32), (p % 64).astype(np.float32)
        ang = np.where(use_col[:, None], col[None, :], row[None, :]).astype(np.float32) * inv_freq[fidx][:, None]
        in_maps.append({
            "xT": xt, "ctxT": np.ascontiguousarray(ctx[b].T), "wpack": wpack, "wmod": wmod, "cpack": cpk, "kpack": kpack,
            "ropeC": np.cos(ang).astype(np.float32), "ropeS": np.sin(ang).astype(np.float32),
        })
    return in_maps


def kernel(**inputs):
    in_maps = _prep(inputs)
    nc = build()
    res = run_bass_kernel_spmd(nc, in_maps, core_ids=list(range(8)))
    out = np.zeros((2, SEQ, D), np.float32)
    for r in range(8):
        b, a = r // 4, (r % 4) * OWN
        o = res.results[r]["outT"].reshape(NT, 128, 8, T).transpose(2, 1, 0, 3).reshape(D, NTOK)
        out[b, a:a + OWN] = o[:, HALO:HALO + OWN].T
    return out
```

```python
import contextlib
import numpy as np
import concourse.bass as bass
import concourse.mybir as mybir
from concourse.bass_utils import run_bass_kernel_spmd

F32 = mybir.dt.float32
BF16 = mybir.dt.bfloat16
F32R = mybir.dt.float32r
ALU = mybir.AluOpType
AF = mybir.ActivationFunctionType

D = 1024
SEQ = 16384
CTX = 256
NL = 2
T = 512
NT = 9
NTOK = NT * T
HALO = 256
OWN = 4096
NB = NTOK // 128
H = 16
ZW = H + T + H
OFF_Q = 768
EPS = 1e-6
NG = 44
NSET = 1
GE = 2048

ENGS = ("pe", "act", "dve", "pool", "sp")
LNF = "force"
SELF_SYNC = True
DEBUG_SCHED = False


class Prog:
    def __init__(self, nc, stack):
        self.nc = nc
        self.stack = stack
        self.ops = {e: [] for e in ENGS}
        self.sig = {e: 0 for e in ENGS}
        self.pending = {e: False for e in ENGS}
        self.sems = {e: stack.enter_context(nc.semaphore("s_" + e)) for e in ENGS}
        self.seen = {e: {} for e in ENGS}
        self.lastw = {}
        self.readers = {}
        self.dsems = {}
        self.semobj = {("e", e): self.sems[e] for e in ENGS}

    def dma_sem(self, name):
        if name not in self.dsems:
            s = self.stack.enter_context(self.nc.semaphore("d_" + name))
            self.dsems[name] = [s, 0]
            self.semobj[("d", name)] = s
        return self.dsems[name]

    def _deps(self, reads, writes):
        deps = {}

        def add(tok):
            if tok is None:
                return
            k, v = tok
            if deps.get(k, 0) < v:
                deps[k] = v
        for k in reads:
            add(self.lastw.get(k))
        for k in writes:
            add(self.lastw.get(k))
            for sk, v in self.readers.get(k, {}).items():
                add((sk, v))
        return deps

    def _waits(self, eng, deps):
        waits = []
        seen = self.seen[eng]
        for sk, v in deps.items():
            if sk == ("e", eng) and (eng == "pe" or not SELF_SYNC):
                continue
            if seen.get(sk, 0) >= v:
                continue
            seen[sk] = v
            waits.append((self.semobj[sk], v))
        return waits

    def _note(self, tok, reads, writes):
        sk, v = tok
        for k in reads:
            r = self.readers.setdefault(k, {})
            if r.get(sk, 0) < v:
                r[sk] = v
        for k in writes:
            self.lastw[k] = tok
            self.readers[k] = {}

    def op(self, eng, fn, reads=(), writes=(), signal=True):
        waits = self._waits(eng, self._deps(reads, writes))
        if signal:
            self.sig[eng] += 1
            tok = (("e", eng), self.sig[eng])
            self.pending[eng] = False
        else:
            tok = (("e", eng), self.sig[eng] + 1)
            self.pending[eng] = True
        self.ops[eng].append((waits, fn, (self.sems[eng], 1) if signal else None))
        self._note(tok, reads, writes)

    def dma(self, eng, fn, reads=(), writes=(), sem="misc"):
        waits = self._waits(eng, self._deps(reads, writes))
        ds = self.dma_sem(sem)
        ds[1] += 16
        tok = (("d", sem), ds[1])
        self.ops[eng].append((waits, fn, (ds[0], 16)))
        self._note(tok, reads, writes)

    def wait_all(self, eng, keys):
        waits = self._waits(eng, self._deps(keys, keys))
        self.ops[eng].append((waits, None, None))

    def emit(self):
        nc = self.nc
        for e in ENGS:
            assert not self.pending[e], f"engine {e} has trailing unsignalled ops"
        with nc.Block() as block:
            def run(e):
                def body(engine):
                    for waits, fn, inc in self.ops[e]:
                        for s, v in waits:
                            engine.wait_ge(s, v)
                        if fn is not None:
                            ins = fn(engine)
                            if inc is not None:
                                ins.then_inc(inc[0], inc[1])
                return body
            block.tensor(run("pe"))
            block.scalar(run("act"))
            block.vector(run("dve"))
            block.gpsimd(run("pool"))
            block.sync(run("sp"))


def _cp_layout():
    off = {}
    n = 0

    def add(name, w):
        nonlocal n
        off[name] = n
        n += w
    for l in range(NL):
        add(("n1g", l), 8)
        add(("n2g", l), 8)
        add(("bmod", l), 48)
        add(("caw", l), 6)
        add(("ccw", l), 62)
        add(("ccb", l), 2)
        add(("lng", l), 2)
        add(("lnb", l), 2)
        add(("qg", l), 1)
        add(("kg", l), 1)
        add(("sink", l), 8)
    add("c", 16)
    add("valid", NB)
    return off, n


CPO, NCP = _cp_layout()
NKP = 5 * 128


def build(debug=False):
    nc = bass.Bass("TRN2", target_bir_lowering=False)
    xT = nc.dram_tensor("xT", [NT, 128, 8 * T], F32, kind="ExternalInput").ap()
    ctxT = nc.dram_tensor("ctxT", [D, CTX], F32, kind="ExternalInput").ap()
    wpack = nc.dram_tensor("wpack", [NL * NG, 128, GE], F32, kind="ExternalInput").ap()
    wmod = nc.dram_tensor("wmod", [NL, 12, 128, 8 * 512], F32, kind="ExternalInput").ap()
    cpack = nc.dram_tensor("cpack", [128, NCP], F32, kind="ExternalInput").ap()
    kpack = nc.dram_tensor("kpack", [128, NKP], F32, kind="ExternalInput").ap()
    ropeC = nc.dram_tensor("ropeC", [128, NTOK], F32, kind="ExternalInput").ap()
    ropeS = nc.dram_tensor("ropeS", [128, NTOK], F32, kind="ExternalInput").ap()
    outT = nc.dram_tensor("outT", [NT, 128, 8 * T], F32, kind="ExternalOutput").ap()
    wbf = nc.dram_tensor("wbf", [NL * NG, 128, GE], BF16, kind="Internal").ap()
    x1T = nc.dram_tensor("x1T", [NT, 128, 8 * T], F32, kind="Internal").ap()
    xc1T = nc.dram_tensor("xc1T", [D, CTX], F32, kind="Internal").ap()
    xT_v, x1T_v, outT_v = xT, x1T, outT
    ctxT_v = ctxT.rearrange("(c p) t -> p c t", p=128)
    xc1T_v = xc1T.rearrange("(c p) t -> p c t", p=128)

    with contextlib.ExitStack() as st:
        P = Prog(nc, st)

        def sb(name, shape, dt):
            return st.enter_context(nc.sbuf_tensor(name, shape, dt))

        xs = [sb(f"xs{i}", [128, 8, T], F32) for i in range(3)]
        wring = [sb(f"wr{i}", [128, GE], BF16) for i in range(6)]
        hT = sb("hT", [128, 8, T], BF16)
        hT2 = sb("hT2", [128, 8, T], BF16)
        tmlp = [sb(f"tmlp{i}", [128, T], F32) for i in range(2)]
        yacc = [sb(f"yacc{i}", [128, T], F32) for i in range(2)]
        ymix = sb("ymix", [128, 8, T], BF16)
        hid = sb("hid", [128, 16, T], BF16)
        pT = [sb(f"pT{i}", [128, T], BF16) for i in range(4)]
        ctab = sb("ctab", [128, T], F32)
        stab = sb("stab", [128, T], F32)
        QT = [[sb(f"QT{l}_{b}", [128, 4, T], BF16) for b in range(2)] for l in range(NSET)]
        kring = [sb(f"kring{g}", [128, 12 * 128], BF16) for g in range(2)]
        vring = [sb(f"vring{l}", [128, 12, 2, 128], BF16) for l in range(NSET)]
        za = [sb(f"za{l}", [128, 2, 2, ZW], F32) for l in range(NSET)]
        zc = [sb(f"zc{l}", [128, 2, 2, ZW], F32) for l in range(NSET)]
        bgate = [sb(f"bg{l}", [128, 2, 2, T], BF16) for l in range(NSET)]
        ctxK = [sb(f"ctxK{g}", [128, CTX], BF16) for g in range(2)]
        ctxV = [sb(f"ctxV{l}", [128, 2, 2, 128], BF16) for l in range(NSET)]
        NT32, NTB = 9, 6
        t32 = [sb(f"t32_{i}", [128, T], F32) for i in range(NT32)]
        tb16 = [sb(f"tb_{i}", [128, T], BF16) for i in range(NTB)]
        cp = sb("cp", [128, NCP], F32)
        kb = sb("kb", [128, 5, 128], BF16)
        ones = sb("ones", [128, 128], BF16)
        epsc = sb("epsc", [128, 1], F32)
        cone = sb("cone", [128, 1], F32)
        silc = sb("silc", [128, 16], F32)
        wst = sb("wst", [128, 8, 128], F32)
        mod = [sb(f"mod{l}", [128, 96], F32) for l in range(NL)]
        avec = [sb(f"avec{l}", [128, 2, 2, 8], F32) for l in range(NL)]
        esk = [sb(f"esk{l}", [128, 8], F32) for l in range(NSET)]
        esrow = sb("esrow", [1, 2, 4, 128], BF16)
        sinkL = sb("sinkL", [1, 128], BF16)
        Rq = [sb(f"Rq{l}", [128, 128], BF16) for l in range(NSET)]
        Rk = [sb(f"Rk{l}", [128, 128], BF16) for l in range(NSET)]
        banks = [st.enter_context(nc.psum_tensor(f"bank{i}", [128, T], F32)) for i in range(8)]
        kp32 = hid[:].rearrange("p a b -> p (a b)").bitcast(F32)[:, 0:NKP]

        ident = kb[:, 0, :]
        BDm = kb[:, 1, :]

        rr = {"mm": 0, "st": 0, "t32": 0, "tb": 0, "pT": 0, "wr": 0, "tmlp": 0}
        bank_groups = {"mm": [0, 1, 2, 3], "st": [6, 7]}

        last_bank = {"key": None}

        def bank(group):
            ids = bank_groups[group]
            i = ids[rr[group] % len(ids)]
            rr[group] += 1
            if group == "mm":
                last_bank["key"] = ("bank", i)
            return banks[i], ("bank", i)

        def tmp32():
            i = rr["t32"] % NT32
            rr["t32"] += 1
            return t32[i], ("t32", i)

        def tmpb():
            i = rr["tb"] % NTB
            rr["tb"] += 1
            return tb16[i], ("tb", i)

        def cpc(name, c=0, w=1):
            o = CPO[name] + c
            return cp[:, o:o + w]

        gstate = {"n": 0}

        def load_granule(gidx):
            s = rr["wr"] % 6
            rr["wr"] += 1
            P.dma("sp", lambda e: e.dma_start(out=wring[s][:], in_=wbf[gidx]),
                  reads=[("wbf", gidx // 2)], writes=[("wr", s)], sem=f"wr{s}")
            return wring[s], ("wr", s)

        def mm_group(out_ap, bkey, pairs, reads, kreads=None):
            n = len(pairs)
            for k, (l_ap, r_ap) in enumerate(pairs):
                rd = list(reads) + (list(kreads[k]) if kreads else [])
                P.op("pe", lambda e, l_ap=l_ap, r_ap=r_ap, k=k: e.matmul(out_ap, lhsT=l_ap, rhs=r_ap, start=(k == 0), stop=(k == n - 1)),
                     reads=rd, writes=[bkey], signal=(k == n - 1))

        P.dma("sp", lambda e: e.dma_start(out=cp[:], in_=cpack), writes=["cp"], sem="c0")
        P.dma("sp", lambda e: e.dma_start(out=kp32, in_=kpack), writes=[("hid", 0), ("hid", 1), ("hid", 2)], sem="c2")
        def cast_layer(l, lo=0, hi=NG // 2, gate=()):
            for g2 in range(l * NG // 2 + lo, l * NG // 2 + hi):
                P.dma("pool", lambda e, g2=g2: e.dma_start(out=wbf[2 * g2:2 * g2 + 2], in_=wpack[2 * g2:2 * g2 + 2]),
                      reads=list(gate), writes=[("wbf", g2)], sem=f"cast{g2}")
        cast_layer(0, 0, 4)
        P.op("dve", lambda e: e.tensor_copy(out=kb[:].rearrange("p a b -> p (a b)"), in_=kp32), reads=[("hid", 0), ("hid", 1), ("hid", 2)], writes=["kb"])
        P.op("pool", lambda e: e.memset(ones[:], 1.0), writes=["ones"])
        P.op("pool", lambda e: e.memset(epsc[:], EPS), writes=["epsc"])
        P.op("pool", lambda e: e.memset(sinkL[0:1, 0:64], 0.0), writes=["sinkL"])
        P.op("pool", lambda e: e.memset(sinkL[0:1, 64:128], 1.0), writes=["sinkL"])
        P.op("pool", lambda e: e.memset(cone[:], 1.0), writes=["cone"])
        for l in range(NSET):
            for g_ in range(2):
                P.op("pool", lambda e, g_=g_: e.memset(kring[g_][:], 0.0), writes=[("kring", 0)])
                P.op("pool", lambda e, g_=g_: e.memset(ctxK[g_][:], 0.0), writes=[("ctxK", 0)])
            P.op("pool", lambda e, l=l: e.memset(vring[l % NSET][:].rearrange("p a b c -> p (a b c)"), 0.0), writes=[("vring", l % NSET)])
            for b in range(2):
                P.op("pool", lambda e, l=l, b=b: e.memset(za[l % NSET][:, b].rearrange("p a b -> p (a b)"), 0.0), writes=[("za", l % NSET, b)])
                P.op("pool", lambda e, l=l, b=b: e.memset(zc[l % NSET][:, b].rearrange("p a b -> p (a b)"), 0.0), writes=[("zc", l % NSET, b)])
        P.op("act", lambda e: e.activation(out=silc[:], in_=cpc("c", 0, 16), func=AF.Silu), reads=["cp"], writes=["silc"])
        def mod_finish(l):
            for s_ in range(2):
                for w, (sco, gname) in enumerate(((8, "n1g"), (32, "n2g"))):
                    P.op("dve", lambda e, s_=s_, w=w, sco=sco, gname=gname: e.scalar_tensor_tensor(
                        out=avec[l][:, s_, w, :], in0=mod[l][:, sco * 2 + s_:(sco + 8) * 2 + s_:2], scalar=1.0,
                        in1=cpc((gname, l), 0, 8), op0=ALU.add, op1=ALU.mult),
                        reads=[("mod", l), "cp"], writes=[("avec", l)])

        hid32 = hid[:].rearrange("p a b -> p (a b)").bitcast(F32)
        HIDK = [("hid", k) for k in range(16)]

        def mod_startup(l):
            mb, mbk = banks[4 + l], ("bank", 4 + l)
            for piece in range(12):
                if piece % 2 == 0:
                    stg, skey, ssem = xs[1][:].rearrange("p a b -> p (a b)"), [("xs", 1)], "xs1"
                else:
                    stg, skey, ssem = hid32, HIDK, "hidst"
                P.dma("pool", lambda e, piece=piece, stg=stg: e.dma_start(out=stg, in_=wmod[l][piece]), writes=skey, sem=ssem)
                stv = stg.rearrange("p (k c) -> p k c", k=8)
                for jj in range(4):
                    j = piece * 4 + jj
                    for k in range(8):
                        P.op("pe", lambda e, stv=stv, jj=jj, j=j, k=k: e.matmul(
                            mb[:, j * 2:j * 2 + 2], lhsT=stv[:, k, jj * 128:(jj + 1) * 128], rhs=silc[:, k * 2:k * 2 + 2],
                            start=(k == 0), stop=(k == 7)),
                            reads=skey + ["silc"], writes=[mbk], signal=(k == 7))
                if piece == 11:
                    cast_layer(0, 4, 6, gate=skey)
                    for g2 in range(6, NG // 2):
                        bgfast.append(lambda g2=g2: cast_layer(0, g2, g2 + 1, gate=[last_bank["key"]] if last_bank["key"] else ()))
            for s_ in range(2):
                P.op("dve", lambda e, s_=s_: e.tensor_tensor(
                    out=mod[l][:, s_:96:2], in0=mb[:, s_:96:2], in1=cpc(("bmod", l), 0, 48), op=ALU.add),
                    reads=["cp"], writes=[mbk, ("mod", l)])
            mod_finish(l)

        def mod_background(l):
            items = []
            for j in range(48):
                def dma_j(j=j):
                    P.dma("sp", lambda e: e.dma_start(out=wst[:], in_=wmod[l][j // 4].rearrange("p (k c) -> p k c", k=8)[:, :, (j % 4) * 128:(j % 4 + 1) * 128]), writes=["wst"], sem="wst")

                def mm_j(j=j):
                    bk, bkey = bank("st")
                    for k in range(8):
                        P.op("pe", lambda e, k=k: e.matmul(bk[:, 0:2], lhsT=wst[:, k, :], rhs=silc[:, k * 2:k * 2 + 2], start=(k == 0), stop=(k == 7)),
                             reads=["wst", "silc"], writes=[bkey], signal=(k == 7))
                    P.op("dve", lambda e: e.tensor_scalar(out=mod[l][:, 2 * j:2 * j + 2], in0=bk[:, 0:2], scalar1=cpc(("bmod", l), j), scalar2=None, op0=ALU.add),
                         reads=["cp"], writes=[bkey, ("mod", l)])
                items += [dma_j, mm_j]
            items.append(lambda: mod_finish(l))
            return items

        bgfast = []
        P.dma("sp", lambda e: e.dma_start(out=xs[2][:, :, 0:CTX], in_=ctxT_v), writes=[("xs", 2)], sem="xs2")
        P.dma("sp", lambda e: e.dma_start(out=xs[0][:].rearrange("p a b -> p (a b)"), in_=xT_v[0]), writes=[("xs", 0)], sem="xs0")
        mod_startup(0)
        bgq = mod_background(1)

        if debug:
            dbg0 = nc.dram_tensor("dbg0", [128, 192], F32, kind="ExternalOutput").ap()
            for l_ in range(NL):
                P.dma("sp", lambda e, l_=l_: e.dma_start(out=dbg0[:, l_ * 96:(l_ + 1) * 96], in_=mod[l_][:]), reads=[("mod", l_)], writes=[("dbg0", l_)], sem="dbg0")

        def layer_setup(l):
            ls = l % NSET
            P.op("act", lambda e: e.activation(out=esk[ls][:], in_=cpc(("sink", l), 0, 8), func=AF.Exp), reads=["cp"], writes=[("esk", ls)])
            for g in range(2):
                for j in range(4):
                    P.op("dve", lambda e, g=g, j=j: e.tensor_copy(out=esrow[0:1, g, j, :], in_=esk[ls][0:1, 4 * g + j:4 * g + j + 1].broadcast_to([1, 128])),
                         reads=[("esk", ls)], writes=["esrow"])
            P.op("dve", lambda e: e.tensor_scalar(out=Rq[ls][:], in0=kb[:, 2, :], scalar1=cpc(("qg", l)), scalar2=None, op0=ALU.mult),
                 reads=["kb", "cp"], writes=[("Rq", ls)])
            P.op("dve", lambda e: e.tensor_scalar(out=Rk[ls][:], in0=kb[:, 2, :], scalar1=cpc(("kg", l)), scalar2=None, op0=ALU.mult),
                 reads=["kb", "cp"], writes=[("Rk", ls)])

        def modc(l, j, s):
            return mod[l][:, j * 2 + s:j * 2 + s + 1]

        def tile_ctx(l, kind, i):
            if kind == "lat":
                slot = i % 3
                return T, 0, i % 2, xs[slot][:], ("xs", slot), slot
            slot = 2 if l == 0 else 1
            return CTX, 1, 1, xs[slot][:, :, 0:CTX], ("xs", slot), slot

        def live_range(l, kind, i):
            if kind != "lat":
                return 0, CTX
            trim = 128 if l == 0 else 256
            if i == 0:
                return trim, T - trim
            if i == NT - 1:
                return 0, T - trim
            return 0, T

        def rms_norm_to(hbuf, hkey, xap, xkey, n, l, s, which, mark="force"):
            sq = ymix
            P.op("act", lambda e: e.activation(out=sq[:, :, :n], in_=xap, func=AF.Square), reads=[xkey], writes=["ymix"])
            yield mark
            bk, bkey = bank("st")
            mm_group(bk[:, :n], bkey, [(ones[:], sq[:, c, :n]) for c in range(8)], reads=["ymix", "ones"])
            rs, rskey = tmp32()
            P.op("act", lambda e: e.activation(out=rs[:, :n], in_=bk[:, :n], func=AF.Ln, bias=epsc[:], scale=1.0 / D),
                 reads=["epsc"], writes=[bkey, rskey])
            P.op("act", lambda e: e.activation(out=rs[:, :n], in_=rs[:, :n], func=AF.Exp, scale=-0.5), reads=[rskey], writes=[rskey])
            sho = 0 if which == 0 else 24
            for c in range(8):
                tm, tmkey = tmp32()
                P.op("dve", lambda e, c=c, tm=tm: e.scalar_tensor_tensor(
                    out=tm[:, :n], in0=xap[:, c, :], scalar=avec[l][:, s, which, c:c + 1], in1=rs[:, :n], op0=ALU.mult, op1=ALU.mult),
                    reads=[xkey, rskey, ("avec", l)], writes=[tmkey])
                P.op("act", lambda e, c=c, tm=tm: e.activation(out=hbuf[:, c, :n], in_=tm[:, :n], func=AF.Identity, bias=modc(l, sho + c, s)),
                     reads=[tmkey, ("mod", l)], writes=[(hkey, c)])
            yield mark
            yield mark

        def qk_chain(bk, bkey, n, l, is_q, out_ap, out_key, ct, st_, tabkeys):
            qb, qbk = tmpb()
            P.op("act", lambda e: e.activation(out=qb[:, :n], in_=bk[:, :n], func=AF.Identity), writes=[bkey, qbk])
            sqq, sqk = tmpb()
            P.op("pool", lambda e: e.tensor_tensor(out=sqq[:, :n], in0=qb[:, :n], in1=qb[:, :n], op=ALU.mult), reads=[qbk], writes=[sqk])

            def part2():
                b1, b1k = bank("st")
                mm_group(b1[:, :n], b1k, [(BDm, sqq[:, :n])], reads=[sqk, "kb"])
                if ct is not None:
                    b2, b2k = bank("st")
                    R = Rq[0] if is_q else Rk[0]
                    mm_group(b2[:, :n], b2k, [(R[:], qb[:, :n])], reads=[qbk, ("Rq", 0), ("Rk", 0)])
                rq, rqk = tmp32()
                P.op("act", lambda e: e.activation(out=rq[:, :n], in_=b1[:, :n], func=AF.Ln, bias=epsc[:], scale=1.0 / 64),
                     reads=["epsc"], writes=[b1k, rqk])
                P.op("act", lambda e: e.activation(out=rq[:, :n], in_=rq[:, :n], func=AF.Exp, scale=-0.5), reads=[rqk], writes=[rqk])
                gcol = cpc(("qg", l)) if is_q else cpc(("kg", l))
                if ct is None:
                    for (o_ap, p0, p1) in out_ap:
                        P.op("dve", lambda e, o_ap=o_ap, p0=p0, p1=p1: e.scalar_tensor_tensor(
                            out=o_ap, in0=qb[p0:p1, :n], scalar=gcol[p0:p1], in1=rq[p0:p1, :n], op0=ALU.mult, op1=ALU.mult),
                            reads=[qbk, "cp", rqk], writes=[out_key])
                    return
                t1, t1k = tmp32()
                P.op("dve", lambda e: e.scalar_tensor_tensor(out=t1[:, :n], in0=qb[:, :n], scalar=gcol, in1=ct, op0=ALU.mult, op1=ALU.mult),
                     reads=[qbk, "cp"] + tabkeys, writes=[t1k])
                t2, t2k = tmp32()
                P.op("dve", lambda e: e.tensor_tensor(out=t2[:, :n], in0=b2[:, :n], in1=st_, op=ALU.mult), reads=tabkeys, writes=[b2k, t2k])
                P.op("dve", lambda e: e.tensor_tensor(out=t1[:, :n], in0=t1[:, :n], in1=t2[:, :n], op=ALU.add), reads=[t1k, t2k], writes=[t1k])
                for (o_ap, p0, p1) in out_ap:
                    P.op("dve", lambda e, o_ap=o_ap, p0=p0, p1=p1: e.tensor_tensor(out=o_ap, in0=t1[p0:p1, :n], in1=rq[p0:p1, :n], op=ALU.mult),
                         reads=[t1k, rqk], writes=[out_key])
            return part2

        def gen_A(l, kind, i):
            lat = kind == "lat"
            n, s, zb, xap, xkey, slot = tile_ctx(l, kind, i)
            if lat:
                src = xT_v if l == 0 else x1T_v
                P.dma("sp", lambda e: e.dma_start(out=ctab[:], in_=ropeC[:, i * T:(i + 1) * T]), writes=["ctab"], sem="tabc")
                P.dma("sp", lambda e: e.dma_start(out=stab[:], in_=ropeS[:, i * T:(i + 1) * T]), writes=["stab"], sem="tabs")
                if not (l == 0 and i == 0):
                    P.dma("sp" if (l == 0 and i == 1) else "pool",
                          lambda e: e.dma_start(out=xs[slot][:].rearrange("p a b -> p (a b)"), in_=src[i]),
                          reads=[("x1T", i)] if l > 0 else [], writes=[xkey], sem=f"xs{slot}")
                ct, st_, tabkeys = ctab[:, :n], stab[:, :n], ["ctab", "stab"]
            else:
                src = ctxT_v if l == 0 else xc1T_v
                if l > 0:
                    P.dma("sp", lambda e: e.dma_start(out=xs[slot][:, :, 0:CTX], in_=src),
                          reads=["xc1T"], writes=[xkey], sem=f"xs{slot}")
                ct, st_, tabkeys = None, None, []
            yield
            yield from rms_norm_to(hT, "hT", xap, xkey, n, l, s, 0)
            zak, zck, bgk, qtk = ("za", 0, zb), ("zc", 0, zb), ("bg", 0, zb), ("QT", 0, zb)
            zav, zcv = za[0][:, zb], zc[0][:, zb]
            if lat and i >= 1:
                pb = (i - 1) % 2
                for zt, nm in ((za, "za"), (zc, "zc")):
                    P.op("pool", lambda e, zt=zt, pb=pb: e.tensor_copy(out=zt[0][:, zb, :, 0:H], in_=zt[0][:, pb, :, T:T + H]),
                         reads=[(nm, 0, pb)], writes=[(nm, 0, zb)])
            else:
                for zt, nm in ((za, "za"), (zc, "zc")):
                    P.op("pool", lambda e, zt=zt: e.memset(zt[0][:, zb, :, 0:H], 0.0), writes=[(nm, 0, zb)])
            if (not lat) or i == NT - 1:
                for zt, nm in ((za, "za"), (zc, "zc")):
                    P.op("pool", lambda e, zt=zt: e.memset(zt[0][:, zb, :, H + n:H + n + H], 0.0), writes=[(nm, 0, zb)])
            xin = [None, None]
            sig = [None, None]
            deferred = []
            for gi in range(8):
                wt, wkey = load_granule(l * NG + gi)
                for jj in range(2):
                    j = gi * 2 + jj
                    bk, bkey = bank("mm")
                    mm_group(bk[:, :n], bkey, [(wt[:, (jj * 8 + k) * 128:(jj * 8 + k + 1) * 128], hT[:, k, :n]) for k in range(8)],
                             reads=[wkey], kreads=[[("hT", k)] for k in range(8)])
                    while len(deferred) > 1:
                        deferred.pop(0)()
                    if j in (0, 1):
                        tm, tk = tmp32()
                        xin[j] = (tm, tk)
                        P.op("act", lambda e, tm=tm, bk=bk: e.activation(out=tm[:, :n], in_=bk[:, :n], func=AF.Identity), writes=[bkey, tk])
                    elif j in (2, 3):
                        tm, tk = xin[j - 2]
                        P.op("dve", lambda e, tm=tm, bk=bk, j=j: e.tensor_tensor(out=zav[:, j - 2, H:H + n], in0=bk[:, :n], in1=tm[:, :n], op=ALU.mult),
                             reads=[tk], writes=[bkey, zak])
                    elif j in (4, 5):
                        P.op("act", lambda e, bk=bk, j=j: e.activation(out=bgate[0][:, zb, j - 4, :n], in_=bk[:, :n], func=AF.Identity), writes=[bkey, bgk])
                    elif 6 <= j <= 9:
                        deferred.append(qk_chain(bk, bkey, n, l, True, [(QT[0][zb][:, j - 6, :n], 0, 128)], qtk, ct, st_, tabkeys))
                    elif j == 10:
                        if lat:
                            c0 = (i % 3) * T
                            o_ap, o_key = [(kring[g_][64 * g_:64 * g_ + 64, c0:c0 + T], 64 * g_, 64 * g_ + 64) for g_ in range(2)], ("kring", 0)
                        else:
                            o_ap, o_key = [(ctxK[g_][64 * g_:64 * g_ + 64, :], 64 * g_, 64 * g_ + 64) for g_ in range(2)], ("ctxK", 0)
                        deferred.append(qk_chain(bk, bkey, n, l, False, o_ap, o_key, ct, st_, tabkeys))
                    elif j == 11:
                        vt, vtk = tmpb()
                        P.op("act", lambda e, vt=vt, bk=bk: e.activation(out=vt[:, :n], in_=bk[:, :n], func=AF.Identity), writes=[bkey, vtk])

                        def vpart(vt=vt, vtk=vtk):
                            for bl in range(n // 128):
                                tb_, tbk = bank("st")
                                tbv = tb_[:].bitcast(BF16)[:, 0:128]
                                P.op("pe", lambda e, vt=vt, bl=bl, tbv=tbv: e.transpose(tbv, vt[:, bl * 128:(bl + 1) * 128], ident),
                                     reads=[vtk, "kb"], writes=[tbk])
                                if lat:
                                    gb = i * 4 + bl
                                    dst = vring[0][:, gb % 12]
                                    dkey = ("vring", 0)
                                    vcol = cpc("valid", gb)
                                else:
                                    dst = ctxV[0][:, bl]
                                    dkey = ("ctxV", 0)
                                    vcol = cone[:, 0:1]
                                P.op("dve", lambda e, dst=dst, tbv=tbv, vcol=vcol: e.tensor_scalar(
                                    out=dst[:, :, 0:64], in0=tbv.rearrange("p (g d) -> p g d", g=2), scalar1=vcol, scalar2=None, op0=ALU.mult),
                                    reads=["cp", "cone"], writes=[tbk, dkey])
                                P.op("dve", lambda e, dst=dst, vcol=vcol: e.tensor_scalar(
                                    out=dst[:, :, 64:128], in0=ones[:].rearrange("p (g d) -> p g d", g=2), scalar1=vcol, scalar2=None, op0=ALU.mult),
                                    reads=["cp", "cone", "ones"], writes=[dkey])
                        deferred.append(vpart)
                    elif j in (12, 13):
                        tm, tk = tmp32()
                        sig[j - 12] = (tm, tk)
                        P.op("act", lambda e, tm=tm, bk=bk: e.activation(out=tm[:, :n], in_=bk[:, :n], func=AF.Sigmoid), writes=[bkey, tk])
                    else:
                        tm, tk = sig[j - 14]
                        P.op("dve", lambda e, tm=tm, bk=bk, j=j: e.tensor_tensor(out=zcv[:, j - 14, H:H + n], in0=bk[:, :n], in1=tm[:, :n], op=ALU.mult),
                             reads=[tk], writes=[bkey, zck])
                    yield
            for fn in deferred:
                fn()
            if lat and i in (0, NT - 1):
                for zt, nm in ((za, "za"), (zc, "zc")):
                    for b in ((0, 1) if i == 0 else (2, 3)):
                        P.op("dve", lambda e, zt=zt, b=b: e.tensor_scalar(
                            out=zt[0][:, zb, :, H + b * 128:H + (b + 1) * 128], in0=zt[0][:, zb, :, H + b * 128:H + (b + 1) * 128],
                            scalar1=cpc("valid", i * 4 + b), scalar2=None, op0=ALU.mult),
                            reads=["cp", (nm, 0, zb)], writes=[(nm, 0, zb)])
            if lat and i >= 1:
                pb = (i - 1) % 2
                for zt, nm in ((za, "za"), (zc, "zc")):
                    P.op("pool", lambda e, zt=zt, pb=pb: e.tensor_copy(out=zt[0][:, pb, :, H + T:H + T + H], in_=zt[0][:, zb, :, H:2 * H]),
                         reads=[(nm, 0, zb)], writes=[(nm, 0, pb)])
            yield

        def gen_attention(l, kind, i, n, zb, taps):
            lat = kind == "lat"
            lt0, ln_ = live_range(l, kind, i)
            for bl in range(lt0 // 128, (lt0 + ln_) // 128):
                gb = i * 4 + bl
                chunks = []
                if lat:
                    def kcol(b):
                        return slice((b % 12) * 128, (b % 12 + 1) * 128)
                    if gb >= 1:
                        chunks.append(("prev", [kring[g_][:, kcol(gb - 1)] for g_ in range(2)], vring[0][:, (gb - 1) % 12], ("kring", 0), ("vring", 0)))
                    chunks.append(("own", [kring[g_][:, kcol(gb)] for g_ in range(2)], vring[0][:, gb % 12], ("kring", 0), ("vring", 0)))
                    if gb + 1 < NB:
                        chunks.append(("next", [kring[g_][:, kcol(gb + 1)] for g_ in range(2)], vring[0][:, (gb + 1) % 12], ("kring", 0), ("vring", 0)))
                for cb in range(2):
                    chunks.append(("ctx", [ctxK[g_][:, cb * 128:(cb + 1) * 128] for g_ in range(2)], ctxV[0][:, cb], ("ctxK", 0), ("ctxV", 0)))
                pend = []

                def flush(pend):
                    for (g, v_ap2, pt2, ptk2, vkey2, ci2) in pend:
                        P.op("pe", lambda e, g=g, v_ap2=v_ap2, pt2=pt2, ci2=ci2: e.matmul(
                            banks[4 + g][:], lhsT=v_ap2[:, g, :], rhs=pt2[:], start=(ci2 == 0), stop=False),
                            reads=[vkey2, ptk2], writes=[("bank", 4 + g)], signal=False)
                for ci, (ckind, k_ap, v_ap, kkey, vkey) in enumerate(chunks):
                    cur = []
                    masked = ckind in ("prev", "next")
                    for g in range(2):
                        bk, bkey = bank("mm")
                        q_ap = QT[0][zb][:, :, bl * 128:(bl + 1) * 128]
                        P.op("pe", lambda e, bk=bk, k_ap=k_ap, q_ap=q_ap, g=g, masked=masked: e.matmul(
                            bk[:].rearrange("p (a b) -> p a b", a=4), lhsT=k_ap[g], rhs=q_ap, start=True, stop=not masked),
                            reads=[kkey, ("QT", 0, zb)], writes=[bkey], signal=not masked)
                        if masked:
                            mk = kb[:, 3 if ckind == "prev" else 4, :].unsqueeze(1).broadcast_to([128, 4, 128])
                            P.op("pe", lambda e, bk=bk, mk=mk: e.matmul(bk[:].rearrange("p (a b) -> p a b", a=4), lhsT=ident, rhs=mk, start=False, stop=True),
                                 reads=["kb"], writes=[bkey])
                        pi = rr["pT"] % 4
                        rr["pT"] += 1
                        pt, ptk = pT[pi], ("pT", pi)
                        P.op("act", lambda e, pt=pt, bk=bk: e.activation(out=pt[:], in_=bk[:], func=AF.Exp, scale=0.125), writes=[bkey, ptk])
                        cur.append((g, v_ap, pt, ptk, vkey, ci))
                    for fn in taps[:3]:
                        fn()
                    del taps[:3]
                    flush(pend)
                    pend = cur
                    yield
                flush(pend)
                for g in range(2):
                    P.op("pe", lambda e, g=g: e.matmul(banks[4 + g][:], lhsT=sinkL[0:1, :], rhs=esrow[0:1, g].rearrange("p a b -> p (a b)"), start=False, stop=True),
                         reads=["sinkL", "esrow"], writes=[("bank", 4 + g)])
                for g in range(2):
                    pv, pvk = banks[4 + g], ("bank", 4 + g)
                    rec, reck = tmp32()
                    P.op("act", lambda e, rec=rec, pv=pv: e.activation(out=rec[64:128, :], in_=pv[64:128, :], func=AF.Ln), writes=[pvk, reck])
                    P.op("act", lambda e, rec=rec: e.activation(out=rec[64:128, :], in_=rec[64:128, :], func=AF.Exp, scale=-1.0), reads=[reck], writes=[reck])
                    for par in range(2):
                        P.op("dve", lambda e, rec=rec, pv=pv, g=g, par=par, bl=bl: e.tensor_tensor(
                            out=ymix[64 * par:64 * par + 64, 2 + 2 * g:4 + 2 * g, bl * 128:(bl + 1) * 128],
                            in0=pv[0:64, :].rearrange("p (a b c) -> p a b c", a=2, b=2)[:, :, par, :],
                            in1=rec[64:128, :].rearrange("p (a b c) -> p a b c", a=2, b=2)[:, :, par, :], op=ALU.mult),
                            reads=[reck], writes=[pvk, "ymix"])
                yield

        def gen_mix(l, kind, i):
            lat = kind == "lat"
            n, s, zb, xap, xkey, slot = tile_ctx(l, kind, i)
            zak, zck, bgk = ("za", 0, zb), ("zc", 0, zb), ("bg", 0, zb)
            zav, zcv = za[0][:, zb], zc[0][:, zb]
            for c in range(2):
                acc, acck = tmp32()
                P.op("dve", lambda e, acc=acc, c=c: e.tensor_scalar(
                    out=acc[:, :n], in0=zav[:, c, H - 1:H - 1 + n], scalar1=cpc(("caw", l), 0 * 2 + c), scalar2=None, op0=ALU.mult),
                    reads=[zak, "cp"], writes=[acck])
                for tap in (1, 2):
                    P.op("dve", lambda e, acc=acc, c=c, tap=tap: e.scalar_tensor_tensor(
                        out=acc[:, :n], in0=zav[:, c, H - 1 + tap:H - 1 + tap + n], scalar=cpc(("caw", l), tap * 2 + c), in1=acc[:, :n],
                        op0=ALU.mult, op1=ALU.add), reads=[zak, "cp", acck], writes=[acck])
                P.op("dve", lambda e, acc=acc, c=c: e.tensor_tensor(out=ymix[:, c, :n], in0=acc[:, :n], in1=bgate[0][:, zb, c, :n], op=ALU.mult),
                     reads=[acck, bgk], writes=["ymix"])
                yield
            ycs = []
            taps = []
            for c in range(2):
                acc, acck = yacc[c], ("yacc", c)
                ycs.append((acc, acck))
                P.op("dve", lambda e, acc=acc, c=c: e.tensor_scalar(
                    out=acc[:, :n], in0=zcv[:, c, H - 15:H - 15 + n], scalar1=cpc(("ccw", l), c), scalar2=cpc(("ccb", l), c),
                    op0=ALU.mult, op1=ALU.add), reads=[zck, "cp"], writes=[acck])
            for tap in range(1, 31):
                for c in range(2):
                    acc, acck = ycs[c]
                    taps.append(lambda acc=acc, acck=acck, c=c, tap=tap: P.op("dve", lambda e: e.scalar_tensor_tensor(
                        out=acc[:, :n], in0=zcv[:, c, H - 15 + tap:H - 15 + tap + n], scalar=cpc(("ccw", l), tap * 2 + c), in1=acc[:, :n],
                        op0=ALU.mult, op1=ALU.add), reads=[zck, "cp", acck], writes=[acck]))
            yield
            yield from gen_attention(l, kind, i, n, zb, taps)
            for fn in taps:
                fn()
            b1, b1k = bank("st")
            b2, b2k = bank("st")
            ybs = []
            for c in range(2):
                acc, acck = ycs[c]
                yb, ybk = tmpb()
                ysq, ysqk = tmpb()
                ybs.append((yb, ybk, ysq, ysqk))
                P.op("act", lambda e, yb=yb, acc=acc: e.activation(out=yb[:, :n], in_=acc[:, :n], func=AF.Identity), reads=[acck], writes=[ybk])
                P.op("pool", lambda e, ysq=ysq, acc=acc: e.tensor_tensor(out=ysq[:, :n], in0=acc[:, :n], in1=acc[:, :n], op=ALU.mult), reads=[acck], writes=[ysqk])
            yield LNF
            mm_group(b1[:, :n], b1k, [(ones[:], ybs[c][0][:, :n]) for c in range(2)], reads=[ybs[0][1], ybs[1][1], "ones"])
            mm_group(b2[:, :n], b2k, [(ones[:], ybs[c][2][:, :n]) for c in range(2)], reads=[ybs[0][3], ybs[1][3], "ones"])
            mean, meank = tmp32()
            P.op("act", lambda e: e.activation(out=mean[:, :n], in_=b1[:, :n], func=AF.Identity, scale=1.0 / 256), writes=[b1k, meank])
            msq, msqk = tmp32()
            P.op("pool", lambda e: e.tensor_tensor(out=msq[:, :n], in0=mean[:, :n], in1=mean[:, :n], op=ALU.mult), reads=[meank], writes=[msqk])
            P.op("dve", lambda e: e.scalar_tensor_tensor(out=msq[:, :n], in0=b2[:, :n], scalar=1.0 / 256, in1=msq[:, :n], op0=ALU.mult, op1=ALU.subtract),
                 reads=[msqk], writes=[b2k, msqk])
            P.op("act", lambda e: e.activation(out=msq[:, :n], in_=msq[:, :n], func=AF.Ln, bias=epsc[:], scale=1.0), reads=[msqk, "epsc"], writes=[msqk])
            P.op("act", lambda e: e.activation(out=msq[:, :n], in_=msq[:, :n], func=AF.Exp, scale=-0.5), reads=[msqk], writes=[msqk])
            for c in range(2):
                acc, acck = ycs[c]
                P.op("pool", lambda e, acc=acc: e.tensor_tensor(out=acc[:, :n], in0=acc[:, :n], in1=mean[:, :n], op=ALU.subtract), reads=[acck, meank], writes=[acck])
                P.op("pool", lambda e, acc=acc: e.tensor_tensor(out=acc[:, :n], in0=acc[:, :n], in1=msq[:, :n], op=ALU.mult), reads=[acck, msqk], writes=[acck])
                P.op("act", lambda e, acc=acc, c=c: e.activation(out=ymix[:, 6 + c, :n], in_=acc[:, :n], func=AF.Silu,
                                                                  bias=cpc(("lnb", l), c), scale=cpc(("lng", l), c)),
                     reads=[acck, "cp"], writes=["ymix"])
            yield LNF
            yield LNF
            lt0, ln_ = live_range(l, kind, i)
            for gi in range(4):
                wt, wkey = load_granule(l * NG + 8 + gi)
                for jj in range(2):
                    m = gi * 2 + jj
                    bk, bkey = bank("mm")
                    mm_group(bk[:, :ln_], bkey, [(wt[:, (jj * 8 + k) * 128:(jj * 8 + k + 1) * 128], ymix[:, k, lt0:lt0 + ln_]) for k in range(8)],
                             reads=[wkey, "ymix"])
                    P.op("dve", lambda e, bk=bk, m=m: e.scalar_tensor_tensor(
                        out=xap[:, m, lt0:lt0 + ln_], in0=bk[:, :ln_], scalar=modc(l, 16 + m, s), in1=xap[:, m, lt0:lt0 + ln_], op0=ALU.mult, op1=ALU.add),
                        reads=[("mod", l), xkey], writes=[bkey, xkey])
                    yield
            yield from rms_norm_to(hT2, "hT2", xap, xkey, n, l, s, 1, mark="tail")

        def mlp_pieces(l, kind, i, last):
            lat = kind == "lat"
            n, s, zb, xap, xkey, slot = tile_ctx(l, kind, i)
            pieces = []

            lt0, ln_ = live_range(l, kind, i)

            def p_mlp1(half, gi, part):
                wt, wkey = load_granule(l * NG + 12 + half * 16 + gi * 2 + part)
                for jj in range(2):
                    hc = gi * 4 + part * 2 + jj
                    bk, bkey = bank("mm")
                    mm_group(bk[:, :ln_], bkey, [(wt[:, (jj * 8 + k) * 128:(jj * 8 + k + 1) * 128], hT2[:, k, lt0:lt0 + ln_]) for k in range(8)],
                             reads=[wkey], kreads=[[("hT2", k)] for k in range(8)])
                    ti = rr["tmlp"] % 2
                    rr["tmlp"] += 1
                    tm, tk = tmlp[ti], ("tmlp", ti)
                    P.op("act", lambda e, tm=tm, bk=bk: e.activation(out=tm[:, :ln_], in_=bk[:, :ln_], func=AF.Relu), writes=[bkey, tk])
                    P.op("pool", lambda e, tm=tm, hc=hc: e.tensor_tensor(out=hid[:, hc, :ln_], in0=tm[:, :ln_], in1=tm[:, :ln_], op=ALU.mult),
                         reads=[tk], writes=[("hid", hc)])

            def p_mlp2(half, gi, part):
                m = gi * 2 + part
                wt, wkey = load_granule(l * NG + 12 + half * 16 + 8 + m)
                bk, bkey = bank("mm")
                mm_group(bk[:, :ln_], bkey, [(wt[:, k * 128:(k + 1) * 128], hid[:, k, :ln_]) for k in range(16)],
                         reads=[wkey], kreads=[[("hid", k)] for k in range(16)])
                P.op("dve", lambda e, bk=bk, m=m: e.scalar_tensor_tensor(
                    out=xap[:, m, lt0:lt0 + ln_], in0=bk[:, :ln_], scalar=modc(l, 40 + m, s), in1=xap[:, m, lt0:lt0 + ln_], op0=ALU.mult, op1=ALU.add),
                    reads=[("mod", l), xkey], writes=[bkey, xkey])
                if half == 1 and gi == 3 and part == 1:
                    if lat:
                        dst = outT_v if last else x1T_v
                        P.dma("pool", lambda e: e.dma_start(out=dst[i], in_=xs[slot][:].rearrange("p a b -> p (a b)")),
                              reads=[xkey], writes=[("outT", i) if last else ("x1T", i)], sem=f"st{slot}")
                    else:
                        P.dma("pool", lambda e: e.dma_start(out=xc1T_v, in_=xs[slot][:, :, 0:CTX]), reads=[xkey], writes=["xc1T"], sem=f"st{slot}")
            for half in range(2):
                for gi in range(4):
                    for part in range(2):
                        pieces.append(lambda half=half, gi=gi, part=part: p_mlp1(half, gi, part))
                for gi in range(4):
                    for part in range(2):
                        pieces.append(lambda half=half, gi=gi, part=part: p_mlp2(half, gi, part))
            return pieces

        def run_step(lat_gens, pieces, lead=4, nforce=12, ntail=6, nyield=50):
            pieces = list(pieces)
            tail = pieces[len(pieces) - ntail:] if len(pieces) > ntail + lead else []
            pieces = pieces[:len(pieces) - len(tail)]
            lat_gens = list(lat_gens)
            if lat_gens:
                next(lat_gens[0], None)
            for _ in range(min(lead, len(pieces))):
                pieces.pop(0)()
            nfree = max(1, len(pieces) - nforce)
            every = max(2, nyield // nfree)
            cnt = 0
            for g in lat_gens:
                for y in g:
                    cnt += 1
                    if bgfast:
                        bgfast.pop(0)()
                    if bgq and cnt % 4 == 0:
                        bgq.pop(0)()
                    if y == "tail":
                        if pieces and DEBUG_SCHED:
                            print("run_step: leftover pieces at norm2:", len(pieces), "yields so far", cnt, "every", every)
                        while pieces:
                            pieces.pop(0)()
                        for _ in range(2):
                            if tail:
                                tail.pop(0)()
                    elif y == "force":
                        for _ in range(2):
                            if pieces:
                                pieces.pop(0)()
                    elif pieces and cnt % every == 0:
                        pieces.pop(0)()
            while pieces:
                pieces.pop(0)()
            while tail:
                tail.pop(0)()

        def gen_call(fn, *a):
            fn(*a)
            yield

        carry = []
        for l in range(NL):
            last = l == NL - 1
            if l == 1:
                while bgq:
                    bgq.pop(0)()
            head = [gen_call(layer_setup, l), gen_A(l, "ctx", 0)]
            if not last:
                run_step(head, carry)
                run_step([gen_mix(l, "ctx", 0)], [])
                carry = mlp_pieces(l, "ctx", 0, last)
                head = []
            gA0, gA1 = gen_A(l, "lat", 0), gen_A(l, "lat", 1)
            next(gA0)
            run_step(head + [gA0, gA1, gen_mix(l, "lat", 0)], carry, nyield=110 + 20 * len(head))
            if l + 1 < NL:
                for g2 in range(NG // 2):
                    bgq.insert(min(len(bgq), 3 * g2), lambda g2=g2: cast_layer(1, g2, g2 + 1))
            carry = mlp_pieces(l, "lat", 0, last)
            for i in range(1, NT):
                gens = []
                if i + 1 < NT:
                    gens.append(gen_A(l, "lat", i + 1))
                gens.append(gen_mix(l, "lat", i))
                run_step(gens, carry, nyield=54 if len(gens) == 2 else 20)
                carry = mlp_pieces(l, "lat", i, last)
        run_step([], carry)
        if debug:
            dbg = nc.dram_tensor("dbg", [128, 192 + 64], F32, kind="ExternalOutput").ap()
            for l_ in range(NL):
                P.dma("sp", lambda e, l_=l_: e.dma_start(out=dbg[:, l_ * 96:(l_ + 1) * 96], in_=mod[l_][:]), reads=[("mod", l_)], writes=[("dbg", l_)], sem="dbg")
                P.dma("sp", lambda e, l_=l_: e.dma_start(out=dbg[:, 192 + l_ * 32:192 + (l_ + 1) * 32], in_=avec[l_][:].rearrange("p a b c -> p (a b c)")), reads=[("avec", l_)], writes=[("dbg", l_)], sem="dbg")
            P.wait_all("sp", [("dbg", 0), ("dbg", 1)])
        P.wait_all("pool", [("outT", i) for i in range(NT)])
        P.wait_all("sp", [("outT", i) for i in range(NT)])
        P.emit()
    return nc


def _fm(v):
    return np.ascontiguousarray(np.asarray(v, np.float32).reshape(-1, 128).T)


def _w_in_colperm():
    cols = []
    cols += list(range(0, 256))
    cols += list(range(512, 768))
    cols += list(range(256, 512))
    for jq in range(4):
        cols += list(range(OFF_Q + 64 * jq, OFF_Q + 64 * jq + 64))
        cols += list(range(OFF_Q + 64 * (jq + 4), OFF_Q + 64 * (jq + 4) + 64))
    cols += list(range(1280, 1408))
    cols += list(range(1408, 1536))
    cols += list(range(1792, 2048))
    cols += list(range(1536, 1792))
    return np.array(cols)


def _granules_k8(W):
    nc_ = W.shape[1] // 128
    Wr = W.reshape(8, 128, nc_ // 2, 2, 128)
    return np.ascontiguousarray(Wr.transpose(2, 1, 3, 0, 4)).reshape(nc_ // 2, 128, GE)


def _granules_mlp2(W2, half):
    Wh = W2[half * 2048:(half + 1) * 2048].reshape(16, 128, 8, 128)
    return np.ascontiguousarray(Wh.transpose(2, 1, 0, 3)).reshape(8, 128, GE)


def _prep(inputs):
    f = lambda k: np.asarray(inputs[k], np.float32)
    x, c, ctx, c_ctx = f("x"), f("c"), f("ctx"), f("c_ctx")
    perm = _w_in_colperm()
    wp = []
    for l in range(NL):
        wp.append(_granules_k8(f("w_in")[l][:, perm]))
        wp.append(_granules_k8(f("w_out")[l]))
        w1 = _granules_k8(f("w_mlp1")[l])
        w2 = f("w_mlp2")[l]
        for half in range(2):
            wp.append(w1[half * 8:(half + 1) * 8])
            wp.append(_granules_mlp2(w2, half))
    wpack = np.ascontiguousarray(np.concatenate(wp, axis=0))
    assert wpack.shape == (NL * NG, 128, GE)
    wmod = np.ascontiguousarray(f("w_mod").reshape(NL, 8, 128, 12, 512).transpose(0, 3, 2, 1, 4)).reshape(NL, 12, 128, 8 * 512)
    kpack = np.zeros((128, 5, 128), np.float32)
    kpack[:, 0, :] = np.eye(128)
    kpack[:, 1, :] = np.kron(np.eye(2), np.ones((64, 64)))
    R = np.zeros((128, 128), np.float32)
    for m in range(128):
        d = m % 64
        dd = d % 32
        if dd < 16:
            R[m + 16, m] = -1.0
        else:
            R[m - 16, m] = 1.0
    kpack[:, 2, :] = R
    kk = np.arange(128)[:, None]
    qq = np.arange(128)[None, :]
    kpack[:, 3, :] = np.where(kk >= qq, 0.0, -30000.0)
    kpack[:, 4, :] = np.where(kk <= qq, 0.0, -30000.0)
    kpack = kpack.reshape(128, NKP)
    inv_freq = (10000.0 ** (-np.arange(0, 32, 2, dtype=np.float32) / 32)).astype(np.float32)
    d = np.arange(128) % 64
    fidx = d % 16
    use_col = d >= 32
    in_maps = []
    for r in range(8):
        b, a = r // 4, (r % 4) * OWN
        pos = np.arange(a - HALO, a - HALO + NTOK)
        ok = (pos >= 0) & (pos < SEQ)
        xt = np.zeros((D, NTOK), np.float32)
        xt[:, ok] = x[b, pos[ok]].T
        xt = np.ascontiguousarray(xt.reshape(8, 128, NT, T).transpose(2, 1, 0, 3)).reshape(NT, 128, 8 * T)
        cpk = np.zeros((128, NCP), np.float32)
        for l in range(NL):
            cpk[:, CPO[("n1g", l)]:CPO[("n1g", l)] + 8] = _fm(f("norm1_g")[l])
            cpk[:, CPO[("n2g", l)]:CPO[("n2g", l)] + 8] = _fm(f("norm2_g")[l])
            cpk[:, CPO[("bmod", l)]:CPO[("bmod", l)] + 48] = _fm(f("b_mod")[l])
            cpk[:, CPO[("caw", l)]:CPO[("caw", l)] + 6] = f("conv_a_w")[l].reshape(3, 2, 128).transpose(2, 0, 1).reshape(128, 6)
            cpk[:, CPO[("ccw", l)]:CPO[("ccw", l)] + 62] = f("conv_c_w")[l].reshape(31, 2, 128).transpose(2, 0, 1).reshape(128, 62)
            cpk[:, CPO[("ccb", l)]:CPO[("ccb", l)] + 2] = _fm(f("conv_c_b")[l])
            cpk[:, CPO[("lng", l)]:CPO[("lng", l)] + 2] = _fm(f("ln_c_g")[l])
            cpk[:, CPO[("lnb", l)]:CPO[("lnb", l)] + 2] = _fm(f("ln_c_b")[l])
            cpk[:, CPO[("qg", l)]] = f("q_norm_g")[l][d]
            cpk[:, CPO[("kg", l)]] = f("k_norm_g")[l][d]
            cpk[:, CPO[("sink", l)]:CPO[("sink", l)] + 8] = f("attn_sink")[l][None, :]
        cc = np.stack([_fm(c[b]), _fm(c_ctx)], axis=2).reshape(128, 16)
        cpk[:, CPO["c"]:CPO["c"] + 16] = cc
        cpk[:, CPO["valid"]:CPO["valid"] + NB] = ok.reshape(NB, 128).T
        p = np.clip(pos, 0, SEQ - 1)
        row, col = (p // 64).astype(np.float32), (p % 64).astype(np.float32)
        ang = np.where(use_col[:, None], col[None, :], row[None, :]).astype(np.float32) * inv_freq[fidx][:, None]
        in_maps.append({
            "xT": xt, "ctxT": np.ascontiguousarray(ctx[b].T), "wpack": wpack, "wmod": wmod, "cpack": cpk, "kpack": kpack,
            "ropeC": np.cos(ang).astype(np.float32), "ropeS": np.sin(ang).astype(np.float32),
        })
    return in_maps


def kernel(**inputs):
    in_maps = _prep(inputs)
    nc = build()
    res = run_bass_kernel_spmd(nc, in_maps, core_ids=list(range(8)))
    out = np.zeros((2, SEQ, D), np.float32)
    for r in range(8):
        b, a = r // 4, (r % 4) * OWN
        o = res.results[r]["outT"].reshape(NT, 128, 8, T).transpose(2, 1, 0, 3).reshape(D, NTOK)
        out[b, a:a + OWN] = o[:, HALO:HALO + OWN].T
    return out
```

```python
import contextlib
import numpy as np
import concourse.bass as bass
import concourse.mybir as mybir
from concourse.bass_utils import run_bass_kernel_spmd

F32 = mybir.dt.float32
BF16 = mybir.dt.bfloat16
F32R = mybir.dt.float32r
ALU = mybir.AluOpType
AF = mybir.ActivationFunctionType

D = 1024
SEQ = 16384
CTX = 256
NL = 2
T = 512
NT = 9
NTOK = NT * T
HALO = 256
OWN = 4096
NB = NTOK // 128
H = 16
ZW = H + T + H
OFF_Q = 768
EPS = 1e-6
NG = 44
NSET = 1
GE = 2048

ENGS = ("pe", "act", "dve", "pool", "sp")
LNF = "force"
SELF_SYNC = True
DEBUG_SCHED = False


class Prog:
    def __init__(self, nc, stack):
        self.nc = nc
        self.stack = stack
        self.ops = {e: [] for e in ENGS}
        self.sig = {e: 0 for e in ENGS}
        self.pending = {e: False for e in ENGS}
        self.sems = {e: stack.enter_context(nc.semaphore("s_" + e)) for e in ENGS}
        self.seen = {e: {} for e in ENGS}
        self.lastw = {}
        self.readers = {}
        self.dsems = {}
        self.semobj = {("e", e): self.sems[e] for e in ENGS}

    def dma_sem(self, name):
        if name not in self.dsems:
            s = self.stack.enter_context(self.nc.semaphore("d_" + name))
            self.dsems[name] = [s, 0]
            self.semobj[("d", name)] = s
        return self.dsems[name]

    def _deps(self, reads, writes):
        deps = {}

        def add(tok):
            if tok is None:
                return
            k, v = tok
            if deps.get(k, 0) < v:
                deps[k] = v
        for k in reads:
            add(self.lastw.get(k))
        for k in writes:
            add(self.lastw.get(k))
            for sk, v in self.readers.get(k, {}).items():
                add((sk, v))
        return deps

    def _waits(self, eng, deps):
        waits = []
        seen = self.seen[eng]
        for sk, v in deps.items():
            if sk == ("e", eng) and (eng == "pe" or not SELF_SYNC):
                continue
            if seen.get(sk, 0) >= v:
                continue
            seen[sk] = v
            waits.append((self.semobj[sk], v))
        return waits

    def _note(self, tok, reads, writes):
        sk, v = tok
        for k in reads:
            r = self.readers.setdefault(k, {})
            if r.get(sk, 0) < v:
                r[sk] = v
        for k in writes:
            self.lastw[k] = tok
            self.readers[k] = {}

    def op(self, eng, fn, reads=(), writes=(), signal=True):
        waits = self._waits(eng, self._deps(reads, writes))
        if signal:
            self.sig[eng] += 1
            tok = (("e", eng), self.sig[eng])
            self.pending[eng] = False
        else:
            tok = (("e", eng), self.sig[eng] + 1)
            self.pending[eng] = True
        self.ops[eng].append((waits, fn, (self.sems[eng], 1) if signal else None))
        self._note(tok, reads, writes)

    def dma(self, eng, fn, reads=(), writes=(), sem="misc"):
        waits = self._waits(eng, self._deps(reads, writes))
        ds = self.dma_sem(sem)
        ds[1] += 16
        tok = (("d", sem), ds[1])
        self.ops[eng].append((waits, fn, (ds[0], 16)))
        self._note(tok, reads, writes)

    def wait_all(self, eng, keys):
        waits = self._waits(eng, self._deps(keys, keys))
        self.ops[eng].append((waits, None, None))

    def emit(self):
        nc = self.nc
        for e in ENGS:
            assert not self.pending[e], f"engine {e} has trailing unsignalled ops"
        with nc.Block() as block:
            def run(e):
                def body(engine):
                    for waits, fn, inc in self.ops[e]:
                        for s, v in waits:
                            engine.wait_ge(s, v)
                        if fn is not None:
                            ins = fn(engine)
                            if inc is not None:
                                ins.then_inc(inc[0], inc[1])
                return body
            block.tensor(run("pe"))
            block.scalar(run("act"))
            block.vector(run("dve"))
            block.gpsimd(run("pool"))
            block.sync(run("sp"))


def _cp_layout():
    off = {}
    n = 0

    def add(name, w):
        nonlocal n
        off[name] = n
        n += w
    for l in range(NL):
        add(("n1g", l), 8)
        add(("n2g", l), 8)
        add(("bmod", l), 48)
        add(("caw", l), 6)
        add(("ccw", l), 62)
        add(("ccb", l), 2)
        add(("lng", l), 2)
        add(("lnb", l), 2)
        add(("qg", l), 1)
        add(("kg", l), 1)
        add(("sink", l), 8)
    add("c", 16)
    add("valid", NB)
    return off, n


CPO, NCP = _cp_layout()
NKP = 5 * 128


def build(debug=False):
    nc = bass.Bass("TRN2", target_bir_lowering=False)
    xT = nc.dram_tensor("xT", [NT, 128, 8 * T], F32, kind="ExternalInput").ap()
    ctxT = nc.dram_tensor("ctxT", [D, CTX], F32, kind="ExternalInput").ap()
    wpack = nc.dram_tensor("wpack", [NL * NG, 128, GE], F32, kind="ExternalInput").ap()
    wmod = nc.dram_tensor("wmod", [NL, 12, 128, 8 * 512], F32, kind="ExternalInput").ap()
    cpack = nc.dram_tensor("cpack", [128, NCP], F32, kind="ExternalInput").ap()
    kpack = nc.dram_tensor("kpack", [128, NKP], F32, kind="ExternalInput").ap()
    ropeC = nc.dram_tensor("ropeC", [128, NTOK], F32, kind="ExternalInput").ap()
    ropeS = nc.dram_tensor("ropeS", [128, NTOK], F32, kind="ExternalInput").ap()
    outT = nc.dram_tensor("outT", [NT, 128, 8 * T], F32, kind="ExternalOutput").ap()
    wbf = nc.dram_tensor("wbf", [NL * NG, 128, GE], BF16, kind="Internal").ap()
    x1T = nc.dram_tensor("x1T", [NT, 128, 8 * T], F32, kind="Internal").ap()
    xc1T = nc.dram_tensor("xc1T", [D, CTX], F32, kind="Internal").ap()
    xT_v, x1T_v, outT_v = xT, x1T, outT
    ctxT_v = ctxT.rearrange("(c p) t -> p c t", p=128)
    xc1T_v = xc1T.rearrange("(c p) t -> p c t", p=128)

    with contextlib.ExitStack() as st:
        P = Prog(nc, st)

        def sb(name, shape, dt):
            return st.enter_context(nc.sbuf_tensor(name, shape, dt))

        xs = [sb(f"xs{i}", [128, 8, T], F32) for i in range(3)]
        wring = [sb(f"wr{i}", [128, GE], BF16) for i in range(6)]
        hT = sb("hT", [128, 8, T], BF16)
        hT2 = sb("hT2", [128, 8, T], BF16)
        tmlp = [sb(f"tmlp{i}", [128, T], F32) for i in range(2)]
        yacc = [sb(f"yacc{i}", [128, T], F32) for i in range(2)]
        ymix = sb("ymix", [128, 8, T], BF16)
        hid = sb("hid", [128, 16, T], BF16)
        pT = [sb(f"pT{i}", [128, T], BF16) for i in range(4)]
        ctab = sb("ctab", [128, T], F32)
        stab = sb("stab", [128, T], F32)
        QT = [[sb(f"QT{l}_{b}", [128, 4, T], BF16) for b in range(2)] for l in range(NSET)]
        kring = [sb(f"kring{g}", [128, 12 * 128], BF16) for g in range(2)]
        vring = [sb(f"vring{l}", [128, 12, 2, 128], BF16) for l in range(NSET)]
        za = [sb(f"za{l}", [128, 2, 2, ZW], F32) for l in range(NSET)]
        zc = [sb(f"zc{l}", [128, 2, 2, ZW], F32) for l in range(NSET)]
        bgate = [sb(f"bg{l}", [128, 2, 2, T], BF16) for l in range(NSET)]
        ctxK = [sb(f"ctxK{g}", [128, CTX], BF16) for g in range(2)]
        ctxV = [sb(f"ctxV{l}", [128, 2, 2, 128], BF16) for l in range(NSET)]
        NT32, NTB = 9, 6
        t32 = [sb(f"t32_{i}", [128, T], F32) for i in range(NT32)]
        tb16 = [sb(f"tb_{i}", [128, T], BF16) for i in range(NTB)]
        cp = sb("cp", [128, NCP], F32)
        kb = sb("kb", [128, 5, 128], BF16)
        ones = sb("ones", [128, 128], BF16)
        epsc = sb("epsc", [128, 1], F32)
        cone = sb("cone", [128, 1], F32)
        silc = sb("silc", [128, 16], F32)
        wst = sb("wst", [128, 8, 128], F32)
        mod = [sb(f"mod{l}", [128, 96], F32) for l in range(NL)]
        avec = [sb(f"avec{l}", [128, 2, 2, 8], F32) for l in range(NL)]
        esk = [sb(f"esk{l}", [128, 8], F32) for l in range(NSET)]
        esrow = sb("esrow", [1, 2, 4, 128], BF16)
        sinkL = sb("sinkL", [1, 128], BF16)
        Rq = [sb(f"Rq{l}", [128, 128], BF16) for l in range(NSET)]
        Rk = [sb(f"Rk{l}", [128, 128], BF16) for l in range(NSET)]
        banks = [st.enter_context(nc.psum_tensor(f"bank{i}", [128, T], F32)) for i in range(8)]
        kp32 = hid[:].rearrange("p a b -> p (a b)").bitcast(F32)[:, 0:NKP]

        ident = kb[:, 0, :]
        BDm = kb[:, 1, :]

        rr = {"mm": 0, "st": 0, "t32": 0, "tb": 0, "pT": 0, "wr": 0, "tmlp": 0}
        bank_groups = {"mm": [0, 1, 2, 3], "st": [6, 7]}

        last_bank = {"key": None}

        def bank(group):
            ids = bank_groups[group]
            i = ids[rr[group] % len(ids)]
            rr[group] += 1
            if group == "mm":
                last_bank["key"] = ("bank", i)
            return banks[i], ("bank", i)

        def tmp32():
            i = rr["t32"] % NT32
            rr["t32"] += 1
            return t32[i], ("t32", i)

        def tmpb():
            i = rr["tb"] % NTB
            rr["tb"] += 1
            return tb16[i], ("tb", i)

        def cpc(name, c=0, w=1):
            o = CPO[name] + c
            return cp[:, o:o + w]

        gstate = {"n": 0}

        def load_granule(gidx):
            s = rr["wr"] % 6
            rr["wr"] += 1
            P.dma("sp", lambda e: e.dma_start(out=wring[s][:], in_=wbf[gidx]),
                  reads=[("wbf", gidx // 2)], writes=[("wr", s)], sem=f"wr{s}")
            return wring[s], ("wr", s)

        def mm_group(out_ap, bkey, pairs, reads, kreads=None):
            n = len(pairs)
            for k, (l_ap, r_ap) in enumerate(pairs):
                rd = list(reads) + (list(kreads[k]) if kreads else [])
                P.op("pe", lambda e, l_ap=l_ap, r_ap=r_ap, k=k: e.matmul(out_ap, lhsT=l_ap, rhs=r_ap, start=(k == 0), stop=(k == n - 1)),
                     reads=rd, writes=[bkey], signal=(k == n - 1))

        P.dma("sp", lambda e: e.dma_start(out=cp[:], in_=cpack), writes=["cp"], sem="c0")
        P.dma("sp", lambda e: e.dma_start(out=kp32, in_=kpack), writes=[("hid", 0), ("hid", 1), ("hid", 2)], sem="c2")
        def cast_layer(l, lo=0, hi=NG // 2, gate=()):
            for g2 in range(l * NG // 2 + lo, l * NG // 2 + hi):
                P.dma("pool", lambda e, g2=g2: e.dma_start(out=wbf[2 * g2:2 * g2 + 2], in_=wpack[2 * g2:2 * g2 + 2]),
                      reads=list(gate), writes=[("wbf", g2)], sem=f"cast{g2}")
        cast_layer(0, 0, 4)
        P.op("dve", lambda e: e.tensor_copy(out=kb[:].rearrange("p a b -> p (a b)"), in_=kp32), reads=[("hid", 0), ("hid", 1), ("hid", 2)], writes=["kb"])
        P.op("pool", lambda e: e.memset(ones[:], 1.0), writes=["ones"])
        P.op("pool", lambda e: e.memset(epsc[:], EPS), writes=["epsc"])
        P.op("pool", lambda e: e.memset(sinkL[0:1, 0:64], 0.0), writes=["sinkL"])
        P.op("pool", lambda e: e.memset(sinkL[0:1, 64:128], 1.0), writes=["sinkL"])
        P.op("pool", lambda e: e.memset(cone[:], 1.0), writes=["cone"])
        for l in range(NSET):
            for g_ in range(2):
                P.op("pool", lambda e, g_=g_: e.memset(kring[g_][:], 0.0), writes=[("kring", 0)])
                P.op("pool", lambda e, g_=g_: e.memset(ctxK[g_][:], 0.0), writes=[("ctxK", 0)])
            P.op("pool", lambda e, l=l: e.memset(vring[l % NSET][:].rearrange("p a b c -> p (a b c)"), 0.0), writes=[("vring", l % NSET)])
            for b in range(2):
                P.op("pool", lambda e, l=l, b=b: e.memset(za[l % NSET][:, b].rearrange("p a b -> p (a b)"), 0.0), writes=[("za", l % NSET, b)])
                P.op("pool", lambda e, l=l, b=b: e.memset(zc[l % NSET][:, b].rearrange("p a b -> p (a b)"), 0.0), writes=[("zc", l % NSET, b)])
        P.op("act", lambda e: e.activation(out=silc[:], in_=cpc("c", 0, 16), func=AF.Silu), reads=["cp"], writes=["silc"])
        def mod_finish(l):
            for s_ in range(2):
                for w, (sco, gname) in enumerate(((8, "n1g"), (32, "n2g"))):
                    P.op("dve", lambda e, s_=s_, w=w, sco=sco, gname=gname: e.scalar_tensor_tensor(
                        out=avec[l][:, s_, w, :], in0=mod[l][:, sco * 2 + s_:(sco + 8) * 2 + s_:2], scalar=1.0,
                        in1=cpc((gname, l), 0, 8), op0=ALU.add, op1=ALU.mult),
                        reads=[("mod", l), "cp"], writes=[("avec", l)])

        hid32 = hid[:].rearrange("p a b -> p (a b)").bitcast(F32)
        HIDK = [("hid", k) for k in range(16)]

        def mod_startup(l):
            mb, mbk = banks[4 + l], ("bank", 4 + l)
            for piece in range(12):
                if piece % 2 == 0:
                    stg, skey, ssem = xs[1][:].rearrange("p a b -> p (a b)"), [("xs", 1)], "xs1"
                else:
                    stg, skey, ssem = hid32, HIDK, "hidst"
                P.dma("pool", lambda e, piece=piece, stg=stg: e.dma_start(out=stg, in_=wmod[l][piece]), writes=skey, sem=ssem)
                stv = stg.rearrange("p (k c) -> p k c", k=8)
                for jj in range(4):
                    j = piece * 4 + jj
                    for k in range(8):
                        P.op("pe", lambda e, stv=stv, jj=jj, j=j, k=k: e.matmul(
                            mb[:, j * 2:j * 2 + 2], lhsT=stv[:, k, jj * 128:(jj + 1) * 128], rhs=silc[:, k * 2:k * 2 + 2],
                            start=(k == 0), stop=(k == 7)),
                            reads=skey + ["silc"], writes=[mbk], signal=(k == 7))
                if piece == 11:
                    cast_layer(0, 4, 6, gate=skey)
                    for g2 in range(6, NG // 2):
                        bgfast.append(lambda g2=g2: cast_layer(0, g2, g2 + 1, gate=[last_bank["key"]] if last_bank["key"] else ()))
            for s_ in range(2):
                P.op("dve", lambda e, s_=s_: e.tensor_tensor(
                    out=mod[l][:, s_:96:2], in0=mb[:, s_:96:2], in1=cpc(("bmod", l), 0, 48), op=ALU.add),
                    reads=["cp"], writes=[mbk, ("mod", l)])
            mod_finish(l)

        def mod_background(l):
            items = []
            for j in range(48):
                def dma_j(j=j):
                    P.dma("sp", lambda e: e.dma_start(out=wst[:], in_=wmod[l][j // 4].rearrange("p (k c) -> p k c", k=8)[:, :, (j % 4) * 128:(j % 4 + 1) * 128]), writes=["wst"], sem="wst")

                def mm_j(j=j):
                    bk, bkey = bank("st")
                    for k in range(8):
                        P.op("pe", lambda e, k=k: e.matmul(bk[:, 0:2], lhsT=wst[:, k, :], rhs=silc[:, k * 2:k * 2 + 2], start=(k == 0), stop=(k == 7)),
                             reads=["wst", "silc"], writes=[bkey], signal=(k == 7))
                    P.op("dve", lambda e: e.tensor_scalar(out=mod[l][:, 2 * j:2 * j + 2], in0=bk[:, 0:2], scalar1=cpc(("bmod", l), j), scalar2=None, op0=ALU.add),
                         reads=["cp"], writes=[bkey, ("mod", l)])
                items += [dma_j, mm_j]
            items.append(lambda: mod_finish(l))
            return items

        bgfast = []
        P.dma("sp", lambda e: e.dma_start(out=xs[2][:, :, 0:CTX], in_=ctxT_v), writes=[("xs", 2)], sem="xs2")
        P.dma("sp", lambda e: e.dma_start(out=xs[0][:].rearrange("p a b -> p (a b)"), in_=xT_v[0]), writes=[("xs", 0)], sem="xs0")
        mod_startup(0)
        bgq = mod_background(1)

        if debug:
            dbg0 = nc.dram_tensor("dbg0", [128, 192], F32, kind="ExternalOutput").ap()
            for l_ in range(NL):
                P.dma("sp", lambda e, l_=l_: e.dma_start(out=dbg0[:, l_ * 96:(l_ + 1) * 96], in_=mod[l_][:]), reads=[("mod", l_)], writes=[("dbg0", l_)], sem="dbg0")

        def layer_setup(l):
            ls = l % NSET
            P.op("act", lambda e: e.activation(out=esk[ls][:], in_=cpc(("sink", l), 0, 8), func=AF.Exp), reads=["cp"], writes=[("esk", ls)])
            for g in range(2):
                for j in range(4):
                    P.op("dve", lambda e, g=g, j=j: e.tensor_copy(out=esrow[0:1, g, j, :], in_=esk[ls][0:1, 4 * g + j:4 * g + j + 1].broadcast_to([1, 128])),
                         reads=[("esk", ls)], writes=["esrow"])
            P.op("dve", lambda e: e.tensor_scalar(out=Rq[ls][:], in0=kb[:, 2, :], scalar1=cpc(("qg", l)), scalar2=None, op0=ALU.mult),
                 reads=["kb", "cp"], writes=[("Rq", ls)])
            P.op("dve", lambda e: e.tensor_scalar(out=Rk[ls][:], in0=kb[:, 2, :], scalar1=cpc(("kg", l)), scalar2=None, op0=ALU.mult),
                 reads=["kb", "cp"], writes=[("Rk", ls)])

        def modc(l, j, s):
            return mod[l][:, j * 2 + s:j * 2 + s + 1]

        def tile_ctx(l, kind, i):
            if kind == "lat":
                slot = i % 3
                return T, 0, i % 2, xs[slot][:], ("xs", slot), slot
            slot = 2 if l == 0 else 1
            return CTX, 1, 1, xs[slot][:, :, 0:CTX], ("xs", slot), slot

        def live_range(l, kind, i):
            if kind != "lat":
                return 0, CTX
            trim = 128 if l == 0 else 256
            if i == 0:
                return trim, T - trim
            if i == NT - 1:
                return 0, T - trim
            return 0, T

        def rms_norm_to(hbuf, hkey, xap, xkey, n, l, s, which, mark="force"):
            sq = ymix
            P.op("act", lambda e: e.activation(out=sq[:, :, :n], in_=xap, func=AF.Square), reads=[xkey], writes=["ymix"])
            yield mark
            bk, bkey = bank("st")
            mm_group(bk[:, :n], bkey, [(ones[:], sq[:, c, :n]) for c in range(8)], reads=["ymix", "ones"])
            rs, rskey = tmp32()
            P.op("act", lambda e: e.activation(out=rs[:, :n], in_=bk[:, :n], func=AF.Ln, bias=epsc[:], scale=1.0 / D),
                 reads=["epsc"], writes=[bkey, rskey])
            P.op("act", lambda e: e.activation(out=rs[:, :n], in_=rs[:, :n], func=AF.Exp, scale=-0.5), reads=[rskey], writes=[rskey])
            sho = 0 if which == 0 else 24
            for c in range(8):
                tm, tmkey = tmp32()
                P.op("dve", lambda e, c=c, tm=tm: e.scalar_tensor_tensor(
                    out=tm[:, :n], in0=xap[:, c, :], scalar=avec[l][:, s, which, c:c + 1], in1=rs[:, :n], op0=ALU.mult, op1=ALU.mult),
                    reads=[xkey, rskey, ("avec", l)], writes=[tmkey])
                P.op("act", lambda e, c=c, tm=tm: e.activation(out=hbuf[:, c, :n], in_=tm[:, :n], func=AF.Identity, bias=modc(l, sho + c, s)),
                     reads=[tmkey, ("mod", l)], writes=[(hkey, c)])
            yield mark
            yield mark

        def qk_chain(bk, bkey, n, l, is_q, out_ap, out_key, ct, st_, tabkeys):
            qb, qbk = tmpb()
            P.op("act", lambda e: e.activation(out=qb[:, :n], in_=bk[:, :n], func=AF.Identity), writes=[bkey, qbk])
            sqq, sqk = tmpb()
            P.op("pool", lambda e: e.tensor_tensor(out=sqq[:, :n], in0=qb[:, :n], in1=qb[:, :n], op=ALU.mult), reads=[qbk], writes=[sqk])

            def part2():
                b1, b1k = bank("st")
                mm_group(b1[:, :n], b1k, [(BDm, sqq[:, :n])], reads=[sqk, "kb"])
                if ct is not None:
                    b2, b2k = bank("st")
                    R = Rq[0] if is_q else Rk[0]
                    mm_group(b2[:, :n], b2k, [(R[:], qb[:, :n])], reads=[qbk, ("Rq", 0), ("Rk", 0)])
                rq, rqk = tmp32()
                P.op("act", lambda e: e.activation(out=rq[:, :n], in_=b1[:, :n], func=AF.Ln, bias=epsc[:], scale=1.0 / 64),
                     reads=["epsc"], writes=[b1k, rqk])
                P.op("act", lambda e: e.activation(out=rq[:, :n], in_=rq[:, :n], func=AF.Exp, scale=-0.5), reads=[rqk], writes=[rqk])
                gcol = cpc(("qg", l)) if is_q else cpc(("kg", l))
                if ct is None:
                    for (o_ap, p0, p1) in out_ap:
                        P.op("dve", lambda e, o_ap=o_ap, p0=p0, p1=p1: e.scalar_tensor_tensor(
                            out=o_ap, in0=qb[p0:p1, :n], scalar=gcol[p0:p1], in1=rq[p0:p1, :n], op0=ALU.mult, op1=ALU.mult),
                            reads=[qbk, "cp", rqk], writes=[out_key])
                    return
                t1, t1k = tmp32()
                P.op("dve", lambda e: e.scalar_tensor_tensor(out=t1[:, :n], in0=qb[:, :n], scalar=gcol, in1=ct, op0=ALU.mult, op1=ALU.mult),
                     reads=[qbk, "cp"] + tabkeys, writes=[t1k])
                t2, t2k = tmp32()
                P.op("dve", lambda e: e.tensor_tensor(out=t2[:, :n], in0=b2[:, :n], in1=st_, op=ALU.mult), reads=tabkeys, writes=[b2k, t2k])
                P.op("dve", lambda e: e.tensor_tensor(out=t1[:, :n], in0=t1[:, :n], in1=t2[:, :n], op=ALU.add), reads=[t1k, t2k], writes=[t1k])
                for (o_ap, p0, p1) in out_ap:
                    P.op("dve", lambda e, o_ap=o_ap, p0=p0, p1=p1: e.tensor_tensor(out=o_ap, in0=t1[p0:p1, :n], in1=rq[p0:p1, :n], op=ALU.mult),
                         reads=[t1k, rqk], writes=[out_key])
            return part2

        def gen_A(l, kind, i):
            lat = kind == "lat"
            n, s, zb, xap, xkey, slot = tile_ctx(l, kind, i)
            if lat:
                src = xT_v if l == 0 else x1T_v
                P.dma("sp", lambda e: e.dma_start(out=ctab[:], in_=ropeC[:, i * T:(i + 1) * T]), writes=["ctab"], sem="tabc")
                P.dma("sp", lambda e: e.dma_start(out=stab[:], in_=ropeS[:, i * T:(i + 1) * T]), writes=["stab"], sem="tabs")
                if not (l == 0 and i == 0):
                    P.dma("sp" if (l == 0 and i == 1) else "pool",
                          lambda e: e.dma_start(out=xs[slot][:].rearrange("p a b -> p (a b)"), in_=src[i]),
                          reads=[("x1T", i)] if l > 0 else [], writes=[xkey], sem=f"xs{slot}")
                ct, st_, tabkeys = ctab[:, :n], stab[:, :n], ["ctab", "stab"]
            else:
                src = ctxT_v if l == 0 else xc1T_v
                if l > 0:
                    P.dma("sp", lambda e: e.dma_start(out=xs[slot][:, :, 0:CTX], in_=src),
                          reads=["xc1T"], writes=[xkey], sem=f"xs{slot}")
                ct, st_, tabkeys = None, None, []
            yield
            yield from rms_norm_to(hT, "hT", xap, xkey, n, l, s, 0)
            zak, zck, bgk, qtk = ("za", 0, zb), ("zc", 0, zb), ("bg", 0, zb), ("QT", 0, zb)
            zav, zcv = za[0][:, zb], zc[0][:, zb]
            if lat and i >= 1:
                pb = (i - 1) % 2
                for zt, nm in ((za, "za"), (zc, "zc")):
                    P.op("pool", lambda e, zt=zt, pb=pb: e.tensor_copy(out=zt[0][:, zb, :, 0:H], in_=zt[0][:, pb, :, T:T + H]),
                         reads=[(nm, 0, pb)], writes=[(nm, 0, zb)])
            else:
                for zt, nm in ((za, "za"), (zc, "zc")):
                    P.op("pool", lambda e, zt=zt: e.memset(zt[0][:, zb, :, 0:H], 0.0), writes=[(nm, 0, zb)])
            if (not lat) or i == NT - 1:
                for zt, nm in ((za, "za"), (zc, "zc")):
                    P.op("pool", lambda e, zt=zt: e.memset(zt[0][:, zb, :, H + n:H + n + H], 0.0), writes=[(nm, 0, zb)])
            xin = [None, None]
            sig = [None, None]
            deferred = []
            only_kv = (not lat) and l == NL - 1
            for gi in ((5,) if only_kv else range(8)):
                wt, wkey = load_granule(l * NG + gi)
                for jj in range(2):
                    j = gi * 2 + jj
                    bk, bkey = bank("mm")
                    mm_group(bk[:, :n], bkey, [(wt[:, (jj * 8 + k) * 128:(jj * 8 + k + 1) * 128], hT[:, k, :n]) for k in range(8)],
                             reads=[wkey], kreads=[[("hT", k)] for k in range(8)])
                    while len(deferred) > 1:
                        deferred.pop(0)()
                    if j in (0, 1):
                        tm, tk = tmp32()
                        xin[j] = (tm, tk)
                        P.op("act", lambda e, tm=tm, bk=bk: e.activation(out=tm[:, :n], in_=bk[:, :n], func=AF.Identity), writes=[bkey, tk])
                    elif j in (2, 3):
                        tm, tk = xin[j - 2]
                        P.op("dve", lambda e, tm=tm, bk=bk, j=j: e.tensor_tensor(out=zav[:, j - 2, H:H + n], in0=bk[:, :n], in1=tm[:, :n], op=ALU.mult),
                             reads=[tk], writes=[bkey, zak])
                    elif j in (4, 5):
                        P.op("act", lambda e, bk=bk, j=j: e.activation(out=bgate[0][:, zb, j - 4, :n], in_=bk[:, :n], func=AF.Identity), writes=[bkey, bgk])
                    elif 6 <= j <= 9:
                        deferred.append(qk_chain(bk, bkey, n, l, True, [(QT[0][zb][:, j - 6, :n], 0, 128)], qtk, ct, st_, tabkeys))
                    elif j == 10:
                        if lat:
                            c0 = (i % 3) * T
                            o_ap, o_key = [(kring[g_][64 * g_:64 * g_ + 64, c0:c0 + T], 64 * g_, 64 * g_ + 64) for g_ in range(2)], ("kring", 0)
                        else:
                            o_ap, o_key = [(ctxK[g_][64 * g_:64 * g_ + 64, :], 64 * g_, 64 * g_ + 64) for g_ in range(2)], ("ctxK", 0)
                        deferred.append(qk_chain(bk, bkey, n, l, False, o_ap, o_key, ct, st_, tabkeys))
                    elif j == 11:
                        vt, vtk = tmpb()
                        P.op("act", lambda e, vt=vt, bk=bk: e.activation(out=vt[:, :n], in_=bk[:, :n], func=AF.Identity), writes=[bkey, vtk])

                        def vpart(vt=vt, vtk=vtk):
                            for bl in range(n // 128):
                                tb_, tbk = bank("st")
                                tbv = tb_[:].bitcast(BF16)[:, 0:128]
                                P.op("pe", lambda e, vt=vt, bl=bl, tbv=tbv: e.transpose(tbv, vt[:, bl * 128:(bl + 1) * 128], ident),
                                     reads=[vtk, "kb"], writes=[tbk])
                                if lat:
                                    gb = i * 4 + bl
                                    dst = vring[0][:, gb % 12]
                                    dkey = ("vring", 0)
                                    vcol = cpc("valid", gb)
                                else:
                                    dst = ctxV[0][:, bl]
                                    dkey = ("ctxV", 0)
                                    vcol = cone[:, 0:1]
                                P.op("dve", lambda e, dst=dst, tbv=tbv, vcol=vcol: e.tensor_scalar(
                                    out=dst[:, :, 0:64], in0=tbv.rearrange("p (g d) -> p g d", g=2), scalar1=vcol, scalar2=None, op0=ALU.mult),
                                    reads=["cp", "cone"], writes=[tbk, dkey])
                                P.op("dve", lambda e, dst=dst, vcol=vcol: e.tensor_scalar(
                                    out=dst[:, :, 64:128], in0=ones[:].rearrange("p (g d) -> p g d", g=2), scalar1=vcol, scalar2=None, op0=ALU.mult),
                                    reads=["cp", "cone", "ones"], writes=[dkey])
                        deferred.append(vpart)
                    elif j in (12, 13):
                        tm, tk = tmp32()
                        sig[j - 12] = (tm, tk)
                        P.op("act", lambda e, tm=tm, bk=bk: e.activation(out=tm[:, :n], in_=bk[:, :n], func=AF.Sigmoid), writes=[bkey, tk])
                    else:
                        tm, tk = sig[j - 14]
                        P.op("dve", lambda e, tm=tm, bk=bk, j=j: e.tensor_tensor(out=zcv[:, j - 14, H:H + n], in0=bk[:, :n], in1=tm[:, :n], op=ALU.mult),
                             reads=[tk], writes=[bkey, zck])
                    yield
            for fn in deferred:
                fn()
            if lat and i in (0, NT - 1):
                for zt, nm in ((za, "za"), (zc, "zc")):
                    for b in ((0, 1) if i == 0 else (2, 3)):
                        P.op("dve", lambda e, zt=zt, b=b: e.tensor_scalar(
                            out=zt[0][:, zb, :, H + b * 128:H + (b + 1) * 128], in0=zt[0][:, zb, :, H + b * 128:H + (b + 1) * 128],
                            scalar1=cpc("valid", i * 4 + b), scalar2=None, op0=ALU.mult),
                            reads=["cp", (nm, 0, zb)], writes=[(nm, 0, zb)])
            if lat and i >= 1:
                pb = (i - 1) % 2
                for zt, nm in ((za, "za"), (zc, "zc")):
                    P.op("pool", lambda e, zt=zt, pb=pb: e.tensor_copy(out=zt[0][:, pb, :, H + T:H + T + H], in_=zt[0][:, zb, :, H:2 * H]),
                         reads=[(nm, 0, zb)], writes=[(nm, 0, pb)])
            yield

        def gen_attention(l, kind, i, n, zb, taps):
            lat = kind == "lat"
            lt0, ln_ = live_range(l, kind, i)
            for bl in range(lt0 // 128, (lt0 + ln_) // 128):
                gb = i * 4 + bl
                chunks = []
                if lat:
                    def kcol(b):
                        return slice((b % 12) * 128, (b % 12 + 1) * 128)
                    if gb >= 1:
                        chunks.append(("prev", [kring[g_][:, kcol(gb - 1)] for g_ in range(2)], vring[0][:, (gb - 1) % 12], ("kring", 0), ("vring", 0)))
                    chunks.append(("own", [kring[g_][:, kcol(gb)] for g_ in range(2)], vring[0][:, gb % 12], ("kring", 0), ("vring", 0)))
                    if gb + 1 < NB:
                        chunks.append(("next", [kring[g_][:, kcol(gb + 1)] for g_ in range(2)], vring[0][:, (gb + 1) % 12], ("kring", 0), ("vring", 0)))
                for cb in range(2):
                    chunks.append(("ctx", [ctxK[g_][:, cb * 128:(cb + 1) * 128] for g_ in range(2)], ctxV[0][:, cb], ("ctxK", 0), ("ctxV", 0)))
                pend = []

                def flush(pend):
                    for (g, v_ap2, pt2, ptk2, vkey2, ci2) in pend:
                        P.op("pe", lambda e, g=g, v_ap2=v_ap2, pt2=pt2, ci2=ci2: e.matmul(
                            banks[4 + g][:], lhsT=v_ap2[:, g, :], rhs=pt2[:], start=(ci2 == 0), stop=False),
                            reads=[vkey2, ptk2], writes=[("bank", 4 + g)], signal=False)
                for ci, (ckind, k_ap, v_ap, kkey, vkey) in enumerate(chunks):
                    cur = []
                    masked = ckind in ("prev", "next")
                    for g in range(2):
                        bk, bkey = bank("mm")
                        q_ap = QT[0][zb][:, :, bl * 128:(bl + 1) * 128]
                        P.op("pe", lambda e, bk=bk, k_ap=k_ap, q_ap=q_ap, g=g, masked=masked: e.matmul(
                            bk[:].rearrange("p (a b) -> p a b", a=4), lhsT=k_ap[g], rhs=q_ap, start=True, stop=not masked),
                            reads=[kkey, ("QT", 0, zb)], writes=[bkey], signal=not masked)
                        if masked:
                            mk = kb[:, 3 if ckind == "prev" else 4, :].unsqueeze(1).broadcast_to([128, 4, 128])
                            P.op("pe", lambda e, bk=bk, mk=mk: e.matmul(bk[:].rearrange("p (a b) -> p a b", a=4), lhsT=ident, rhs=mk, start=False, stop=True),
                                 reads=["kb"], writes=[bkey])
                        pi = rr["pT"] % 4
                        rr["pT"] += 1
                        pt, ptk = pT[pi], ("pT", pi)
                        P.op("act", lambda e, pt=pt, bk=bk: e.activation(out=pt[:], in_=bk[:], func=AF.Exp, scale=0.125), writes=[bkey, ptk])
                        cur.append((g, v_ap, pt, ptk, vkey, ci))
                    for fn in taps[:3]:
                        fn()
                    del taps[:3]
                    flush(pend)
                    pend = cur
                    yield
                flush(pend)
                for g in range(2):
                    P.op("pe", lambda e, g=g: e.matmul(banks[4 + g][:], lhsT=sinkL[0:1, :], rhs=esrow[0:1, g].rearrange("p a b -> p (a b)"), start=False, stop=True),
                         reads=["sinkL", "esrow"], writes=[("bank", 4 + g)])
                for g in range(2):
                    pv, pvk = banks[4 + g], ("bank", 4 + g)
                    rec, reck = tmp32()
                    P.op("act", lambda e, rec=rec, pv=pv: e.activation(out=rec[64:128, :], in_=pv[64:128, :], func=AF.Ln), writes=[pvk, reck])
                    P.op("act", lambda e, rec=rec: e.activation(out=rec[64:128, :], in_=rec[64:128, :], func=AF.Exp, scale=-1.0), reads=[reck], writes=[reck])
                    for par in range(2):
                        P.op("dve", lambda e, rec=rec, pv=pv, g=g, par=par, bl=bl: e.tensor_tensor(
                            out=ymix[64 * par:64 * par + 64, 2 + 2 * g:4 + 2 * g, bl * 128:(bl + 1) * 128],
                            in0=pv[0:64, :].rearrange("p (a b c) -> p a b c", a=2, b=2)[:, :, par, :],
                            in1=rec[64:128, :].rearrange("p (a b c) -> p a b c", a=2, b=2)[:, :, par, :], op=ALU.mult),
                            reads=[reck], writes=[pvk, "ymix"])
                yield

        def gen_mix(l, kind, i):
            lat = kind == "lat"
            n, s, zb, xap, xkey, slot = tile_ctx(l, kind, i)
            zak, zck, bgk = ("za", 0, zb), ("zc", 0, zb), ("bg", 0, zb)
            zav, zcv = za[0][:, zb], zc[0][:, zb]
            for c in range(2):
                acc, acck = tmp32()
                P.op("dve", lambda e, acc=acc, c=c: e.tensor_scalar(
                    out=acc[:, :n], in0=zav[:, c, H - 1:H - 1 + n], scalar1=cpc(("caw", l), 0 * 2 + c), scalar2=None, op0=ALU.mult),
                    reads=[zak, "cp"], writes=[acck])
                for tap in (1, 2):
                    P.op("dve", lambda e, acc=acc, c=c, tap=tap: e.scalar_tensor_tensor(
                        out=acc[:, :n], in0=zav[:, c, H - 1 + tap:H - 1 + tap + n], scalar=cpc(("caw", l), tap * 2 + c), in1=acc[:, :n],
                        op0=ALU.mult, op1=ALU.add), reads=[zak, "cp", acck], writes=[acck])
                P.op("dve", lambda e, acc=acc, c=c: e.tensor_tensor(out=ymix[:, c, :n], in0=acc[:, :n], in1=bgate[0][:, zb, c, :n], op=ALU.mult),
                     reads=[acck, bgk], writes=["ymix"])
                yield
            ycs = []
            taps = []
            for c in range(2):
                acc, acck = yacc[c], ("yacc", c)
                ycs.append((acc, acck))
                P.op("dve", lambda e, acc=acc, c=c: e.tensor_scalar(
                    out=acc[:, :n], in0=zcv[:, c, H - 15:H - 15 + n], scalar1=cpc(("ccw", l), c), scalar2=cpc(("ccb", l), c),
                    op0=ALU.mult, op1=ALU.add), reads=[zck, "cp"], writes=[acck])
            for tap in range(1, 31):
                for c in range(2):
                    acc, acck = ycs[c]
                    taps.append(lambda acc=acc, acck=acck, c=c, tap=tap: P.op("dve", lambda e: e.scalar_tensor_tensor(
                        out=acc[:, :n], in0=zcv[:, c, H - 15 + tap:H - 15 + tap + n], scalar=cpc(("ccw", l), tap * 2 + c), in1=acc[:, :n],
                        op0=ALU.mult, op1=ALU.add), reads=[zck, "cp", acck], writes=[acck]))
            yield
            yield from gen_attention(l, kind, i, n, zb, taps)
            for fn in taps:
                fn()
            b1, b1k = bank("st")
            b2, b2k = bank("st")
            ybs = []
            for c in range(2):
                acc, acck = ycs[c]
                yb, ybk = tmpb()
                ysq, ysqk = tmpb()
                ybs.append((yb, ybk, ysq, ysqk))
                P.op("act", lambda e, yb=yb, acc=acc: e.activation(out=yb[:, :n], in_=acc[:, :n], func=AF.Identity), reads=[acck], writes=[ybk])
                P.op("pool", lambda e, ysq=ysq, acc=acc: e.tensor_tensor(out=ysq[:, :n], in0=acc[:, :n], in1=acc[:, :n], op=ALU.mult), reads=[acck], writes=[ysqk])
            yield LNF
            mm_group(b1[:, :n], b1k, [(ones[:], ybs[c][0][:, :n]) for c in range(2)], reads=[ybs[0][1], ybs[1][1], "ones"])
            mm_group(b2[:, :n], b2k, [(ones[:], ybs[c][2][:, :n]) for c in range(2)], reads=[ybs[0][3], ybs[1][3], "ones"])
            mean, meank = tmp32()
            P.op("act", lambda e: e.activation(out=mean[:, :n], in_=b1[:, :n], func=AF.Identity, scale=1.0 / 256), writes=[b1k, meank])
            msq, msqk = tmp32()
            P.op("pool", lambda e: e.tensor_tensor(out=msq[:, :n], in0=mean[:, :n], in1=mean[:, :n], op=ALU.mult), reads=[meank], writes=[msqk])
            P.op("dve", lambda e: e.scalar_tensor_tensor(out=msq[:, :n], in0=b2[:, :n], scalar=1.0 / 256, in1=msq[:, :n], op0=ALU.mult, op1=ALU.subtract),
                 reads=[msqk], writes=[b2k, msqk])
            P.op("act", lambda e: e.activation(out=msq[:, :n], in_=msq[:, :n], func=AF.Ln, bias=epsc[:], scale=1.0), reads=[msqk, "epsc"], writes=[msqk])
            P.op("act", lambda e: e.activation(out=msq[:, :n], in_=msq[:, :n], func=AF.Exp, scale=-0.5), reads=[msqk], writes=[msqk])
            for c in range(2):
                acc, acck = ycs[c]
                P.op("pool", lambda e, acc=acc: e.tensor_tensor(out=acc[:, :n], in0=acc[:, :n], in1=mean[:, :n], op=ALU.subtract), reads=[acck, meank], writes=[acck])
                P.op("pool", lambda e, acc=acc: e.tensor_tensor(out=acc[:, :n], in0=acc[:, :n], in1=msq[:, :n], op=ALU.mult), reads=[acck, msqk], writes=[acck])
                P.op("act", lambda e, acc=acc, c=c: e.activation(out=ymix[:, 6 + c, :n], in_=acc[:, :n], func=AF.Silu,
                                                                  bias=cpc(("lnb", l), c), scale=cpc(("lng", l), c)),
                     reads=[acck, "cp"], writes=["ymix"])
            yield LNF
            yield LNF
            lt0, ln_ = live_range(l, kind, i)
            for gi in range(4):
                wt, wkey = load_granule(l * NG + 8 + gi)
                for jj in range(2):
                    m = gi * 2 + jj
                    bk, bkey = bank("mm")
                    mm_group(bk[:, :ln_], bkey, [(wt[:, (jj * 8 + k) * 128:(jj * 8 + k + 1) * 128], ymix[:, k, lt0:lt0 + ln_]) for k in range(8)],
                             reads=[wkey, "ymix"])
                    P.op("dve", lambda e, bk=bk, m=m: e.scalar_tensor_tensor(
                        out=xap[:, m, lt0:lt0 + ln_], in0=bk[:, :ln_], scalar=modc(l, 16 + m, s), in1=xap[:, m, lt0:lt0 + ln_], op0=ALU.mult, op1=ALU.add),
                        reads=[("mod", l), xkey], writes=[bkey, xkey])
                    yield
            yield from rms_norm_to(hT2, "hT2", xap, xkey, n, l, s, 1, mark="tail")

        def mlp_pieces(l, kind, i, last):
            lat = kind == "lat"
            n, s, zb, xap, xkey, slot = tile_ctx(l, kind, i)
            pieces = []

            lt0, ln_ = live_range(l, kind, i)

            def p_mlp1(half, gi, part):
                wt, wkey = load_granule(l * NG + 12 + half * 16 + gi * 2 + part)
                for jj in range(2):
                    hc = gi * 4 + part * 2 + jj
                    bk, bkey = bank("mm")
                    mm_group(bk[:, :ln_], bkey, [(wt[:, (jj * 8 + k) * 128:(jj * 8 + k + 1) * 128], hT2[:, k, lt0:lt0 + ln_]) for k in range(8)],
                             reads=[wkey], kreads=[[("hT2", k)] for k in range(8)])
                    ti = rr["tmlp"] % 2
                    rr["tmlp"] += 1
                    tm, tk = tmlp[ti], ("tmlp", ti)
                    P.op("act", lambda e, tm=tm, bk=bk: e.activation(out=tm[:, :ln_], in_=bk[:, :ln_], func=AF.Relu), writes=[bkey, tk])
                    P.op("pool", lambda e, tm=tm, hc=hc: e.tensor_tensor(out=hid[:, hc, :ln_], in0=tm[:, :ln_], in1=tm[:, :ln_], op=ALU.mult),
                         reads=[tk], writes=[("hid", hc)])

            def p_mlp2(half, gi, part):
                m = gi * 2 + part
                wt, wkey = load_granule(l * NG + 12 + half * 16 + 8 + m)
                bk, bkey = bank("mm")
                mm_group(bk[:, :ln_], bkey, [(wt[:, k * 128:(k + 1) * 128], hid[:, k, :ln_]) for k in range(16)],
                         reads=[wkey], kreads=[[("hid", k)] for k in range(16)])
                P.op("dve", lambda e, bk=bk, m=m: e.scalar_tensor_tensor(
                    out=xap[:, m, lt0:lt0 + ln_], in0=bk[:, :ln_], scalar=modc(l, 40 + m, s), in1=xap[:, m, lt0:lt0 + ln_], op0=ALU.mult, op1=ALU.add),
                    reads=[("mod", l), xkey], writes=[bkey, xkey])
                if half == 1 and gi == 3 and part == 1:
                    if lat:
                        dst = outT_v if last else x1T_v
                        P.dma("pool", lambda e: e.dma_start(out=dst[i], in_=xs[slot][:].rearrange("p a b -> p (a b)")),
                              reads=[xkey], writes=[("outT", i) if last else ("x1T", i)], sem=f"st{slot}")
                    else:
                        P.dma("pool", lambda e: e.dma_start(out=xc1T_v, in_=xs[slot][:, :, 0:CTX]), reads=[xkey], writes=["xc1T"], sem=f"st{slot}")
            for half in range(2):
                for gi in range(4):
                    for part in range(2):
                        pieces.append(lambda half=half, gi=gi, part=part: p_mlp1(half, gi, part))
                for gi in range(4):
                    for part in range(2):
                        pieces.append(lambda half=half, gi=gi, part=part: p_mlp2(half, gi, part))
            return pieces

        def run_step(lat_gens, pieces, lead=4, nforce=12, ntail=6, nyield=50):
            pieces = list(pieces)
            tail = pieces[len(pieces) - ntail:] if len(pieces) > ntail + lead else []
            pieces = pieces[:len(pieces) - len(tail)]
            lat_gens = list(lat_gens)
            if lat_gens:
                next(lat_gens[0], None)
            for _ in range(min(lead, len(pieces))):
                pieces.pop(0)()
            nfree = max(1, len(pieces) - nforce)
            every = max(2, nyield // nfree)
            cnt = 0
            for g in lat_gens:
                for y in g:
                    cnt += 1
                    if bgfast:
                        bgfast.pop(0)()
                    if bgq and cnt % 4 == 0:
                        bgq.pop(0)()
                    if y == "tail":
                        if pieces and DEBUG_SCHED:
                            print("run_step: leftover pieces at norm2:", len(pieces), "yields so far", cnt, "every", every)
                        while pieces:
                            pieces.pop(0)()
                        for _ in range(2):
                            if tail:
                                tail.pop(0)()
                    elif y == "force":
                        for _ in range(2):
                            if pieces:
                                pieces.pop(0)()
                    elif pieces and cnt % every == 0:
                        pieces.pop(0)()
            while pieces:
                pieces.pop(0)()
            while tail:
                tail.pop(0)()

        def gen_call(fn, *a):
            fn(*a)
            yield

        carry = []
        for l in range(NL):
            last = l == NL - 1
            if l == 1:
                while bgq:
                    bgq.pop(0)()
            head = [gen_call(layer_setup, l), gen_A(l, "ctx", 0)]
            if not last:
                run_step(head, carry)
                run_step([gen_mix(l, "ctx", 0)], [])
                carry = mlp_pieces(l, "ctx", 0, last)
                head = []
            gA0, gA1 = gen_A(l, "lat", 0), gen_A(l, "lat", 1)
            next(gA0)
            run_step(head + [gA0, gA1, gen_mix(l, "lat", 0)], carry, nyield=110 + 20 * len(head))
            if l + 1 < NL:
                for g2 in range(NG // 2):
                    bgq.insert(min(len(bgq), 3 * g2), lambda g2=g2: cast_layer(1, g2, g2 + 1))
            carry = mlp_pieces(l, "lat", 0, last)
            for i in range(1, NT):
                gens = []
                if i + 1 < NT:
                    gens.append(gen_A(l, "lat", i + 1))
                gens.append(gen_mix(l, "lat", i))
                run_step(gens, carry, nyield=54 if len(gens) == 2 else 20)
                carry = mlp_pieces(l, "lat", i, last)
        run_step([], carry)
        if debug:
            dbg = nc.dram_tensor("dbg", [128, 192 + 64], F32, kind="ExternalOutput").ap()
            for l_ in range(NL):
                P.dma("sp", lambda e, l_=l_: e.dma_start(out=dbg[:, l_ * 96:(l_ + 1) * 96], in_=mod[l_][:]), reads=[("mod", l_)], writes=[("dbg", l_)], sem="dbg")
                P.dma("sp", lambda e, l_=l_: e.dma_start(out=dbg[:, 192 + l_ * 32:192 + (l_ + 1) * 32], in_=avec[l_][:].rearrange("p a b c -> p (a b c)")), reads=[("avec", l_)], writes=[("dbg", l_)], sem="dbg")
            P.wait_all("sp", [("dbg", 0), ("dbg", 1)])
        P.wait_all("pool", [("outT", i) for i in range(NT)])
        P.wait_all("sp", [("outT", i) for i in range(NT)])
        P.emit()
    return nc


def _fm(v):
    return np.ascontiguousarray(np.asarray(v, np.float32).reshape(-1, 128).T)


def _w_in_colperm():
    cols = []
    cols += list(range(0, 256))
    cols += list(range(512, 768))
    cols += list(range(256, 512))
    for jq in range(4):
        cols += list(range(OFF_Q + 64 * jq, OFF_Q + 64 * jq + 64))
        cols += list(range(OFF_Q + 64 * (jq + 4), OFF_Q + 64 * (jq + 4) + 64))
    cols += list(range(1280, 1408))
    cols += list(range(1408, 1536))
    cols += list(range(1792, 2048))
    cols += list(range(1536, 1792))
    return np.array(cols)


def _granules_k8(W):
    nc_ = W.shape[1] // 128
    Wr = W.reshape(8, 128, nc_ // 2, 2, 128)
    return np.ascontiguousarray(Wr.transpose(2, 1, 3, 0, 4)).reshape(nc_ // 2, 128, GE)


def _granules_mlp2(W2, half):
    Wh = W2[half * 2048:(half + 1) * 2048].reshape(16, 128, 8, 128)
    return np.ascontiguousarray(Wh.transpose(2, 1, 0, 3)).reshape(8, 128, GE)


def _prep(inputs):
    f = lambda k: np.asarray(inputs[k], np.float32)
    x, c, ctx, c_ctx = f("x"), f("c"), f("ctx"), f("c_ctx")
    perm = _w_in_colperm()
    wp = []
    for l in range(NL):
        wp.append(_granules_k8(f("w_in")[l][:, perm]))
        wp.append(_granules_k8(f("w_out")[l]))
        w1 = _granules_k8(f("w_mlp1")[l])
        w2 = f("w_mlp2")[l]
        for half in range(2):
            wp.append(w1[half * 8:(half + 1) * 8])
            wp.append(_granules_mlp2(w2, half))
    wpack = np.ascontiguousarray(np.concatenate(wp, axis=0))
    assert wpack.shape == (NL * NG, 128, GE)
    wmod = np.ascontiguousarray(f("w_mod").reshape(NL, 8, 128, 12, 512).transpose(0, 3, 2, 1, 4)).reshape(NL, 12, 128, 8 * 512)
    kpack = np.zeros((128, 5, 128), np.float32)
    kpack[:, 0, :] = np.eye(128)
    kpack[:, 1, :] = np.kron(np.eye(2), np.ones((64, 64)))
    R = np.zeros((128, 128), np.float32)
    for m in range(128):
        d = m % 64
        dd = d % 32
        if dd < 16:
            R[m + 16, m] = -1.0
        else:
            R[m - 16, m] = 1.0
    kpack[:, 2, :] = R
    kk = np.arange(128)[:, None]
    qq = np.arange(128)[None, :]
    kpack[:, 3, :] = np.where(kk >= qq, 0.0, -30000.0)
    kpack[:, 4, :] = np.where(kk <= qq, 0.0, -30000.0)
    kpack = kpack.reshape(128, NKP)
    inv_freq = (10000.0 ** (-np.arange(0, 32, 2, dtype=np.float32) / 32)).astype(np.float32)
    d = np.arange(128) % 64
    fidx = d % 16
    use_col = d >= 32
    in_maps = []
    for r in range(8):
        b, a = r // 4, (r % 4) * OWN
        pos = np.arange(a - HALO, a - HALO + NTOK)
        ok = (pos >= 0) & (pos < SEQ)
        xt = np.zeros((D, NTOK), np.float32)
        xt[:, ok] = x[b, pos[ok]].T
        xt = np.ascontiguousarray(xt.reshape(8, 128, NT, T).transpose(2, 1, 0, 3)).reshape(NT, 128, 8 * T)
        cpk = np.zeros((128, NCP), np.float32)
        for l in range(NL):
            cpk[:, CPO[("n1g", l)]:CPO[("n1g", l)] + 8] = _fm(f("norm1_g")[l])
            cpk[:, CPO[("n2g", l)]:CPO[("n2g", l)] + 8] = _fm(f("norm2_g")[l])
            cpk[:, CPO[("bmod", l)]:CPO[("bmod", l)] + 48] = _fm(f("b_mod")[l])
            cpk[:, CPO[("caw", l)]:CPO[("caw", l)] + 6] = f("conv_a_w")[l].reshape(3, 2, 128).transpose(2, 0, 1).reshape(128, 6)
            cpk[:, CPO[("ccw", l)]:CPO[("ccw", l)] + 62] = f("conv_c_w")[l].reshape(31, 2, 128).transpose(2, 0, 1).reshape(128, 62)
            cpk[:, CPO[("ccb", l)]:CPO[("ccb", l)] + 2] = _fm(f("conv_c_b")[l])
            cpk[:, CPO[("lng", l)]:CPO[("lng", l)] + 2] = _fm(f("ln_c_g")[l])
            cpk[:, CPO[("lnb", l)]:CPO[("lnb", l)] + 2] = _fm(f("ln_c_b")[l])
            cpk[:, CPO[("qg", l)]] = f("q_norm_g")[l][d]
            cpk[:, CPO[("kg", l)]] = f("k_norm_g")[l][d]
            cpk[:, CPO[("sink", l)]:CPO[("sink", l)] + 8] = f("attn_sink")[l][None, :]
        cc = np.stack([_fm(c[b]), _fm(c_ctx)], axis=2).reshape(128, 16)
        cpk[:, CPO["c"]:CPO["c"] + 16] = cc
        cpk[:, CPO["valid"]:CPO["valid"] + NB] = ok.reshape(NB, 128).T
        p = np.clip(pos, 0, SEQ - 1)
        row, col = (p // 64).astype(np.float32), (p % 64).astype(np.float32)
        ang = np.where(use_col[:, None], col[None, :], row[None, :]).astype(np.float32) * inv_freq[fidx][:, None]
        in_maps.append({
            "xT": xt, "ctxT": np.ascontiguousarray(ctx[b].T), "wpack": wpack, "wmod": wmod, "cpack": cpk, "kpack": kpack,
            "ropeC": np.cos(ang).astype(np.float32), "ropeS": np.sin(ang).astype(np.float32),
        })
    return in_maps


def kernel(**inputs):
    in_maps = _prep(inputs)
    nc = build()
    res = run_bass_kernel_spmd(nc, in_maps, core_ids=list(range(8)))
    out = np.zeros((2, SEQ, D), np.float32)
    for r in range(8):
        b, a = r // 4, (r % 4) * OWN
        o = res.results[r]["outT"].reshape(NT, 128, 8, T).transpose(2, 1, 0, 3).reshape(D, NTOK)
        out[b, a:a + OWN] = o[:, HALO:HALO + OWN].T
    return out
```
